# Optimizing a Trainium2 kernel written in Bass

```python
import math
import jax, jax.numpy as jnp
from jax import lax
import numpy as np

D_MODEL = 4096
BATCH = 2
SEQ = 8192
DEPTH = 2

HEAD_DIM = 128
H_A = (3 * D_MODEL) // (8 * HEAD_DIM)
H_B = D_MODEL // (4 * HEAD_DIM)
H_C = (3 * D_MODEL) // (8 * HEAD_DIM)
D_A = H_A * HEAD_DIM
D_B = H_B * HEAD_DIM
D_C = H_C * HEAD_DIM
D_MIX = D_A + D_B + D_C
SPLIT_SIZES = (3 * D_A, D_A, H_A, H_A, 2 * D_B, 3 * D_C, H_C)
D_IN = sum(SPLIT_SIZES)
CONV_K = 4
DN_CHUNK = 64
SGU_CHUNK = 128
FOX_BLOCK = 128
D_FF = 2 * D_MODEL
N_MOD = 9
ALPHA = (2.0 * DEPTH) ** 0.25
BETA_INIT = (8.0 * DEPTH) ** -0.25
LN_EPS = 1e-5
RMS_EPS = 1e-6

kernel_name = "hybrid_deltanet_gmlp_fox_macaron_deepnorm"


def layer_norm(x, g, b):
    xf = x.astype(jnp.float32)
    mu = jnp.mean(xf, -1, keepdims=True)
    var = jnp.mean(jnp.square(xf - mu), -1, keepdims=True)
    return ((xf - mu) * lax.rsqrt(var + LN_EPS) * g.astype(jnp.float32) + b.astype(jnp.float32)).astype(x.dtype)


def rms_norm(x, g):
    xf = x.astype(jnp.float32)
    return xf * lax.rsqrt(jnp.mean(jnp.square(xf), -1, keepdims=True) + RMS_EPS) * g.astype(jnp.float32)


def l2_normalize(x):
    return x * lax.rsqrt(jnp.sum(jnp.square(x), -1, keepdims=True) + RMS_EPS)


def modulate(x, shift, scale):
    return x * (1.0 + scale[:, None, :]) + shift[:, None, :]


def swiglu_ffn(h, w_in, w_out):
    gate, up = jnp.split(h @ w_in, 2, axis=-1)
    return (jax.nn.silu(gate) * up) @ w_out


def causal_depthwise_conv(x, w):
    return lax.conv_general_dilated(
        x, w[:, None, :].astype(x.dtype), window_strides=(1,), padding=((CONV_K - 1, 0),),
        dimension_numbers=("NWC", "WIO", "NWC"), feature_group_count=x.shape[-1])


def gated_delta_rule(q, k, v, g, beta):
    bsz, seq, nh, dh = q.shape
    n = seq // DN_CHUNK
    C = DN_CHUNK

    def chunks(t):
        return jnp.moveaxis(t.reshape(bsz, n, C, nh, *t.shape[3:]), 3, 1)

    q, k, v, g, beta = chunks(q) * dh ** -0.5, chunks(k), chunks(v), chunks(g), chunks(beta)
    g = jnp.cumsum(g, axis=-1)
    tri = jnp.tril(jnp.ones((C, C), bool))
    strict = jnp.tril(jnp.ones((C, C), bool), -1)
    diff = g[..., :, None] - g[..., None, :]
    decay = jnp.exp(jnp.where(tri, diff, -jnp.inf))
    k_beta = k * beta[..., None]
    v_beta = v * beta[..., None]
    kk = jnp.einsum("bhncd,bhnsd->bhncs", k_beta, k) * decay
    lower = jnp.where(strict, kk, 0.0) + jnp.eye(C, dtype=jnp.float32)
    rhs = jnp.concatenate([v_beta, k_beta * jnp.exp(g)[..., None]], axis=-1)
    sol = lax.linalg.triangular_solve(lower, rhs, left_side=True, lower=True)
    u, w = sol[..., :dh], sol[..., dh:]
    qk = jnp.einsum("bhncd,bhnsd->bhncs", q, k) * decay

    def step(state, inp):
        q_i, k_i, u_i, w_i, qk_i, g_i = inp
        v_new = u_i - jnp.einsum("bhcd,bhde->bhce", w_i, state)
        o = (jnp.einsum("bhcd,bhde->bhce", q_i * jnp.exp(g_i)[..., None], state)
             + jnp.einsum("bhcs,bhse->bhce", qk_i, v_new))
        g_last = g_i[..., -1]
        state = (state * jnp.exp(g_last)[..., None, None]
                 + jnp.einsum("bhcd,bhce->bhde", k_i * jnp.exp(g_last[..., None] - g_i)[..., None], v_new))
        return state, o

    xs = tuple(jnp.moveaxis(t, 2, 0) for t in (q, k, u, w, qk, g))
    s0 = jnp.zeros((bsz, nh, dh, dh), jnp.float32)
    _, o = lax.scan(step, s0, xs)
    return jnp.transpose(o, (1, 0, 3, 2, 4)).reshape(bsz, seq, nh, dh)


def spatial_gating(uv, ln_g, ln_b, w_s, b_s):
    u, v = jnp.split(uv, 2, axis=-1)
    v = layer_norm(v, ln_g, ln_b)
    bsz, seq, _ = v.shape
    n = seq // SGU_CHUNK
    vc = v.reshape(bsz, n, SGU_CHUNK, H_B, HEAD_DIM)
    w_causal = jnp.where(jnp.tril(jnp.ones((SGU_CHUNK, SGU_CHUNK), bool)), w_s, 0.0).astype(v.dtype)
    mixed = jnp.einsum("gts,bnsgc->bntgc", w_causal, vc) + b_s.T[None, None, :, :, None].astype(v.dtype)
    return u * mixed.reshape(bsz, seq, D_B)


def forgetting_attention(q, k, v, log_f):
    seq = q.shape[1]
    dh = q.shape[-1]
    cum = jnp.cumsum(log_f, axis=1)
    cum_h = jnp.transpose(cum, (0, 2, 1))
    outs = []
    for i in range(seq // FOX_BLOCK):
        lo, hi = i * FOX_BLOCK, (i + 1) * FOX_BLOCK
        logits = jnp.einsum("bthd,bshd->bhts", q[:, lo:hi], k[:, :hi],
                            preferred_element_type=jnp.float32) * dh ** -0.5
        logits = logits + cum_h[:, :, lo:hi, None] - cum_h[:, :, None, :hi]
        mask = jnp.arange(hi)[None, :] <= (lo + jnp.arange(FOX_BLOCK))[:, None]
        p = jax.nn.softmax(jnp.where(mask, logits, -jnp.inf), axis=-1)
        outs.append(jnp.einsum("bhts,bshd->bthd", p.astype(v.dtype), v[:, :hi]))
    return jnp.concatenate(outs, axis=1)


def hybrid_mixer(h, w_in, conv_w, a_log, dt_bias, norm_a, sgu_ln_g, sgu_ln_b, w_s, b_s, b_f, norm_c, w_o):
    bsz, seq, _ = h.shape
    points = [int(p) for p in np.cumsum(SPLIT_SIZES)[:-1]]
    qkv_a, z_a, beta_a, dec_a, uv_b, qkv_c, f_c = jnp.split(h @ w_in, points, axis=-1)
    qkv_a = jax.nn.silu(causal_depthwise_conv(qkv_a, conv_w))
    q_a, k_a, v_a = [t.reshape(bsz, seq, H_A, HEAD_DIM).astype(jnp.float32) for t in jnp.split(qkv_a, 3, axis=-1)]
    beta = jax.nn.sigmoid(beta_a.astype(jnp.float32))
    g = -jnp.exp(a_log.astype(jnp.float32)) * jax.nn.softplus(dec_a.astype(jnp.float32) + dt_bias.astype(jnp.float32))
    o_a = gated_delta_rule(l2_normalize(q_a), l2_normalize(k_a), v_a, g, beta)
    o_a = rms_norm(o_a, norm_a) * jax.nn.silu(z_a.reshape(bsz, seq, H_A, HEAD_DIM).astype(jnp.float32))
    o_a = o_a.reshape(bsz, seq, D_A).astype(h.dtype)
    o_b = spatial_gating(jax.nn.gelu(uv_b, approximate=False), sgu_ln_g, sgu_ln_b, w_s, b_s)
    q_c, k_c, v_c = [t.reshape(bsz, seq, H_C, HEAD_DIM) for t in jnp.split(qkv_c, 3, axis=-1)]
    log_f = jax.nn.log_sigmoid(f_c.astype(jnp.float32) + b_f.astype(jnp.float32))
    o_c = rms_norm(forgetting_attention(q_c, k_c, v_c, log_f), norm_c).reshape(bsz, seq, D_C).astype(h.dtype)
    return jnp.concatenate([o_a, o_b, o_c], axis=-1) @ w_o


def setup_inputs(seed: int = 0) -> dict:
    key = jax.random.key(seed)
    ks = jax.random.split(key, 20)
    f32 = jnp.float32
    nrm = lambda k, shape, s: jax.random.normal(k, shape, f32) * s
    dt = jnp.exp(jax.random.uniform(ks[10], (DEPTH, H_A), f32, math.log(1e-3), math.log(1e-1)))
    return {
        "x": nrm(ks[0], (BATCH, SEQ, D_MODEL), 1.0),
        "c": nrm(ks[1], (BATCH, D_MODEL), 1.0),
        "w_ada": nrm(ks[2], (DEPTH, D_MODEL, N_MOD * D_MODEL), 0.5 * D_MODEL ** -0.5),
        "b_ada": nrm(ks[3], (DEPTH, N_MOD * D_MODEL), 0.02),
        "ln_g": 1.0 + nrm(ks[4], (DEPTH, 3, D_MODEL), 0.02),
        "ln_b": nrm(ks[5], (DEPTH, 3, D_MODEL), 0.02),
        "w_ffn_in": nrm(ks[6], (DEPTH, 2, D_MODEL, 2 * D_FF), D_MODEL ** -0.5),
        "w_ffn_out": nrm(ks[7], (DEPTH, 2, D_FF, D_MODEL), BETA_INIT * D_FF ** -0.5),
        "w_in": nrm(ks[8], (DEPTH, D_MODEL, D_IN), D_MODEL ** -0.5),
        "conv_w": nrm(ks[9], (DEPTH, CONV_K, 3 * D_A), CONV_K ** -0.5),
        "a_log": jnp.log(jax.random.uniform(ks[11], (DEPTH, H_A), f32, 1.0, 16.0)),
        "dt_bias": dt + jnp.log(-jnp.expm1(-dt)),
        "norm_a": 1.0 + nrm(ks[12], (DEPTH, HEAD_DIM), 0.02),
        "sgu_ln_g": 1.0 + nrm(ks[13], (DEPTH, D_B), 0.02),
        "sgu_ln_b": nrm(ks[14], (DEPTH, D_B), 0.02),
        "w_s": nrm(ks[15], (DEPTH, H_B, SGU_CHUNK, SGU_CHUNK), 0.5 * SGU_CHUNK ** -0.5),
        "b_s": 1.0 + nrm(ks[16], (DEPTH, H_B, SGU_CHUNK), 0.1),
        "b_f": 4.0 + nrm(ks[17], (DEPTH, H_C), 0.5),
        "norm_c": 1.0 + nrm(ks[18], (DEPTH, HEAD_DIM), 0.02),
        "w_o": nrm(ks[19], (DEPTH, D_MIX, D_MODEL), BETA_INIT * D_MIX ** -0.5),
    }


def reference(x, c, w_ada, b_ada, ln_g, ln_b, w_ffn_in, w_ffn_out, w_in, conv_w, a_log, dt_bias,
              norm_a, sgu_ln_g, sgu_ln_b, w_s, b_s, b_f, norm_c, w_o):
    c_act = jax.nn.silu(c)
    for l in range(DEPTH):
        mod = c_act @ w_ada[l] + b_ada[l]
        sh1, sc1, ga1, sh2, sc2, ga2, sh3, sc3, ga3 = jnp.split(mod, N_MOD, axis=-1)
        y = swiglu_ffn(modulate(x, sh1, sc1), w_ffn_in[l, 0], w_ffn_out[l, 0])
        x = layer_norm(ALPHA * x + 0.5 * ga1[:, None, :] * y, ln_g[l, 0], ln_b[l, 0])
        y = hybrid_mixer(modulate(x, sh2, sc2), w_in[l], conv_w[l], a_log[l], dt_bias[l], norm_a[l],
                         sgu_ln_g[l], sgu_ln_b[l], w_s[l], b_s[l], b_f[l], norm_c[l], w_o[l])
        x = layer_norm(ALPHA * x + ga2[:, None, :] * y, ln_g[l, 1], ln_b[l, 1])
        y = swiglu_ffn(modulate(x, sh3, sc3), w_ffn_in[l, 1], w_ffn_out[l, 1])
        x = layer_norm(ALPHA * x + 0.5 * ga3[:, None, :] * y, ln_g[l, 2], ln_b[l, 2])
    return x
```

```python
import numpy as np
from contextlib import ExitStack
import concourse.bass as bass
import concourse.mybir as mybir
from concourse.bass_utils import run_bass_kernel_spmd
import ml_dtypes

F32 = mybir.dt.float32
BF16 = mybir.dt.bfloat16
AF = mybir.ActivationFunctionType
ALU = mybir.AluOpType
NPBF = ml_dtypes.bfloat16

D = 4096
B = 2
S = 8192
DEPTH = 2
HD = 128
H_A = 12
H_B = 8
H_C = 12
D_A = 1536
D_B = 1024
D_C = 1536
DFF = 8192
NCORE = 8
NTOK = 2048
T = 512
NTT = NTOK // T
KC = D // 128
ALPHA = (2.0 * DEPTH) ** 0.25
LN_EPS = 1e-5
RMS_EPS = 1e-6


class Tl:
    __slots__ = ("h", "w", "r", "dsem", "dcnt", "name", "excl")

    def __init__(self, h, name):
        self.excl = False
        self.h = h
        self.name = name
        self.w = None
        self.r = {}
        self.dsem = None
        self.dcnt = 0

    def __getitem__(self, k):
        return self.h[k]


class Eng:
    def __init__(self, key, h, sem):
        self.key = key
        self.h = h
        self.sem = sem
        self.cnt = 0
        self.seen = {}
        self.prog = []


class KB:
    def __init__(self, nc, stack):
        self.nc = nc
        self.stack = stack
        self.pe = self._eng("pe", nc.tensor)
        self.act = self._eng("act", nc.scalar)
        self.dve = self._eng("dve", nc.vector)
        self.pool = self._eng("pool", nc.gpsimd)
        self.sp = self._eng("sp", nc.sync)
        self.nt = 0
        self.dma_tls = []

    def _sem(self, name):
        return self.stack.enter_context(self.nc.semaphore(name))

    def _eng(self, key, h):
        return Eng(key, h, self._sem("s_" + key))

    def sb(self, shape, dt, name):
        h = self.stack.enter_context(self.nc.sbuf_tensor("sb_" + name, list(shape), dt))
        return Tl(h, name)

    def ps(self, shape, dt, name):
        h = self.stack.enter_context(self.nc.psum_tensor(name, list(shape), dt))
        t = Tl(h, name)
        t.excl = True
        return t

    def sub(self, name):
        self.nt += 1
        return Tl(None, f"{name}_{self.nt}")

    def _deps(self, eng, reads, writes, accum=False, join=None):
        deps = {}

        def add(tok):
            if tok is None:
                return
            key, sem, val = tok
            if key not in deps or deps[key][1] < val:
                deps[key] = (sem, val)

        for t in reads:
            add(t.w)
            if t.excl:
                for kk, tok in t.r.items():
                    if kk != eng.key:
                        add(tok)
        for t in writes:
            w = t.w
            if w is not None:
                if accum and w[0] == eng.key:
                    pass
                elif join is not None and w[0] == join:
                    pass
                else:
                    add(w)
            for tok in t.r.values():
                add(tok)
        waits = []
        for key, (sem, val) in deps.items():
            if eng.seen.get(key, 0) >= val:
                continue
            eng.seen[key] = val
            waits.append((sem, val))
        return waits

    @staticmethod
    def _mark(tok, reads, writes):
        key = tok[0]
        for t in reads:
            t.r[key] = tok
        for t in writes:
            t.w = tok
            t.r = {}

    def op(self, eng, fn, reads=(), writes=(), accum=False):
        waits = self._deps(eng, reads, writes, accum=accum)
        eng.cnt += 1
        tok = (eng.key, eng.sem, eng.cnt)
        eng.prog.append((waits, fn, eng.sem, 1))
        self._mark(tok, reads, writes)
        return tok

    def dma(self, eng, out_ap, in_ap, sbt, reads=(), writes=(), join=False, **kw):
        if sbt.dsem is None:
            sbt.dsem = self._sem("d_" + sbt.name)
            self.dma_tls.append(sbt)
        dkey = ("d", sbt.name)
        waits = self._deps(eng, reads, writes, join=dkey if join else None)
        sbt.dcnt += 16
        tok = (dkey, sbt.dsem, sbt.dcnt)
        eng.prog.append((waits, lambda e: e.dma_start(out=out_ap, in_=in_ap, **kw), sbt.dsem, 16))
        self._mark(tok, reads, writes)
        return tok

    def barrier(self):
        engs = [self.pe, self.act, self.dve, self.pool, self.sp]
        for e in engs:
            waits = []
            for o in engs:
                if o is e or o.cnt == 0:
                    continue
                if e.seen.get(o.key, 0) < o.cnt:
                    e.seen[o.key] = o.cnt
                    waits.append((o.sem, o.cnt))
            for t in self.dma_tls:
                dk = ("d", t.name)
                if e.seen.get(dk, 0) < t.dcnt:
                    e.seen[dk] = t.dcnt
                    waits.append((t.dsem, t.dcnt))
            if waits:
                e.prog.append((waits, None, None, 0))

    def wait_tiles(self, eng, tiles):
        waits = self._deps(eng, (), tiles)
        if waits:
            eng.prog.append((waits, None, None, 0))

    def finish(self):
        nc = self.nc
        with nc.Block() as block:
            def run(eng):
                def body(e):
                    for waits, fn, sem, inc in eng.prog:
                        for s, v in waits:
                            e.wait_ge(s, v)
                        if fn is not None:
                            fn(e).then_inc(sem, inc)
                return body

            block.tensor(run(self.pe))
            block.scalar(run(self.act))
            block.vector(run(self.dve))
            block.gpsimd(run(self.pool))
            block.sync(run(self.sp))


class Ring:
    def __init__(self, k, n, shape, dt, name, alloc=None):
        alloc = alloc or k.sb
        self.tiles = [alloc(shape, dt, f"{name}{i}") for i in range(n)]
        self.i = 0

    def next(self):
        t = self.tiles[self.i % len(self.tiles)]
        self.i += 1
        return t


def vec_pp(v):
    v = np.asarray(v, np.float32)
    return np.ascontiguousarray(v.reshape(-1, 128).T)


def blocks_in(W, cols_blocks):
    K = W.shape[0]
    kc = K // 128
    nblk, ncol = cols_blocks.shape
    Wp = np.concatenate([W, np.zeros((K, 1), W.dtype)], axis=1)
    out = np.empty((nblk, 128, kc, ncol), np.float32)
    for b in range(nblk):
        blk = Wp[:, cols_blocks[b]]
        out[b] = blk.reshape(kc, 128, ncol).transpose(1, 0, 2)
    return out.reshape(nblk, 128, kc * ncol)


OFF_QKVA = 0
OFF_Z = 4608
OFF_BETA = 6144
OFF_DEC = 6156
OFF_UV = 6168
OFF_QKVC = 8216
OFF_F = 12824
D_IN = 12836


def mixin_block_plan():
    blocks = []
    for i in range(36):
        blocks.append(("A", i, np.arange(OFF_QKVA + i * 128, OFF_QKVA + (i + 1) * 128)))
    for i in range(12):
        blocks.append(("A", 36 + i, np.arange(OFF_Z + i * 128, OFF_Z + (i + 1) * 128)))
    for i in range(36):
        blocks.append(("C", i, np.arange(OFF_QKVC + i * 128, OFF_QKVC + (i + 1) * 128)))
    for i in range(8):
        blocks.append(("U", i, np.arange(OFF_UV + i * 128, OFF_UV + (i + 1) * 128)))
    for i in range(8):
        blocks.append(("V", i, np.arange(OFF_UV + 1024 + i * 128, OFF_UV + 1024 + (i + 1) * 128)))
    small = np.full(128, -1, np.int64)
    small[0:12] = np.arange(OFF_BETA, OFF_BETA + 12)
    small[12:24] = np.arange(OFF_DEC, OFF_DEC + 12)
    small[24:36] = np.arange(OFF_F, OFF_F + 12)
    blocks.append(("S", 0, small))
    blocks.append(("P", 0, np.full(128, -1, np.int64)))
    return blocks


class TokProg:
    def __init__(self, stages, ntt=NTT, debug=False):
        self.debug = debug
        self.stages = stages
        self.ntt = ntt
        self.ntok = ntt * T
        nc = bass.Bass("TRN2", target_bir_lowering=False)
        self.nc = nc
        self.in_names = []
        self.out_names = []
        with ExitStack() as st:
            self.k = KB(nc, st)
            self._alloc()
            self._emit()
            self.k.finish()

    def din(self, name, shape, dt=F32):
        self.in_names.append(name)
        return self.nc.dram_tensor(name, list(shape), dt, kind="ExternalInput").ap()

    def dout(self, name, shape, dt=F32):
        self.out_names.append(name)
        return self.nc.dram_tensor(name, list(shape), dt, kind="ExternalOutput").ap()

    def dint(self, name, shape, dt=F32):
        return self.nc.dram_tensor(name, list(shape), dt, kind="Internal").ap()

    def _alloc(self):
        k = self.k
        self.hx = k.sb([128, 16384], F32, "hx")
        self.xn = self.hx[:, :].rearrange("p (c t) -> p c t", t=T)
        self.hTv = self.hx[:, 0:8192].bitcast(BF16).rearrange("p (c t) -> p c t", t=T)
        self.hT = k.sub("hT")
        self.xn_t = [k.sub("xn") for _ in range(KC)]
        self.big = k.sb([128, 16384], F32, "big")
        self.bigv = self.big[:, :].bitcast(BF16).rearrange("p (c t) -> p c t", t=T)
        self.wslots = Ring(k, 2, [128, 8192], BF16, "wsl")
        self.xs = Ring(k, 2, [128, T], F32, "xs")
        self.xo = Ring(k, 2, [128, T], F32, "xo")
        self.tb = Ring(k, 2, [128, T], F32, "tb")
        self.sq = Ring(k, 2, [128, T], F32, "sq")
        self.cst = Ring(k, 2, [128, T], BF16, "cst")
        self.ones = k.sb([128, 128], F32, "ones")
        self.ident = k.sb([128, 128], F32, "ident")
        self.mean = k.sb([128, T], F32, "mean")
        self.rstd = k.sb([128, T], F32, "rstd")
        self.vecs = {}
        self.P = [k.ps([128, 512], F32, f"P{i}") for i in range(8)]
        self.ident_d = self.din("ident_in", [128, 128])
        k.op(k.dve, lambda e: e.memset(self.ones[:, :], 1.0), writes=[self.ones])
        k.dma(k.sp, self.ident[:, :], self.ident_d[:, :], self.ident, writes=[self.ident])
        self.eps_ln = k.sb([128, 1], F32, "eps_ln")
        k.op(k.dve, lambda e: e.memset(self.eps_ln[:, :], float(LN_EPS)), writes=[self.eps_ln])

    def vec(self, name):
        if name in self.vecs:
            return self.vecs[name]
        k = self.k
        d = self.din(name, [128, KC])
        t = k.sb([128, KC], F32, "v_" + name)
        k.dma(k.sp, t[:, :], d[:, :], t, writes=[t])
        self.vecs[name] = t
        return t

    def derived(self, name, src, mul, add):
        if name in self.vecs:
            return self.vecs[name]
        k = self.k
        t = k.sb([128, KC], F32, "v_" + name)
        k.op(k.dve, lambda e: e.tensor_scalar(out=t[:, :], in0=src[:, :], scalar1=float(mul), scalar2=float(add),
                                              op0=ALU.mult, op1=ALU.add), reads=[src], writes=[t])
        self.vecs[name] = t
        return t

    def stage_mod(self, x_d, tt, scp, sh):
        k = self.k
        hT = self.hTv
        for kc in range(KC):
            xs = self.xs.next()
            k.dma(k.sp, xs[:, :], x_d[kc * 128:(kc + 1) * 128, tt * T:(tt + 1) * T], xs, writes=[xs])
            k.op(k.act, lambda e, xs=xs, kc=kc: e.activation(out=hT[:, kc, :], in_=xs[:, :], func=AF.Identity,
                                                              bias=sh[:, kc:kc + 1], scale=scp[:, kc:kc + 1]),
                 reads=[xs, scp, sh], writes=[self.hT] + self.xn_t[0:16])

    def linear_in(self, w_d, npairs, epilogue, bank0=0):
        k = self.k
        nc = self.nc
        hT = self.hTv
        for pi in range(npairs):
            ws = self.wslots.next()
            k.dma(k.pool, ws[:, :], w_d[pi, :, :], ws, writes=[ws])
            banks = [self.P[bank0 + 2 * (pi % 2)], self.P[bank0 + 2 * (pi % 2) + 1]]
            w3 = ws[:, :].rearrange("p (k c) -> p k c", c=256)
            nb = epilogue(pi, None)
            with nc.allow_low_precision("bf16 matmul"):
                for j in range(nb):
                    ps = banks[j]
                    for kc in range(KC):
                        k.op(k.pe, lambda e, ps=ps, kc=kc, j=j, w3=w3: e.matmul(
                            ps[:, :], lhsT=w3[:, kc, j * 128:(j + 1) * 128], rhs=hT[:, kc, :],
                            start=(kc == 0), stop=(kc == KC - 1)),
                            reads=[ws, self.hT], writes=[ps], accum=(kc > 0))
            epilogue(pi, banks)

    def linear_out(self, w_d, nkc, bps, opnd, xres_d, xout_d, tt, gco, lng, lnb, bank0=4):
        k = self.k
        nc = self.nc
        ones = self.ones
        psum_s, psum_q = self.P[6], self.P[7]
        pend = None
        for m in range(KC):
            if m % bps == 0:
                ws = self.wslots.next()
                k.dma(k.pool, ws[:, 0:bps * nkc * 128], w_d[m // bps, :, :], ws, writes=[ws])
                w4 = ws[:, 0:bps * nkc * 128].rearrange("p (b k c) -> p b k c", b=bps, c=128)
            ps = self.P[bank0 + (m % 2)]
            with nc.allow_low_precision("bf16 matmul"):
                for kc in range(nkc):
                    k.op(k.pe, lambda e, ps=ps, kc=kc, w4=w4, bi=m % bps: e.matmul(
                        ps[:, :], lhsT=w4[:, bi, kc, :], rhs=opnd(kc), start=(kc == 0), stop=(kc == nkc - 1)),
                        reads=[ws, self.big], writes=[ps], accum=(kc > 0))
            if pend is not None:
                self._stats_mm(*pend)
            xs = self.xs.next()
            k.dma(k.sp, xs[:, :], xres_d[m * 128:(m + 1) * 128, tt * T:(tt + 1) * T], xs, writes=[xs])
            tb = self.tb.next()
            k.op(k.act, lambda e, ps=ps, tb=tb, m=m: e.activation(out=tb[:, :], in_=ps[:, :], func=AF.Copy,
                                                                  scale=gco[:, m:m + 1]),
                 reads=[ps, gco], writes=[tb])
            xt = self.xn_t[m]
            k.op(k.dve, lambda e, xs=xs, tb=tb, m=m: e.scalar_tensor_tensor(
                out=self.xn[:, m, :], in0=xs[:, :], scalar=float(ALPHA), in1=tb[:, :], op0=ALU.mult, op1=ALU.add),
                reads=[xs, tb], writes=[xt] + ([self.hT] if m < 16 else []))
            sq = self.sq.next()
            k.op(k.act, lambda e, sq=sq, m=m: e.activation(out=sq[:, :], in_=self.xn[:, m, :], func=AF.Square),
                 reads=[xt], writes=[sq])
            pend = (m, xt, sq)
        self._stats_mm(*pend)
        mean, rstd = self.mean, self.rstd
        msq = self.tb.next()
        k.op(k.dve, lambda e: e.tensor_scalar(out=mean[:, :], in0=psum_s[:, :], scalar1=1.0 / D, scalar2=None,
                                              op0=ALU.mult), reads=[psum_s], writes=[mean])
        k.op(k.dve, lambda e: e.tensor_tensor(out=msq[:, :], in0=mean[:, :], in1=mean[:, :], op=ALU.mult),
             reads=[mean], writes=[msq])
        k.op(k.dve, lambda e: e.scalar_tensor_tensor(out=rstd[:, :], in0=psum_q[:, :], scalar=1.0 / D, in1=msq[:, :],
                                                    op0=ALU.mult, op1=ALU.subtract),
             reads=[psum_q, msq], writes=[rstd])
        self.rsqrt(rstd, rstd[:, :], LN_EPS)
        for m in range(KC):
            xt = self.xn_t[m]
            tb = self.tb.next()
            k.op(k.dve, lambda e, tb=tb, m=m: e.tensor_tensor(out=tb[:, :], in0=self.xn[:, m, :], in1=mean[:, :],
                                                              op=ALU.subtract), reads=[xt, mean], writes=[tb])
            k.op(k.pool, lambda e, tb=tb: e.tensor_tensor(out=tb[:, :], in0=tb[:, :], in1=rstd[:, :], op=ALU.mult),
                 reads=[tb, rstd], writes=[tb])
            xo = self.xo.next()
            k.op(k.act, lambda e, tb=tb, xo=xo, m=m: e.activation(out=xo[:, :], in_=tb[:, :], func=AF.Identity,
                                                                  bias=lnb[:, m:m + 1], scale=lng[:, m:m + 1]),
                 reads=[tb, lng, lnb], writes=[xo])
            k.dma(k.sp, xout_d[m * 128:(m + 1) * 128, tt * T:(tt + 1) * T], xo[:, :], xo, reads=[xo])

    def rsqrt(self, tl, ap, eps):
        k = self.k
        k.op(k.act, lambda e: e.activation(out=ap, in_=ap, func=AF.Sqrt, bias=self.eps_ln[:, 0:1], scale=1.0),
             reads=[tl, self.eps_ln], writes=[tl])
        k.op(k.dve, lambda e: e.reciprocal(out=ap, in_=ap), reads=[tl], writes=[tl])

    def _stats_mm(self, m, xt, sq):
        k = self.k
        psum_s, psum_q = self.P[6], self.P[7]
        k.op(k.pe, lambda e: e.matmul(psum_s[:, :], lhsT=self.ones[:, :], rhs=self.xn[:, m, :],
                                      start=(m == 0), stop=(m == KC - 1)),
             reads=[self.ones, xt], writes=[psum_s], accum=(m > 0))
        k.op(k.pe, lambda e: e.matmul(psum_q[:, :], lhsT=self.ones[:, :], rhs=sq[:, :],
                                      start=(m == 0), stop=(m == KC - 1)),
             reads=[self.ones, sq], writes=[psum_q], accum=(m > 0))

    def _emit(self):
        k = self.k
        self.x_in = self.din("x_in", [D, self.ntok])
        cur = self.x_in
        plans = []
        last_x = max([i for i, (kd, _) in enumerate(self.stages) if kd in ("ffn", "mixout")] + [-1])
        for si, (kind, tag) in enumerate(self.stages):
            last = si == len(self.stages) - 1
            p = {"kind": kind, "tag": tag, "x_src": cur}
            if kind == "ffn":
                p["w1"] = self.din(f"wfi_{tag}", [64, 128, KC * 256])
                p["w2"] = self.din(f"wfo_{tag}", [32, 128, 64 * 128])
                sc = self.vec(f"sc_{tag}")
                p["scp"] = self.derived(f"scp_{tag}", sc, 1.0, 1.0)
                p["sh"] = self.vec(f"sh_{tag}")
                p["gco"] = self.derived(f"gco_{tag}", self.vec(f"ga_{tag}"), 0.5, 0.0)
                p["lng"] = self.vec(f"lng_{tag}")
                p["lnb"] = self.vec(f"lnb_{tag}")
            elif kind == "mixout":
                p["mix"] = self.din(f"mix_{tag}", [D, self.ntok], BF16)
                p["w2"] = self.din(f"wo_{tag}", [16, 128, 2 * KC * 128])
                p["gco"] = self.vec(f"ga_{tag}")
                p["lng"] = self.vec(f"lng_{tag}")
                p["lnb"] = self.vec(f"lnb_{tag}")
            elif kind == "mixin":
                p["w1"] = self.din(f"win_{tag}", [51, 128, KC * 256])
                sc = self.vec(f"sc_{tag}")
                p["scp"] = self.derived(f"scp_{tag}", sc, 1.0, 1.0)
                p["sh"] = self.vec(f"sh_{tag}")
                p["EA"] = self.dout(f"EA_{tag}", [48 * 128, self.ntok])
                p["EC"] = self.dout(f"EC_{tag}", [36 * 128, self.ntok], BF16)
                p["ES"] = self.dout(f"ES_{tag}", [36, self.ntok])
                p["EB"] = self.dout(f"EB_{tag}", [D_B, self.ntok], BF16)
                p["wsT"] = self.din(f"wsT_{tag}", [128, H_B, 128])
                p["bsr"] = self.din(f"bsr_{tag}", [128, H_B, 128])
                p["sg"] = self.din(f"sgg_{tag}", [128, D_B])
                p["sb"] = self.din(f"sgb_{tag}", [128, D_B])
            if kind in ("ffn", "mixout"):
                if si == last_x:
                    p["x_dst"] = self.dout("x_out", [D, self.ntok])
                else:
                    p["x_dst"] = self.dint(f"xmid_{si}", [D, self.ntok])
                cur = p["x_dst"]
            plans.append(p)
        self.plans = plans
        if any(p["kind"] == "mixin" for p in plans):
            self._alloc_b()
        for tt in range(self.ntt):
            for p in plans:
                getattr(self, "_st_" + p["kind"])(p, tt)
        k.wait_tiles(k.sp, self.xo.tiles + self.cst.tiles + self.tb.tiles + [self.big])

    def _st_ffn(self, p, tt):
        k = self.k
        big = self.bigv
        self.stage_mod(p["x_src"], tt, p["scp"], p["sh"])

        def epi(pi, banks):
            if banks is None:
                return 2
            g = self.tb.next()
            k.op(k.act, lambda e: e.activation(out=g[:, :], in_=banks[0][:, :], func=AF.Silu),
                 reads=[banks[0]], writes=[g])
            k.op(k.dve, lambda e: e.tensor_tensor(out=big[:, pi, :], in0=g[:, :], in1=banks[1][:, :], op=ALU.mult),
                 reads=[g, banks[1]], writes=[self.big])
            return 2

        self.linear_in(p["w1"], 64, epi)
        self.linear_out(p["w2"], 64, 1, lambda kc: big[:, kc, :], p["x_src"], p["x_dst"], tt,
                        p["gco"], p["lng"], p["lnb"])

    def _st_mixout(self, p, tt):
        k = self.k
        big = self.bigv
        mix = p["mix"]
        for q in range(4):
            k.dma(k.sp, big[:, q * 8:(q + 1) * 8, :],
                  mix[q * 1024:(q + 1) * 1024, tt * T:(tt + 1) * T].rearrange("(c p) t -> p c t", p=128),
                  self.big, writes=[self.big], join=(q > 0))
        self.linear_out(p["w2"], KC, 2, lambda kc: big[:, kc, :], p["x_src"], p["x_dst"], tt,
                        p["gco"], p["lng"], p["lnb"])

    def _alloc_b(self):
        k = self.k
        p = [q for q in self.plans if q["kind"] == "mixin"]
        b = self.big
        self.uT = b[:, 0:4096].rearrange("p (g t) -> p g t", t=T)
        self.vT = b[:, 4096:8192].rearrange("p (g t) -> p g t", t=T)
        self.vtok = b[:, 8192:9216]
        self.vn = b[:, 9216:9728].bitcast(BF16)
        self.obT = [b[:, 9728 + i * 512:9728 + (i + 1) * 512].bitcast(BF16).rearrange("p (g t) -> p g t", t=128)
                    for i in range(2)]
        self.obi = 0
        self.bst = k.sb([128, 2, 6], F32, "bst")
        self.bag = k.sb([128, 2], F32, "bag")
        self.brs = k.sb([128, 2], F32, "brs")
        self.mask_su = k.sb([128, 128], F32, "mask_su")
        self.mask_d = self.din("mask_su_in", [128, 128])
        k.dma(k.sp, self.mask_su[:, :], self.mask_d[:, :], self.mask_su, writes=[self.mask_su])
        self.bc = {}
        for q in p:
            tag = q["tag"]
            wsT = k.sb([128, H_B, 128], F32, f"wsT_{tag}")
            k.dma(k.sp, wsT[:, :, :], q["wsT"][:, :, :], wsT, writes=[wsT])
            wc = k.sb([128, H_B, 128], BF16, f"wc_{tag}")
            for g in range(H_B):
                k.op(k.dve, lambda e, g=g, wc=wc, wsT=wsT: e.tensor_tensor(out=wc[:, g, :], in0=wsT[:, g, :],
                                                                         in1=self.mask_su[:, :], op=ALU.mult),
                     reads=[wsT, self.mask_su], writes=[wc])
            bsr = k.sb([128, H_B, 128], F32, f"bsr_{tag}")
            k.dma(k.sp, bsr[:, :, :], q["bsr"][:, :, :], bsr, writes=[bsr])
            sg = k.sb([128, D_B], F32, f"sgg_{tag}")
            k.dma(k.sp, sg[:, :], q["sg"][:, :], sg, writes=[sg])
            sb_ = k.sb([128, D_B], F32, f"sgb_{tag}")
            k.dma(k.sp, sb_[:, :], q["sb"][:, :], sb_, writes=[sb_])
            self.bc[tag] = (wc, bsr, sg, sb_)

    def _st_mixin(self, p, tt):
        k = self.k
        nc = self.nc
        self.stage_mod(p["x_src"], tt, p["scp"], p["sh"])
        plan = mixin_block_plan()
        tsl = slice(tt * T, (tt + 1) * T)

        def epi(pi, banks):
            kinds = [plan[2 * pi], plan[2 * pi + 1]]
            nb = 1 if kinds[1][0] == "P" else 2
            if banks is None:
                return nb
            for j in range(nb):
                kind, idx, _ = kinds[j]
                ps = banks[j]
                if kind == "A":
                    xo = self.xo.next()
                    k.op(k.act, lambda e, xo=xo, ps=ps: e.copy(out=xo[:, :], in_=ps[:, :]), reads=[ps], writes=[xo])
                    k.dma(k.sp, p["EA"][idx * 128:(idx + 1) * 128, tsl], xo[:, :], xo, reads=[xo])
                elif kind == "C":
                    c = self.cst.next()
                    k.op(k.act, lambda e, c=c, ps=ps: e.copy(out=c[:, :], in_=ps[:, :]), reads=[ps], writes=[c])
                    k.dma(k.sp, p["EC"][idx * 128:(idx + 1) * 128, tsl], c[:, :], c, reads=[c])
                elif kind == "S":
                    xo = self.xo.next()
                    k.op(k.act, lambda e, xo=xo, ps=ps: e.copy(out=xo[:, :], in_=ps[:, :]), reads=[ps], writes=[xo])
                    k.dma(k.sp, p["ES"][:, tsl], xo[0:36, :], xo, reads=[xo])
                elif kind == "U":
                    k.op(k.act, lambda e, ps=ps, idx=idx: e.activation(out=self.uT[:, idx, :], in_=ps[:, :],
                                                                       func=AF.Gelu), reads=[ps], writes=[self.big])
                elif kind == "V":
                    k.op(k.act, lambda e, ps=ps, idx=idx: e.activation(out=self.vT[:, idx, :], in_=ps[:, :],
                                                                       func=AF.Gelu), reads=[ps], writes=[self.big])
            return nb

        self.linear_in(p["w1"], 51, epi)
        if getattr(self, "debug", False):
            if not hasattr(self, "dbg_u"):
                self.dbg_u = self.dout("dbg_u", [128, H_B * T])
                self.dbg_v = self.dout("dbg_v", [128, H_B * T])
                self.dbg_vn = self.dout("dbg_vn", [4, 128, D_B], BF16)
                self.dbg_vt = self.dout("dbg_vt", [4, 128, D_B])
            k.dma(k.sp, self.dbg_u[:, :], self.big[:, 0:4096], self.big, reads=[self.big])
            k.dma(k.sp, self.dbg_v[:, :], self.big[:, 4096:8192], self.big, reads=[self.big])
        wc, bsr, sg, sb_ = self.bc[p["tag"]]
        vtok, vn = self.vtok, self.vn
        for j in range(T // 128):
            js = slice(j * 128, (j + 1) * 128)
            pt = [self.P[4], self.P[5]]
            for g in range(H_B):
                k.op(k.pe, lambda e, g=g, js=js: e.transpose(pt[g // 4][:, (g % 4) * 128:(g % 4 + 1) * 128],
                                                             self.vT[:, g, js], self.ident[:, :]),
                     reads=[self.big, self.ident], writes=[pt[g // 4]], accum=(g % 4 > 0))
            for h in range(2):
                k.op(k.act, lambda e, h=h: e.copy(out=vtok[:, h * 512:(h + 1) * 512], in_=pt[h][:, :]),
                     reads=[pt[h]], writes=[self.big])
            if getattr(self, "debug", False):
                k.dma(k.sp, self.dbg_vt[j, :, :], vtok, self.big, reads=[self.big])
            bst, bag, brs = self.bst, self.bag, self.brs
            for h in range(2):
                sqh = self.tb.next()
                k.op(k.act, lambda e, h=h, sqh=sqh: e.activation(out=sqh[:, :], in_=vtok[:, h * 512:(h + 1) * 512],
                                                                func=AF.Square), reads=[self.big], writes=[sqh])
                k.op(k.dve, lambda e, h=h, sqh=sqh: e.reduce_sum(out=bst[:, 0, 2 + h:3 + h], in_=sqh[:, :],
                                                                axis=mybir.AxisListType.X), reads=[sqh], writes=[bst])
                k.op(k.dve, lambda e, h=h: e.reduce_sum(out=bst[:, 0, h:h + 1], in_=vtok[:, h * 512:(h + 1) * 512],
                                                        axis=mybir.AxisListType.X), reads=[self.big], writes=[bst])
            k.op(k.dve, lambda e: e.tensor_scalar(out=bag[:, 0:1], in0=bst[:, 0, 0:1], scalar1=bst[:, 0, 1:2],
                                                  scalar2=1.0 / D_B, op0=ALU.add, op1=ALU.mult),
                 reads=[bst], writes=[bag])
            k.op(k.dve, lambda e: e.tensor_scalar(out=bag[:, 1:2], in0=bst[:, 0, 2:3], scalar1=bst[:, 0, 3:4],
                                                  scalar2=1.0 / D_B, op0=ALU.add, op1=ALU.mult),
                 reads=[bst], writes=[bag])
            k.op(k.dve, lambda e: e.tensor_tensor(out=brs[:, 1:2], in0=bag[:, 0:1], in1=bag[:, 0:1], op=ALU.mult),
                 reads=[bag], writes=[brs])
            k.op(k.dve, lambda e: e.tensor_tensor(out=bag[:, 1:2], in0=bag[:, 1:2], in1=brs[:, 1:2], op=ALU.subtract),
                 reads=[bag, brs], writes=[bag])
            k.op(k.act, lambda e: e.activation(out=brs[:, 0:1], in_=bag[:, 1:2], func=AF.Sqrt, bias=self.eps_ln[:, 0:1],
                                               scale=1.0), reads=[bag, self.eps_ln], writes=[brs])
            k.op(k.dve, lambda e: e.reciprocal(out=brs[:, 0:1], in_=brs[:, 0:1]), reads=[brs], writes=[brs])
            k.op(k.dve, lambda e: e.scalar_tensor_tensor(out=brs[:, 1:2], in0=bag[:, 0:1], scalar=-1.0, in1=brs[:, 0:1],
                                                        op0=ALU.mult, op1=ALU.mult), reads=[bag, brs], writes=[brs])
            k.op(k.act, lambda e: e.activation(out=vtok[:, :], in_=vtok[:, :], func=AF.Identity,
                                               bias=brs[:, 1:2], scale=brs[:, 0:1]), reads=[self.big, brs], writes=[self.big])
            k.op(k.dve, lambda e: e.tensor_tensor(out=vtok[:, :], in0=vtok[:, :], in1=sg[:, :], op=ALU.mult),
                 reads=[self.big, sg], writes=[self.big])
            k.op(k.dve, lambda e: e.tensor_tensor(out=vn[:, :], in0=vtok[:, :], in1=sb_[:, :], op=ALU.add),
                 reads=[self.big, sb_], writes=[self.big])
            if getattr(self, "debug", False):
                k.dma(k.sp, self.dbg_vn[j, :, :], vn, self.big, reads=[self.big])
            ob = self.obT[self.obi % 2]
            self.obi += 1
            pm = [self.P[6], self.P[7]]
            with nc.allow_low_precision("bf16 matmul"):
                for g in range(H_B):
                    k.op(k.pe, lambda e, g=g: e.matmul(pm[g // 4][:, (g % 4) * 128:(g % 4 + 1) * 128],
                                                       lhsT=vn[:, g * 128:(g + 1) * 128], rhs=wc[:, g, :],
                                                       start=True, stop=True),
                         reads=[self.big, wc], writes=[pm[g // 4]], accum=(g % 4 > 0))
            for h in range(2):
                tb = self.tb.next()
                if getattr(self, "debug", False):
                    if not hasattr(self, "dbg_pm"):
                        self.dbg_pm = self.dout("dbg_pm", [4, 2, 128, 512])
                        self.dbg_wc = self.dout("dbg_wc", [128, H_B * 128], BF16)
                        k.dma(k.sp, self.dbg_wc[:, :], wc[:, :, :].rearrange("p g t -> p (g t)"), wc, reads=[wc])
                    xo = self.xo.next()
                    k.op(k.act, lambda e, h=h, xo=xo: e.copy(out=xo[:, :], in_=pm[h][:, :]), reads=[pm[h]], writes=[xo])
                    k.dma(k.sp, self.dbg_pm[j, h, :, :], xo[:, :], xo, reads=[xo])
                k.op(k.dve, lambda e, h=h, tb=tb: e.tensor_tensor(
                    out=tb[:, :], in0=pm[h][:, :], in1=bsr[:, h * 4:(h + 1) * 4, :].rearrange("p g t -> p (g t)"),
                    op=ALU.add), reads=[pm[h], bsr], writes=[tb])
                k.op(k.dve, lambda e, h=h, tb=tb, ob=ob, js=js: e.tensor_tensor(
                    out=ob[:, h * 4:(h + 1) * 4, :], in0=tb[:, :].rearrange("p (g t) -> p g t", g=4),
                    in1=self.uT[:, h * 4:(h + 1) * 4, js], op=ALU.mult), reads=[tb, self.big], writes=[self.big])
            k.dma(k.sp, p["EB"][:, tt * T + j * 128: tt * T + (j + 1) * 128].rearrange("(g c) t -> c g t", c=128),
                  ob[:, :, :], self.big, reads=[self.big])


def prep_ffn_in(W):
    return np.ascontiguousarray(W.reshape(32, 128, 2, 64, 128).transpose(3, 1, 0, 2, 4)).reshape(64, 128, 8192)


def prep_ffn_out(W):
    return np.ascontiguousarray(W.reshape(64, 128, 32, 128).transpose(2, 1, 0, 3)).reshape(32, 128, 8192)


def prep_wo(W):
    return np.ascontiguousarray(W.reshape(32, 128, 16, 2, 128).transpose(2, 1, 3, 0, 4)).reshape(16, 128, 8192)


def prep_win(W):
    plan = mixin_block_plan()
    cols = np.stack([np.concatenate([plan[2 * i][2], plan[2 * i + 1][2]]) for i in range(51)])
    return blocks_in(W, cols)


def b_consts(tag, w_s, b_s, g, b):
    return {
        f"wsT_{tag}": np.ascontiguousarray(w_s.transpose(2, 0, 1)).astype(np.float32),
        f"bsr_{tag}": np.ascontiguousarray(np.broadcast_to(b_s[None], (128, H_B, 128))).astype(np.float32),
        f"sgg_{tag}": np.ascontiguousarray(np.broadcast_to(g[None], (128, D_B))).astype(np.float32),
        f"sgb_{tag}": np.ascontiguousarray(np.broadcast_to(b[None], (128, D_B))).astype(np.float32),
        "mask_su_in": np.triu(np.ones((128, 128), np.float32)),
    }


class HeadProg:
    def __init__(self, na=3, ncu=3, seq=S, debug=False):
        self.debug = debug
        self.na, self.ncu, self.seq = na, ncu, seq
        nc = bass.Bass("TRN2", target_bir_lowering=False)
        self.nc = nc
        self.in_names, self.out_names = [], []
        with ExitStack() as st:
            self.k = KB(nc, st)
            self._common()
            if na:
                self.aoff = 0
                self._alloc_a()
                for u in range(na):
                    self._unit_a(u)
                self.k.barrier()
            if ncu:
                self.aoff = 0
                self._alloc_c()
                for u in range(ncu):
                    self._unit_c(u)
            self.k.wait_tiles(self.k.sp, self.out_tiles)
            self.k.finish()

    def din(self, name, shape, dt=F32):
        self.in_names.append(name)
        return self.nc.dram_tensor(name, list(shape), dt, kind="ExternalInput").ap()

    def dout(self, name, shape, dt=F32):
        self.out_names.append(name)
        return self.nc.dram_tensor(name, list(shape), dt, kind="ExternalOutput").ap()

    def carve(self, shape, dt, name):
        n = 1
        for d_ in shape[1:]:
            n *= d_
        nf = n if dt == F32 else (n + 1) // 2
        ap = self.arena[:, self.aoff:self.aoff + nf]
        self.aoff += nf
        assert self.aoff <= self.arena_n, (name, self.aoff)
        if dt != F32:
            ap = ap.bitcast(dt)
        if len(shape) == 3:
            ap = ap.rearrange("p (a b) -> p a b", b=shape[2])
        self.k.nt += 1
        return Tl(ap, f"{name}_{self.k.nt}")

    def dbg(self, name, tl, ap, shape, dt=F32):
        if not getattr(self, "debug", False):
            return
        if name in self.in_names or name in self.out_names:
            return
        d = self.dout(name, shape, dt)
        self.k.dma(self.k.sp, d, ap, tl, reads=[tl])
        self.out_tiles.append(tl)

    def _common(self):
        k = self.k
        self.out_tiles = []
        self.arena_n = 47 * 1024
        self.arena = self.k.sb([128, self.arena_n], F32, "arena")
        self.P = [k.ps([128, 512], F32, f"P{i}") for i in range(7)]
        self.Pb = k.ps([128, 1024], BF16, "Pb")
        self.ident = k.sb([128, 128], F32, "ident")
        self.identb = k.sb([128, 128], BF16, "identb")
        self.ones = k.sb([128, 128], F32, "ones")
        self.onesb = k.sb([128, 128], BF16, "onesb")
        self.mask_su = k.sb([128, 128], F32, "mask_su")
        self.mask_sub = k.sb([128, 128], BF16, "mask_sub")
        self.c_one = k.sb([128, 1], F32, "c_one")
        self.c_eps = k.sb([128, 1], F32, "c_eps")
        idd = self.din("ident_in", [128, 128])
        md = self.din("mask_su_in", [128, 128])
        k.dma(k.sp, self.ident[:, :], idd[:, :], self.ident, writes=[self.ident])
        k.dma(k.sp, self.mask_su[:, :], md[:, :], self.mask_su, writes=[self.mask_su])
        k.op(k.dve, lambda e: e.tensor_copy(out=self.identb[:, :], in_=self.ident[:, :]), reads=[self.ident],
             writes=[self.identb])
        k.op(k.dve, lambda e: e.tensor_copy(out=self.mask_sub[:, :], in_=self.mask_su[:, :]), reads=[self.mask_su],
             writes=[self.mask_sub])
        k.op(k.dve, lambda e: e.memset(self.ones[:, :], 1.0), writes=[self.ones])
        k.op(k.dve, lambda e: e.memset(self.onesb[:, :], 1.0), writes=[self.onesb])
        k.op(k.dve, lambda e: e.memset(self.c_one[:, :], 1.0), writes=[self.c_one])
        k.op(k.dve, lambda e: e.memset(self.c_eps[:, :], float(RMS_EPS)), writes=[self.c_eps])

    def _alloc_c(self):
        k = self.k
        sq = self.seq
        self.c_q = self.carve([128, sq], BF16, "c_q")
        self.c_k = self.carve([128, sq], BF16, "c_k")
        self.c_v = self.carve([128, sq], BF16, "c_v")
        self.c_vt = self.carve([128, sq // 128, 128], BF16, "c_vt")
        self.c_f = self.carve([128, sq], F32, "c_f")
        self.c_nc = self.carve([128, sq], F32, "c_nc")
        self.c_ncc = self.carve([128, sq // 128], F32, "c_ncc")
        self.c_par = self.carve([128, 2], F32, "c_par")
        self.c_tmp = Ring(k, 3, [128, 512], F32, "c_tmp", self.carve)
        self.c_p = Ring(k, 3, [128, 512], BF16, "c_p", self.carve)
        self.c_on = self.carve([128, 512], F32, "c_on")
        self.c_rl = self.carve([128, 512], F32, "c_rl")
        self.c_sq = self.carve([128, 512], F32, "c_sq")
        self.c_ob = Ring(k, 2, [128, 512], BF16, "c_ob", self.carve)
        self.out_tiles += self.c_ob.tiles

    def _unit_c(self, u):
        k = self.k
        nc = self.nc
        sq = self.seq
        nb = sq // 128
        nI = sq // 512
        qk_d = self.din(f"c_qk_{u}", [2, 128, sq], BF16)
        v_d = self.din(f"c_v_{u}", [128, sq], BF16)
        f_d = self.din(f"c_f_{u}", [1, sq])
        par_d = self.din(f"c_par_{u}", [128, 2])
        o_d = self.dout(f"c_o_{u}", [128, sq], BF16)
        cq, ck, cv, cvt, cf, cnc, cncc, cpar = (self.c_q, self.c_k, self.c_v, self.c_vt, self.c_f, self.c_nc,
                                                self.c_ncc, self.c_par)
        k.dma(k.sp, cq[:, :], qk_d[0, :, :], cq, writes=[cq])
        k.dma(k.sp, ck[:, :], qk_d[1, :, :], ck, writes=[ck])
        k.dma(k.sp, cv[:, :], v_d[:, :], cv, writes=[cv])
        k.dma(k.sp, cf[:, :], f_d[0:1, :].partition_broadcast(128), cf, writes=[cf])
        k.dma(k.sp, cpar[:, :], par_d[:, :], cpar, writes=[cpar])
        k.op(k.dve, lambda e: e.tensor_scalar(out=cpar[:, 0:1], in0=cpar[:, 0:1], scalar1=-1.0, scalar2=None,
                                              op0=ALU.mult), reads=[cpar], writes=[cpar])
        k.op(k.act, lambda e: e.activation(out=cf[:, :], in_=cf[:, :], func=AF.Exp, bias=cpar[:, 0:1], scale=-1.0),
             reads=[cf, cpar], writes=[cf])
        k.op(k.act, lambda e: e.activation(out=cf[:, :], in_=cf[:, :], func=AF.Ln, bias=self.c_one[:, 0:1], scale=1.0),
             reads=[cf, self.c_one], writes=[cf])
        k.op(k.dve, lambda e: e.tensor_tensor_scan(out=cnc[:, :], data0=self.c_one[:, 0:1].to_broadcast([128, sq]),
                                                  data1=cf[:, :], initial=0.0, op0=ALU.mult, op1=ALU.add),
             reads=[cf, self.c_one], writes=[cnc])
        ptr = self.P[6]
        for g4 in range(nb // 4):
            for i in range(4):
                j = g4 * 4 + i
                k.op(k.pe, lambda e, i=i, j=j: e.transpose(ptr[:, i * 128:(i + 1) * 128], cnc[:, j * 128:(j + 1) * 128],
                                                           self.ident[:, :]),
                     reads=[cnc, self.ident], writes=[ptr], accum=(i > 0))
            k.op(k.dve, lambda e, g4=g4: e.tensor_copy(
                out=cncc[:, g4 * 4:(g4 + 1) * 4], in_=ptr[:, :].rearrange("p (a b) -> p a b", b=128)[:, :, 0]),
                reads=[ptr], writes=[cncc])
        pb = self.Pb
        for g8 in range(nb // 8):
            for i in range(8):
                j = g8 * 8 + i
                k.op(k.pe, lambda e, i=i, j=j: e.transpose(pb[:, i * 128:(i + 1) * 128], cv[:, j * 128:(j + 1) * 128],
                                                           self.identb[:, :]),
                     reads=[cv, self.identb], writes=[pb], accum=(i > 0))
            k.op(k.act, lambda e, g8=g8: e.copy(out=cvt[:, g8 * 8:(g8 + 1) * 8, :].rearrange("p a b -> p (a b)"),
                                                in_=pb[:, :]), reads=[pb], writes=[cvt])
        scale = float(HD ** -0.5)
        oacc, lacc, pss = self.P[3], self.P[4], self.P[5]
        for I in range(nI):
            jobs = list(range(4 * I + 4))
            pend = []

            def qk(j, I=I):
                t0 = max(512 * I, 128 * j)
                off = t0 - 512 * I
                ps = self.P[j % 3]
                with nc.allow_low_precision("bf16 matmul"):
                    k.op(k.pe, lambda e: e.matmul(ps[:, off:512], lhsT=ck[:, j * 128:(j + 1) * 128],
                                                  rhs=cq[:, t0:512 * (I + 1)], start=True, stop=True),
                         reads=[ck, cq], writes=[ps])
                tmp = self.c_tmp.next()
                k.op(k.dve, lambda e: e.scalar_tensor_tensor(out=tmp[:, off:512], in0=ps[:, off:512], scalar=scale,
                                                            in1=cnc[:, t0:512 * (I + 1)], op0=ALU.mult,
                                                            op1=ALU.subtract), reads=[ps, cnc], writes=[tmp])
                pt = self.c_p.next()
                k.op(k.act, lambda e: e.activation(out=pt[:, off:512], in_=tmp[:, off:512], func=AF.Exp,
                                                   bias=cncc[:, j:j + 1], scale=1.0), reads=[tmp, cncc], writes=[pt])
                if 128 * j >= 512 * I:
                    k.op(k.pool, lambda e: e.tensor_tensor(out=pt[:, off:off + 128], in0=pt[:, off:off + 128],
                                                           in1=self.mask_sub[:, :], op=ALU.mult),
                         reads=[pt, self.mask_sub], writes=[pt])
                return (j, off, pt)

            def pv(j, off, pt, I=I):
                last = 4 * I + 3
                with nc.allow_low_precision("bf16 matmul"):
                    k.op(k.pe, lambda e: e.matmul(oacc[:, off:512], lhsT=cvt[:, j, :], rhs=pt[:, off:512],
                                                  start=(j == 0), stop=(j == last)),
                         reads=[cvt, pt], writes=[oacc], accum=(j > 0))
                    k.op(k.pe, lambda e: e.matmul(lacc[:, off:512], lhsT=self.onesb[:, :], rhs=pt[:, off:512],
                                                  start=(j == 0), stop=(j == last)),
                         reads=[self.onesb, pt], writes=[lacc], accum=(j > 0))

            for j in jobs:
                pend.append(qk(j))
                if len(pend) > 2:
                    pv(*pend.pop(0))
            while pend:
                pv(*pend.pop(0))
            con, crl, csq = self.c_on, self.c_rl, self.c_sq
            k.op(k.dve, lambda e: e.reciprocal(out=crl[:, :], in_=lacc[:, :]), reads=[lacc], writes=[crl])
            k.op(k.dve, lambda e: e.tensor_tensor(out=con[:, :], in0=oacc[:, :], in1=crl[:, :], op=ALU.mult),
                 reads=[oacc, crl], writes=[con])
            k.op(k.act, lambda e: e.activation(out=csq[:, :], in_=con[:, :], func=AF.Square), reads=[con], writes=[csq])
            k.op(k.pe, lambda e: e.matmul(pss[:, :], lhsT=self.ones[:, :], rhs=csq[:, :], start=True, stop=True),
                 reads=[self.ones, csq], writes=[pss])
            k.op(k.act, lambda e: e.activation(out=crl[:, :], in_=pss[:, :], func=AF.Sqrt, bias=self.c_eps[:, 0:1],
                                               scale=1.0 / HD), reads=[pss, self.c_eps], writes=[crl])
            k.op(k.dve, lambda e: e.reciprocal(out=crl[:, :], in_=crl[:, :]), reads=[crl], writes=[crl])
            k.op(k.dve, lambda e: e.tensor_tensor(out=con[:, :], in0=con[:, :], in1=crl[:, :], op=ALU.mult),
                 reads=[con, crl], writes=[con])
            ob = self.c_ob.next()
            k.op(k.act, lambda e, ob=ob: e.activation(out=ob[:, :], in_=con[:, :], func=AF.Copy, scale=cpar[:, 1:2]),
                 reads=[con, cpar], writes=[ob])
            k.dma(k.sp, o_d[:, 512 * I:512 * (I + 1)], ob[:, :], ob, reads=[ob])

    def _alloc_a(self):
        k = self.k
        cv = self.carve
        SG = self.SG = min(2048, self.seq)
        NB = self.NB = SG // 128
        self.a_xin = Ring(k, 2, [128, SG + 4], F32, "a_xin", cv)
        self.a_qs = cv([128, SG], F32, "a_qs")
        self.a_ks = cv([128, SG], F32, "a_ks")
        self.a_vs = cv([128, SG], F32, "a_vs")
        self.a_sq = cv([128, SG], F32, "a_sq")
        self.a_rn = cv([128, SG], F32, "a_rn")
        self.a_br = cv([128, SG], F32, "a_br")
        self.a_gr = cv([128, SG], F32, "a_gr")
        self.a_Gr = cv([128, SG], F32, "a_Gr")
        self.a_m64 = cv([128, SG], F32, "a_m64")
        self.a_qg = cv([128, SG], BF16, "a_qg")
        self.a_wT = cv([128, SG], BF16, "a_wT")
        self.a_u = cv([128, NB, 128], F32, "a_u")
        self.a_qkT = cv([128, NB, 128], BF16, "a_qkT")
        self.a_kd = cv([128, NB, 128], BF16, "a_kd")
        self.a_o = cv([128, NB, 128], F32, "a_o")
        self.a_z = cv([128, NB, 128], F32, "a_z")
        self.a_ob = Ring(k, 2, [128, NB, 128], BF16, "a_ob", cv)
        self.out_tiles += self.a_ob.tiles
        self.a_col = cv([128, 2, NB], F32, "a_col")
        self.a_cg = cv([128, NB], F32, "a_cg")
        self.a_cG = cv([128, NB], F32, "a_cG")
        self.a_cGl = cv([128, NB], F32, "a_cGl")
        self.a_cbg = cv([128, NB], F32, "a_cbg")
        self.a_cdl = cv([128, NB], F32, "a_cdl")
        self.a_ss = cv([128, NB], F32, "a_ss")
        self.a_par = cv([128, 16], F32, "a_par")
        self.a_nea = cv([128, 1], F32, "a_nea")
        self.a_na = cv([128, 128], F32, "a_na")
        sm = lambda n: cv([128, 128], F32, n)
        self.a_dm, self.a_E, self.a_U0 = sm("a_dm"), sm("a_E"), sm("a_U0")
        self.a_X = [sm("a_X0"), sm("a_X1")]
        self.a_XT = [sm("a_XT0"), sm("a_XT1")]
        self.a_R = sm("a_R")
        self.a_kbg, self.a_vb = sm("a_kbg"), sm("a_vb")
        self.a_S = sm("a_S")
        self.a_Sb = cv([128, 128], BF16, "a_Sb")
        self.a_vn = Ring(k, 2, [128, 128], BF16, "a_vn", cv)
        self.maskU = sm("maskU")
        self.maskS = sm("maskS")
        self.triC = sm("triC")
        self.sameC = sm("sameC")
        mu_d = self.din("maskU_in", [128, 128])
        ms_d = self.din("maskS_in", [128, 128])
        sc_d = self.din("sameC_in", [128, 128])
        k.dma(k.sp, self.maskU[:, :], mu_d[:, :], self.maskU, writes=[self.maskU])
        k.dma(k.sp, self.maskS[:, :], ms_d[:, :], self.maskS, writes=[self.maskS])
        k.dma(k.sp, self.sameC[:, :], sc_d[:, :], self.sameC, writes=[self.sameC])
        m64 = self.a_m64
        k.op(k.dve, lambda e: e.memset(m64[:, :], 1.0), writes=[m64])
        k.op(k.dve, lambda e: e.memset(m64[:, :].rearrange("p (a b) -> p a b", b=64)[:, :, 0:1], 0.0), writes=[m64])

    def _unit_a(self, u):
        k = self.k
        nc = self.nc
        sq = self.seq
        SG, NB = self.SG, self.NB
        nseg = sq // SG
        qkv_d = self.din(f"a_qkv_{u}", [3, 128, sq])
        z_d = self.din(f"a_z_{u}", [sq, 128])
        bd_d = self.din(f"a_bd_{u}", [2, sq])
        bdc_d = self.din(f"a_bdc_{u}", [128, 2, sq // 128])
        par_d = self.din(f"a_par_{u}", [128, 16])
        na_d = self.din(f"a_na_{u}", [128, 128])
        o_d = self.dout(f"a_o_{u}", [sq, 128], BF16)
        par, nea, na = self.a_par, self.a_nea, self.a_na
        ident, ones = self.ident, self.ones
        k.dma(k.sp, par[:, :], par_d[:, :], par, writes=[par])
        k.dma(k.sp, na[:, :], na_d[:, :], na, writes=[na])
        k.op(k.act, lambda e: e.activation(out=nea[:, :], in_=par[:, 12:13], func=AF.Exp), reads=[par], writes=[nea])
        k.op(k.dve, lambda e: e.tensor_scalar(out=nea[:, :], in0=nea[:, :], scalar1=-1.0, scalar2=None, op0=ALU.mult),
             reads=[nea], writes=[nea])
        S, Sb = self.a_S, self.a_Sb
        k.op(k.dve, lambda e: e.memset(S[:, :], 0.0), writes=[S])
        k.op(k.dve, lambda e: e.memset(Sb[:, :], 0.0), writes=[Sb])
        P = self.P
        for sg in range(nseg):
            s0 = sg * SG
            outs = [self.a_qs, self.a_ks, self.a_vs]
            for i in range(3):
                xin = self.a_xin.next()
                if sg == 0:
                    k.op(k.dve, lambda e, xin=xin: e.memset(xin[:, 0:3], 0.0), writes=[xin])
                    k.dma(k.sp, xin[:, 3:3 + SG], qkv_d[i, :, 0:SG], xin, writes=[xin], join=False)
                else:
                    k.dma(k.sp, xin[:, 0:3 + SG], qkv_d[i, :, s0 - 3:s0 + SG], xin, writes=[xin])
                o = outs[i]
                eng = k.dve
                k.op(eng, lambda e, xin=xin, o=o, i=i: e.tensor_scalar(out=o[:, :], in0=xin[:, 3:3 + SG],
                                                                      scalar1=par[:, 4 * i + 3:4 * i + 4], scalar2=None,
                                                                      op0=ALU.mult), reads=[xin, par], writes=[o])
                for j in range(3):
                    k.op(eng, lambda e, xin=xin, o=o, i=i, j=j: e.scalar_tensor_tensor(
                        out=o[:, :], in0=xin[:, j:j + SG], scalar=par[:, 4 * i + j:4 * i + j + 1], in1=o[:, :],
                        op0=ALU.mult, op1=ALU.add), reads=[xin, par, o], writes=[o])
                k.op(k.act, lambda e, o=o: e.activation(out=o[:, :], in_=o[:, :], func=AF.Silu), reads=[o], writes=[o])
            qs, ks, vs, sqb, rn = self.a_qs, self.a_ks, self.a_vs, self.a_sq, self.a_rn
            for i, t_ in enumerate((qs, ks)):
                k.op(k.act, lambda e, t_=t_: e.activation(out=sqb[:, :], in_=t_[:, :], func=AF.Square),
                     reads=[t_], writes=[sqb])
                for c in range(SG // 512):
                    ps = P[c % 2]
                    k.op(k.pe, lambda e, c=c, ps=ps: e.matmul(ps[:, :], lhsT=ones[:, :], rhs=sqb[:, c * 512:(c + 1) * 512],
                                                              start=True, stop=True), reads=[ones, sqb], writes=[ps])
                    k.op(k.act, lambda e, c=c, ps=ps: e.activation(out=rn[:, c * 512:(c + 1) * 512], in_=ps[:, :],
                                                                   func=AF.Sqrt, bias=self.c_eps[:, 0:1], scale=1.0),
                         reads=[ps, self.c_eps], writes=[rn])
                k.op(k.dve, lambda e: e.reciprocal(out=rn[:, :], in_=rn[:, :]), reads=[rn], writes=[rn])
                if i == 0:
                    k.op(k.dve, lambda e: e.scalar_tensor_tensor(out=qs[:, :], in0=qs[:, :], scalar=float(HD ** -0.5),
                                                                in1=rn[:, :], op0=ALU.mult, op1=ALU.mult),
                         reads=[qs, rn], writes=[qs])
                else:
                    k.op(k.dve, lambda e: e.tensor_tensor(out=ks[:, :], in0=ks[:, :], in1=rn[:, :], op=ALU.mult),
                         reads=[ks, rn], writes=[ks])
            self.dbg("d_qs", qs, qs[:, :], [128, SG])
            self.dbg("d_ks", ks, ks[:, :], [128, SG])
            self.dbg("d_vs", vs, vs[:, :], [128, SG])
            br, gr, Gr = self.a_br, self.a_gr, self.a_Gr
            k.dma(k.sp, br[:, :], bd_d[0:1, s0:s0 + SG].partition_broadcast(128), br, writes=[br])
            k.dma(k.sp, gr[:, :], bd_d[1:2, s0:s0 + SG].partition_broadcast(128), gr, writes=[gr])
            k.op(k.act, lambda e: e.activation(out=br[:, :], in_=br[:, :], func=AF.Sigmoid), reads=[br], writes=[br])
            k.op(k.act, lambda e: e.activation(out=gr[:, :], in_=gr[:, :], func=AF.Exp, bias=par[:, 13:14], scale=1.0),
                 reads=[gr, par], writes=[gr])
            k.op(k.act, lambda e: e.activation(out=gr[:, :], in_=gr[:, :], func=AF.Ln, bias=self.c_one[:, 0:1], scale=1.0),
                 reads=[gr, self.c_one], writes=[gr])
            k.op(k.dve, lambda e: e.tensor_scalar(out=gr[:, :], in0=gr[:, :], scalar1=nea[:, 0:1], scalar2=None,
                                                  op0=ALU.mult), reads=[gr, nea], writes=[gr])
            k.op(k.dve, lambda e: e.tensor_tensor_scan(out=Gr[:, :], data0=self.a_m64[:, :], data1=gr[:, :], initial=0.0,
                                                      op0=ALU.mult, op1=ALU.add), reads=[gr, self.a_m64], writes=[Gr])
            k.op(k.act, lambda e: e.activation(out=gr[:, :], in_=Gr[:, :], func=AF.Exp), reads=[Gr], writes=[gr])
            qg = self.a_qg
            k.op(k.dve, lambda e: e.tensor_tensor(out=qg[:, :], in0=qs[:, :], in1=gr[:, :], op=ALU.mult),
                 reads=[qs, gr], writes=[qg])
            col, cg, cG, cGl, cbg, cdl = self.a_col, self.a_cg, self.a_cG, self.a_cGl, self.a_cbg, self.a_cdl
            k.dma(k.sp, col[:, :, :], bdc_d[:, :, sg * NB:(sg + 1) * NB], col, writes=[col])
            k.op(k.act, lambda e: e.activation(out=col[:, 0, :], in_=col[:, 0, :], func=AF.Sigmoid), reads=[col], writes=[col])
            k.op(k.act, lambda e: e.activation(out=cg[:, :], in_=col[:, 1, :], func=AF.Exp, bias=par[:, 13:14], scale=1.0),
                 reads=[col, par], writes=[cg])
            k.op(k.act, lambda e: e.activation(out=cg[:, :], in_=cg[:, :], func=AF.Ln, bias=self.c_one[:, 0:1], scale=1.0),
                 reads=[cg, self.c_one], writes=[cg])
            k.op(k.dve, lambda e: e.tensor_scalar(out=cg[:, :], in0=cg[:, :], scalar1=nea[:, 0:1], scalar2=None,
                                                  op0=ALU.mult), reads=[cg, nea], writes=[cg])
            k.op(k.pe, lambda e: e.matmul(P[0][:, 0:NB], lhsT=self.maskU[:, :], rhs=cg[:, :], start=True, stop=True),
                 reads=[self.maskU, cg], writes=[P[0]])
            k.op(k.pe, lambda e: e.matmul(P[1][:, 0:NB], lhsT=self.sameC[:, :], rhs=cg[:, :], start=True, stop=True),
                 reads=[self.sameC, cg], writes=[P[1]])
            k.op(k.dve, lambda e: e.tensor_copy(out=cG[:, :], in_=P[0][:, 0:NB]), reads=[P[0]], writes=[cG])
            k.op(k.dve, lambda e: e.tensor_tensor(out=cdl[:, :], in0=P[1][:, 0:NB], in1=cG[:, :], op=ALU.subtract),
                 reads=[P[1], cG], writes=[cdl])
            k.op(k.act, lambda e: e.activation(out=cdl[:, :], in_=cdl[:, :], func=AF.Exp), reads=[cdl], writes=[cdl])
            k.op(k.act, lambda e: e.activation(out=cbg[:, :], in_=cG[:, :], func=AF.Exp), reads=[cG], writes=[cbg])
            k.op(k.dve, lambda e: e.tensor_tensor(out=cbg[:, :], in0=cbg[:, :], in1=col[:, 0, :], op=ALU.mult),
                 reads=[cbg, col], writes=[cbg])
            self.dbg("d_br", br, br[:, :], [128, SG])
            self.dbg("d_Gr", Gr, Gr[:, :], [128, SG])
            self.dbg("d_eG", gr, gr[:, :], [128, SG])
            self.dbg("d_cG", cG, cG[:, :], [128, NB])
            self.dbg("d_cdl", cdl, cdl[:, :], [128, NB])
            self.dbg("d_cbg", cbg, cbg[:, :], [128, NB])
            dm, E, U0, R = self.a_dm, self.a_E, self.a_U0, self.a_R
            for b in range(NB):
                bs = slice(b * 128, (b + 1) * 128)
                k.op(k.pe, lambda e, bs=bs: e.matmul(P[0][:, 0:128], lhsT=ks[:, bs], rhs=ks[:, bs], start=True, stop=True),
                     reads=[ks], writes=[P[0]])
                k.op(k.pe, lambda e, bs=bs: e.matmul(P[1][:, 0:128], lhsT=ks[:, bs], rhs=qs[:, bs], start=True, stop=True),
                     reads=[ks, qs], writes=[P[1]])
                k.op(k.dve, lambda e, bs=bs, b=b: e.tensor_scalar(out=dm[:, :], in0=Gr[:, bs], scalar1=cG[:, b:b + 1],
                                                                 scalar2=0.0, op0=ALU.subtract, op1=ALU.min),
                     reads=[Gr, cG], writes=[dm])
                k.op(k.act, lambda e: e.activation(out=E[:, :], in_=dm[:, :], func=AF.Exp), reads=[dm], writes=[E])
                k.op(k.dve, lambda e: e.tensor_tensor(out=E[:, :], in0=E[:, :], in1=self.maskU[:, :], op=ALU.mult),
                     reads=[E, self.maskU], writes=[E])
                k.op(k.dve, lambda e, b=b: e.tensor_tensor(out=self.a_qkT[:, b, :], in0=P[1][:, 0:128], in1=E[:, :],
                                                           op=ALU.mult), reads=[P[1], E], writes=[self.a_qkT])
                k.op(k.dve, lambda e: e.tensor_tensor(out=U0[:, :], in0=P[0][:, 0:128], in1=E[:, :], op=ALU.mult),
                     reads=[P[0], E], writes=[U0])
                k.op(k.pool, lambda e: e.tensor_tensor(out=U0[:, :], in0=U0[:, :], in1=self.maskS[:, :], op=ALU.mult),
                     reads=[U0, self.maskS], writes=[U0])
                X, XT = self.a_X[0], self.a_XT[0]
                k.op(k.pool, lambda e, bs=bs, X=X: e.tensor_tensor(out=X[:, :], in0=U0[:, :], in1=br[:, bs], op=ALU.mult),
                     reads=[U0, br], writes=[X])
                k.op(k.pe, lambda e, X=X: e.transpose(P[3][:, 0:128], X[:, :], ident[:, :]), reads=[X, ident],
                     writes=[P[3]])
                k.op(k.act, lambda e, XT=XT: e.copy(out=XT[:, :], in_=P[3][:, 0:128]), reads=[P[3]], writes=[XT])
                k.op(k.dve, lambda e, X=X: e.tensor_tensor(out=R[:, :], in0=ident[:, :], in1=X[:, :], op=ALU.subtract),
                     reads=[ident, X], writes=[R])
                self.dbg("d_U", X, X[:, :], [128, 128])
                self.dbg("d_UT", XT, XT[:, :], [128, 128])
                self.dbg("d_E", E, E[:, :], [128, 128])
                for lvl in range(1, 6):
                    Xn, XTn = self.a_X[lvl % 2], self.a_XT[lvl % 2]
                    if lvl < 5:
                        k.op(k.pe, lambda e, X=X, XT=XT: e.matmul(P[0][:, 0:128], lhsT=XT[:, :], rhs=X[:, :], start=True,
                                                                  stop=True), reads=[X, XT], writes=[P[0]])
                    k.op(k.pe, lambda e, X=X, XT=XT: e.matmul(P[1][:, 0:128], lhsT=X[:, :], rhs=XT[:, :], start=True,
                                                              stop=True), reads=[X, XT], writes=[P[1]])
                    if lvl < 5:
                        k.op(k.act, lambda e, Xn=Xn: e.copy(out=Xn[:, :], in_=P[0][:, 0:128]), reads=[P[0]], writes=[Xn])
                    k.op(k.dve, lambda e, XTn=XTn: e.tensor_copy(out=XTn[:, :], in_=P[1][:, 0:128]), reads=[P[1]],
                         writes=[XTn])
                    k.op(k.pe, lambda e, XTn=XTn: e.matmul(P[2][:, 0:128], lhsT=XTn[:, :], rhs=R[:, :], start=True,
                                                           stop=True), reads=[XTn, R], writes=[P[2]])
                    k.op(k.dve, lambda e: e.tensor_tensor(out=R[:, :], in0=R[:, :], in1=P[2][:, 0:128], op=ALU.add),
                         reads=[R, P[2]], writes=[R])
                    X, XT = Xn, XTn
                self.dbg("d_R", R, R[:, :], [128, 128])
                kbg, vb = self.a_kbg, self.a_vb
                k.op(k.pe, lambda e, bs=bs: e.transpose(P[3][:, 0:128], ks[:, bs], ident[:, :]), reads=[ks, ident],
                     writes=[P[3]])
                k.op(k.act, lambda e, b=b: e.activation(out=kbg[:, :], in_=P[3][:, 0:128], func=AF.Copy,
                                                        scale=cbg[:, b:b + 1]), reads=[P[3], cbg], writes=[kbg])
                k.op(k.dve, lambda e, b=b: e.tensor_scalar(out=self.a_kd[:, b, :], in0=P[3][:, 0:128],
                                                           scalar1=cdl[:, b:b + 1], scalar2=None, op0=ALU.mult),
                     reads=[P[3], cdl], writes=[self.a_kd])
                k.op(k.pe, lambda e, bs=bs: e.transpose(P[3][:, 0:128], vs[:, bs], ident[:, :]), reads=[vs, ident],
                     writes=[P[3]])
                k.op(k.act, lambda e, b=b: e.activation(out=vb[:, :], in_=P[3][:, 0:128], func=AF.Copy,
                                                        scale=col[:, 0, b:b + 1]), reads=[P[3], col], writes=[vb])
                k.op(k.pe, lambda e: e.matmul(P[0][:, 0:128], lhsT=R[:, :], rhs=vb[:, :], start=True, stop=True),
                     reads=[R, vb], writes=[P[0]])
                k.op(k.pe, lambda e: e.matmul(P[1][:, 0:128], lhsT=kbg[:, :], rhs=R[:, :], start=True, stop=True),
                     reads=[R, kbg], writes=[P[1]])
                k.op(k.act, lambda e, b=b: e.copy(out=self.a_u[:, b, :], in_=P[0][:, 0:128]), reads=[P[0]],
                     writes=[self.a_u])
                k.op(k.dve, lambda e, bs=bs: e.tensor_copy(out=self.a_wT[:, bs], in_=P[1][:, 0:128]), reads=[P[1]],
                     writes=[self.a_wT])
            self.dbg("d_u", self.a_u, self.a_u[:, :, :], [128, NB, 128])
            self.dbg("d_wT", self.a_wT, self.a_wT[:, :], [128, SG], BF16)
            self.dbg("d_qkT", self.a_qkT, self.a_qkT[:, :, :], [128, NB, 128], BF16)
            self.dbg("d_kd", self.a_kd, self.a_kd[:, :, :], [128, NB, 128], BF16)
            k.dma(k.sp, self.a_z[:, :, :], z_d[s0:s0 + SG, :].rearrange("(b p) e -> p b e", p=128), self.a_z,
                  writes=[self.a_z])
            wT, au, qkT, kd, ao = self.a_wT, self.a_u, self.a_qkT, self.a_kd, self.a_o
            for n in range(2 * NB):
                b, hf = n // 2, n % 2
                bs = slice(b * 128, (b + 1) * 128)
                rs = slice(hf * 64, hf * 64 + 64)
                vn = self.a_vn.next()
                with nc.allow_low_precision("bf16 matmul"):
                    k.op(k.pe, lambda e, bs=bs: e.matmul(P[4][:, 0:128], lhsT=wT[:, bs], rhs=Sb[:, :], start=True, stop=True),
                         reads=[wT, Sb], writes=[P[4]])
                    k.op(k.dve, lambda e, rs=rs, b=b, vn=vn: e.tensor_tensor(out=vn[rs, :], in0=au[rs, b, :],
                                                                            in1=P[4][rs, 0:128], op=ALU.subtract),
                         reads=[au, P[4]], writes=[vn])
                    k.op(k.pe, lambda e, bs=bs: e.matmul(P[5][:, 0:128], lhsT=qg[:, bs], rhs=Sb[:, :], start=True, stop=False),
                         reads=[qg, Sb], writes=[P[5]])
                    k.op(k.pe, lambda e, rs=rs, b=b, vn=vn: e.matmul(P[5][:, 0:128], lhsT=qkT[rs, b, :], rhs=vn[rs, :],
                                                                    start=False, stop=True),
                         reads=[qkT, vn], writes=[P[5]], accum=True)
                    k.op(k.pe, lambda e, rs=rs, b=b, vn=vn: e.matmul(P[6][:, 0:128], lhsT=kd[rs, b, :], rhs=vn[rs, :],
                                                                    start=True, stop=True),
                         reads=[kd, vn], writes=[P[6]])
                c63 = n * 64 + 63
                k.op(k.dve, lambda e, c63=c63: e.scalar_tensor_tensor(out=S[:, :], in0=S[:, :], scalar=gr[:, c63:c63 + 1],
                                                                     in1=P[6][:, 0:128], op0=ALU.mult, op1=ALU.add),
                     reads=[S, gr, P[6]], writes=[S])
                k.op(k.act, lambda e: e.copy(out=Sb[:, :], in_=S[:, :]), reads=[S], writes=[Sb])
                k.op(k.act, lambda e, rs=rs, b=b: e.copy(out=ao[rs, b, :], in_=P[5][rs, 0:128]), reads=[P[5]], writes=[ao])
            self.dbg("d_o", ao, ao[:, :, :], [128, NB, 128])
            ss, az = self.a_ss, self.a_z
            sq3 = self.a_sq[:, :].rearrange("p (a b) -> p a b", b=128)
            k.op(k.act, lambda e: e.activation(out=sq3, in_=ao[:, :, :], func=AF.Square), reads=[ao], writes=[self.a_sq])
            k.op(k.dve, lambda e: e.reduce_sum(out=ss[:, :], in_=sq3, axis=mybir.AxisListType.X), reads=[self.a_sq],
                 writes=[ss])
            k.op(k.act, lambda e: e.activation(out=ss[:, :], in_=ss[:, :], func=AF.Sqrt, bias=self.c_eps[:, 0:1],
                                               scale=1.0 / HD), reads=[ss, self.c_eps], writes=[ss])
            k.op(k.dve, lambda e: e.reciprocal(out=ss[:, :], in_=ss[:, :]), reads=[ss], writes=[ss])
            k.op(k.act, lambda e: e.activation(out=az[:, :, :], in_=az[:, :, :], func=AF.Silu), reads=[az], writes=[az])
            k.op(k.dve, lambda e: e.tensor_tensor(out=ao[:, :, :], in0=ao[:, :, :],
                                                  in1=ss[:, :].unsqueeze(2).to_broadcast([128, NB, 128]), op=ALU.mult),
                 reads=[ao, ss], writes=[ao])
            k.op(k.dve, lambda e: e.tensor_tensor(out=ao[:, :, :], in0=ao[:, :, :],
                                                  in1=na[:, :].unsqueeze(1).to_broadcast([128, NB, 128]), op=ALU.mult),
                 reads=[ao, na], writes=[ao])
            ob = self.a_ob.next()
            k.op(k.dve, lambda e, ob=ob: e.tensor_tensor(out=ob[:, :, :], in0=ao[:, :, :], in1=az[:, :, :], op=ALU.mult),
                 reads=[ao, az], writes=[ob])
            k.dma(k.sp, o_d[s0:s0 + SG, :].rearrange("(b p) e -> p b e", p=128), ob[:, :, :], ob, reads=[ob])


class AdaProg:
    NCOL = 9 * D // NCORE

    def __init__(self):
        nc = bass.Bass("TRN2", target_bir_lowering=False)
        self.nc = nc
        NCOL = self.NCOL
        w_d = nc.dram_tensor("ada_w", [DEPTH, D, NCOL], F32, kind="ExternalInput").ap()
        b_d = nc.dram_tensor("ada_b", [DEPTH, 2, NCOL], F32, kind="ExternalInput").ap()
        c_d = nc.dram_tensor("ada_c", [128, KC, 2], F32, kind="ExternalInput").ap()
        o_d = nc.dram_tensor("ada_o", [DEPTH, 2, NCOL], F32, kind="ExternalOutput").ap()
        with ExitStack() as st:
            k = KB(nc, st)
            ca = k.sb([128, KC, 2], F32, "ca")
            k.dma(k.sp, ca[:, :, :], c_d[:, :, :], ca, writes=[ca])
            k.op(k.act, lambda e: e.activation(out=ca[:, :, :], in_=ca[:, :, :], func=AF.Silu), reads=[ca], writes=[ca])
            HW = 2560
            wr = Ring(k, 3, [128, HW], F32, "wr")
            P = [k.ps([128, 512], F32, f"P{i}") for i in range(5)]
            bt = k.sb([2, DEPTH, NCOL], F32, "bt")
            res = k.sb([2, DEPTH, NCOL], F32, "res")
            for l in range(DEPTH):
                k.dma(k.sp, bt[:, l, :], b_d[l, :, :], bt, writes=[bt], join=(l > 0))
            for l in range(DEPTH):
                for c0 in (0, HW):
                    cw = min(HW, NCOL - c0)
                    nb = (cw + 511) // 512
                    for kc in range(KC):
                        wt = wr.next()
                        k.dma(k.sp if kc % 2 == 0 else k.act, wt[:, 0:cw], w_d[l, kc * 128:(kc + 1) * 128, c0:c0 + cw], wt,
                              writes=[wt])
                        for bi in range(nb):
                            n = min(512, cw - bi * 512)
                            k.op(k.pe, lambda e, bi=bi, n=n, kc=kc, wt=wt: e.matmul(
                                P[bi][0:2, 0:n], lhsT=ca[:, kc, :], rhs=wt[:, bi * 512:bi * 512 + n],
                                start=(kc == 0), stop=(kc == KC - 1)), reads=[ca, wt], writes=[P[bi]], accum=(kc > 0))
                    for bi in range(nb):
                        n = min(512, cw - bi * 512)
                        k.op(k.dve, lambda e, bi=bi, n=n, l=l, c0=c0: e.tensor_tensor(
                            out=res[0:2, l, c0 + bi * 512:c0 + bi * 512 + n], in0=P[bi][0:2, 0:n],
                            in1=bt[0:2, l, c0 + bi * 512:c0 + bi * 512 + n], op=ALU.add),
                            reads=[P[bi], bt], writes=[res])
            for l in range(DEPTH):
                k.dma(k.sp, o_d[l, :, :], res[0:2, l, :], res, reads=[res])
            k.wait_tiles(k.sp, [res])
            k.finish()


_PROGS = {}


def _prog(key, fn):
    if key not in _PROGS:
        _PROGS[key] = fn()
    return _PROGS[key]


def _run(prog_nc, in_maps):
    res = run_bass_kernel_spmd(prog_nc, in_maps, core_ids=list(range(NCORE)))
    return res.results


def _consts():
    idx = np.arange(128)
    same = (idx[:, None] // 64) == (idx[None, :] // 64)
    return {
        "ident_in": np.eye(128, dtype=np.float32),
        "mask_su_in": np.triu(np.ones((128, 128), np.float32)),
        "maskU_in": (same & (idx[:, None] <= idx[None, :])).astype(np.float32),
        "maskS_in": (same & (idx[:, None] < idx[None, :])).astype(np.float32),
        "sameC_in": same.astype(np.float32),
    }


def _pick(d, names):
    return {n: d[n] for n in names}


def kernel(x, c, w_ada, b_ada, ln_g, ln_b, w_ffn_in, w_ffn_out, w_in, conv_w, a_log, dt_bias, norm_a,
           sgu_ln_g, sgu_ln_b, w_s, b_s, b_f, norm_c, w_o):
    f32 = np.float32
    x = np.asarray(x, f32)
    consts = _consts()
    ada = _prog("ada", AdaProg)
    NCOL = AdaProg.NCOL
    cT = np.ascontiguousarray(np.asarray(c, f32).reshape(B, KC, 128).transpose(2, 1, 0))
    in_maps = []
    for i in range(NCORE):
        cs = slice(i * NCOL, (i + 1) * NCOL)
        in_maps.append({
            "ada_w": np.ascontiguousarray(np.asarray(w_ada)[:, :, cs]),
            "ada_b": np.ascontiguousarray(np.broadcast_to(np.asarray(b_ada, f32)[:, None, cs], (DEPTH, 2, NCOL))),
            "ada_c": cT,
        })
    r = _run(ada.nc, in_maps)
    del in_maps
    mod = np.concatenate([r[i]["ada_o"] for i in range(NCORE)], axis=-1)
    mods = mod.reshape(DEPTH, B, 9, D)

    def stage_vecs(tag, l, b_, kind, j=0):
        out = {}
        if kind == "ffn":
            base, lni = (0, 0) if j == 0 else (6, 2)
            out[f"sh_{tag}"] = vec_pp(mods[l, b_, base])
            out[f"sc_{tag}"] = vec_pp(mods[l, b_, base + 1])
            out[f"ga_{tag}"] = vec_pp(mods[l, b_, base + 2])
            out[f"lng_{tag}"] = vec_pp(ln_g[l][lni])
            out[f"lnb_{tag}"] = vec_pp(ln_b[l][lni])
        elif kind == "mixin":
            out[f"sh_{tag}"] = vec_pp(mods[l, b_, 3])
            out[f"sc_{tag}"] = vec_pp(mods[l, b_, 4])
        elif kind == "mixout":
            out[f"ga_{tag}"] = vec_pp(mods[l, b_, 5])
            out[f"lng_{tag}"] = vec_pp(ln_g[l][1])
            out[f"lnb_{tag}"] = vec_pp(ln_b[l][1])
        return out

    def core_tok(i):
        return i // 4, (i % 4) * NTOK

    def weights_for(stages):
        w = {}
        for kind, tag, l, j in stages:
            if kind == "ffn":
                w[f"wfi_{tag}"] = prep_ffn_in(np.asarray(w_ffn_in[l][j], f32))
                w[f"wfo_{tag}"] = prep_ffn_out(np.asarray(w_ffn_out[l][j], f32))
            elif kind == "mixin":
                w[f"win_{tag}"] = prep_win(np.asarray(w_in[l], f32))
                w.update(b_consts(tag, np.asarray(w_s[l], f32), np.asarray(b_s[l], f32), np.asarray(sgu_ln_g[l], f32),
                                  np.asarray(sgu_ln_b[l], f32)))
            elif kind == "mixout":
                w[f"wo_{tag}"] = prep_wo(np.asarray(w_o[l], f32))
        return w

    def run_tok(stages, xT_list, mix_list=None):
        key = ("tok",) + tuple((kd, tg) for kd, tg, _, _ in stages)
        prog = _prog(key, lambda: TokProg([(kd, tg) for kd, tg, _, _ in stages]))
        w = weights_for(stages)
        in_maps = []
        for i in range(NCORE):
            b_, s0 = core_tok(i)
            m = {"ident_in": consts["ident_in"], "x_in": xT_list[i]}
            m.update(w)
            for kind, tag, l, j in stages:
                m.update(stage_vecs(tag, l, b_, kind, j))
                if kind == "mixout":
                    m[f"mix_{tag}"] = mix_list[i]
            in_maps.append(_pick(m, prog.in_names))
        r_ = _run(prog.nc, in_maps)
        del in_maps, w
        return r_

    def run_head(l, EA, EC, ES):
        prog = _prog("head", lambda: HeadProg(3, 3, S))
        in_maps = []
        for i in range(NCORE):
            m = dict(consts)
            for sl in range(3):
                uid = i * 3 + sl
                b_, h = uid // H_A, uid % H_A
                m[f"a_qkv_{sl}"] = np.ascontiguousarray(np.stack(
                    [EA[b_][t_ * D_A + h * 128:t_ * D_A + (h + 1) * 128] for t_ in range(3)]))
                m[f"a_z_{sl}"] = np.ascontiguousarray(EA[b_][3 * D_A + h * 128:3 * D_A + (h + 1) * 128].T)
                bd = np.ascontiguousarray(np.stack([ES[b_][h], ES[b_][H_A + h]]))
                m[f"a_bd_{sl}"] = bd
                m[f"a_bdc_{sl}"] = np.ascontiguousarray(bd.reshape(2, S // 128, 128).transpose(2, 0, 1))
                par = np.zeros((128, 16), f32)
                for t_ in range(3):
                    for j in range(4):
                        par[:, 4 * t_ + j] = conv_w[l][j, t_ * D_A + h * 128:t_ * D_A + (h + 1) * 128]
                par[:, 12] = a_log[l][h]
                par[:, 13] = dt_bias[l][h]
                m[f"a_par_{sl}"] = par
                m[f"a_na_{sl}"] = np.ascontiguousarray(np.broadcast_to(np.asarray(norm_a[l], f32)[None], (128, 128)))
                m[f"c_qk_{sl}"] = np.ascontiguousarray(np.stack(
                    [EC[b_][t_ * D_C + h * 128:t_ * D_C + (h + 1) * 128] for t_ in range(2)]))
                m[f"c_v_{sl}"] = np.ascontiguousarray(EC[b_][2 * D_C + h * 128:2 * D_C + (h + 1) * 128])
                m[f"c_f_{sl}"] = np.ascontiguousarray(ES[b_][2 * H_A + h][None])
                cp = np.zeros((128, 2), f32)
                cp[:, 0] = b_f[l][h]
                cp[:, 1] = norm_c[l]
                m[f"c_par_{sl}"] = cp
            in_maps.append(_pick(m, prog.in_names))
        r_ = _run(prog.nc, in_maps)
        o_a = [[None] * H_A for _ in range(B)]
        o_c = [[None] * H_C for _ in range(B)]
        for i in range(NCORE):
            for sl in range(3):
                uid = i * 3 + sl
                b_, h = uid // H_A, uid % H_A
                o_a[b_][h] = r_[i][f"a_o_{sl}"]
                o_c[b_][h] = r_[i][f"c_o_{sl}"]
        return o_a, o_c

    def gather(r_, name):
        return [np.concatenate([r_[b_ * 4 + q][name] for q in range(4)], axis=1) for b_ in range(B)]

    def build_mix(o_a, o_c, EB):
        mix = []
        for i in range(NCORE):
            b_, s0 = core_tok(i)
            rows = [np.ascontiguousarray(o_a[b_][h][s0:s0 + NTOK].T) for h in range(H_A)]
            rows.append(EB[i])
            rows += [o_c[b_][h][:, s0:s0 + NTOK] for h in range(H_C)]
            mix.append(np.ascontiguousarray(np.concatenate(rows, axis=0)))
        return mix

    xT = []
    for i in range(NCORE):
        b_, s0 = core_tok(i)
        xT.append(np.ascontiguousarray(x[b_, s0:s0 + NTOK].T))
    r1 = run_tok([("ffn", "f00", 0, 0), ("mixin", "m0", 0, 0)], xT)
    x1 = [r1[i]["x_out"] for i in range(NCORE)]
    EB = [r1[i]["EB_m0"] for i in range(NCORE)]
    o_a, o_c = run_head(0, gather(r1, "EA_m0"), gather(r1, "EC_m0"), gather(r1, "ES_m0"))
    del r1
    mix = build_mix(o_a, o_c, EB)
    r3 = run_tok([("mixout", "o0", 0, 0), ("ffn", "f01", 0, 1), ("ffn", "f10", 1, 0), ("mixin", "m1", 1, 0)], x1, mix)
    x1 = [r3[i]["x_out"] for i in range(NCORE)]
    EB = [r3[i]["EB_m1"] for i in range(NCORE)]
    o_a, o_c = run_head(1, gather(r3, "EA_m1"), gather(r3, "EC_m1"), gather(r3, "ES_m1"))
    del r3
    mix = build_mix(o_a, o_c, EB)
    r5 = run_tok([("mixout", "o1", 1, 0), ("ffn", "f11", 1, 1)], x1, mix)
    out = np.empty((B, S, D), f32)
    for i in range(NCORE):
        b_, s0 = core_tok(i)
        out[b_, s0:s0 + NTOK] = r5[i]["x_out"].T
    return out
```

```python
import numpy as np
from contextlib import ExitStack
import concourse.bass as bass
import concourse.mybir as mybir
from concourse.bass_utils import run_bass_kernel_spmd
import ml_dtypes

F32 = mybir.dt.float32
BF16 = mybir.dt.bfloat16
AF = mybir.ActivationFunctionType
ALU = mybir.AluOpType
NPBF = ml_dtypes.bfloat16

D = 4096
B = 2
S = 8192
DEPTH = 2
HD = 128
H_A = 12
H_B = 8
H_C = 12
D_A = 1536
D_B = 1024
D_C = 1536
DFF = 8192
NCORE = 8
NTOK = 2048
T = 512
NTT = NTOK // T
KC = D // 128
ALPHA = (2.0 * DEPTH) ** 0.25
LN_EPS = 1e-5
RMS_EPS = 1e-6


class Tl:
    __slots__ = ("h", "w", "r", "dsem", "dcnt", "name", "excl")

    def __init__(self, h, name):
        self.excl = False
        self.h = h
        self.name = name
        self.w = None
        self.r = {}
        self.dsem = None
        self.dcnt = 0

    def __getitem__(self, k):
        return self.h[k]


class Eng:
    def __init__(self, key, h, sem):
        self.key = key
        self.h = h
        self.sem = sem
        self.cnt = 0
        self.seen = {}
        self.prog = []


class KB:
    def __init__(self, nc, stack):
        self.nc = nc
        self.stack = stack
        self.pe = self._eng("pe", nc.tensor)
        self.act = self._eng("act", nc.scalar)
        self.dve = self._eng("dve", nc.vector)
        self.pool = self._eng("pool", nc.gpsimd)
        self.sp = self._eng("sp", nc.sync)
        self.nt = 0
        self.dma_tls = []

    def _sem(self, name):
        return self.stack.enter_context(self.nc.semaphore(name))

    def _eng(self, key, h):
        return Eng(key, h, self._sem("s_" + key))

    def sb(self, shape, dt, name):
        h = self.stack.enter_context(self.nc.sbuf_tensor("sb_" + name, list(shape), dt))
        return Tl(h, name)

    def ps(self, shape, dt, name):
        h = self.stack.enter_context(self.nc.psum_tensor(name, list(shape), dt))
        t = Tl(h, name)
        t.excl = True
        return t

    def sub(self, name):
        self.nt += 1
        return Tl(None, f"{name}_{self.nt}")

    def _deps(self, eng, reads, writes, accum=False, join=None):
        deps = {}

        def add(tok):
            if tok is None:
                return
            key, sem, val = tok
            if key not in deps or deps[key][1] < val:
                deps[key] = (sem, val)

        for t in reads:
            add(t.w)
            if t.excl:
                for kk, tok in t.r.items():
                    if kk != eng.key:
                        add(tok)
        for t in writes:
            w = t.w
            if w is not None:
                if accum and w[0] == eng.key:
                    pass
                elif join is not None and w[0] == join:
                    pass
                else:
                    add(w)
            for tok in t.r.values():
                add(tok)
        waits = []
        for key, (sem, val) in deps.items():
            if eng.seen.get(key, 0) >= val:
                continue
            eng.seen[key] = val
            waits.append((sem, val))
        return waits

    @staticmethod
    def _mark(tok, reads, writes):
        key = tok[0]
        for t in reads:
            t.r[key] = tok
        for t in writes:
            t.w = tok
            t.r = {}

    def op(self, eng, fn, reads=(), writes=(), accum=False):
        waits = self._deps(eng, reads, writes, accum=accum)
        eng.cnt += 1
        tok = (eng.key, eng.sem, eng.cnt)
        eng.prog.append((waits, fn, eng.sem, 1))
        self._mark(tok, reads, writes)
        return tok

    def dma(self, eng, out_ap, in_ap, sbt, reads=(), writes=(), join=False, **kw):
        if sbt.dsem is None:
            sbt.dsem = self._sem("d_" + sbt.name)
            self.dma_tls.append(sbt)
        dkey = ("d", sbt.name)
        waits = self._deps(eng, reads, writes, join=dkey if join else None)
        sbt.dcnt += 16
        tok = (dkey, sbt.dsem, sbt.dcnt)
        eng.prog.append((waits, lambda e: e.dma_start(out=out_ap, in_=in_ap, **kw), sbt.dsem, 16))
        self._mark(tok, reads, writes)
        return tok

    def barrier(self):
        engs = [self.pe, self.act, self.dve, self.pool, self.sp]
        for e in engs:
            waits = []
            for o in engs:
                if o is e or o.cnt == 0:
                    continue
                if e.seen.get(o.key, 0) < o.cnt:
                    e.seen[o.key] = o.cnt
                    waits.append((o.sem, o.cnt))
            for t in self.dma_tls:
                dk = ("d", t.name)
                if e.seen.get(dk, 0) < t.dcnt:
                    e.seen[dk] = t.dcnt
                    waits.append((t.dsem, t.dcnt))
            if waits:
                e.prog.append((waits, None, None, 0))

    def wait_tiles(self, eng, tiles):
        waits = self._deps(eng, (), tiles)
        if waits:
            eng.prog.append((waits, None, None, 0))

    def finish(self):
        nc = self.nc
        with nc.Block() as block:
            def run(eng):
                def body(e):
                    for waits, fn, sem, inc in eng.prog:
                        for s, v in waits:
                            e.wait_ge(s, v)
                        if fn is not None:
                            fn(e).then_inc(sem, inc)
                return body

            block.tensor(run(self.pe))
            block.scalar(run(self.act))
            block.vector(run(self.dve))
            block.gpsimd(run(self.pool))
            block.sync(run(self.sp))


class Ring:
    def __init__(self, k, n, shape, dt, name, alloc=None):
        alloc = alloc or k.sb
        self.tiles = [alloc(shape, dt, f"{name}{i}") for i in range(n)]
        self.i = 0

    def next(self):
        t = self.tiles[self.i % len(self.tiles)]
        self.i += 1
        return t


def vec_pp(v):
    v = np.asarray(v, np.float32)
    return np.ascontiguousarray(v.reshape(-1, 128).T)


def blocks_in(W, cols_blocks):
    K = W.shape[0]
    kc = K // 128
    nblk, ncol = cols_blocks.shape
    Wp = np.concatenate([W, np.zeros((K, 1), W.dtype)], axis=1)
    out = np.empty((nblk, 128, kc, ncol), np.float32)
    for b in range(nblk):
        blk = Wp[:, cols_blocks[b]]
        out[b] = blk.reshape(kc, 128, ncol).transpose(1, 0, 2)
    return out.reshape(nblk, 128, kc * ncol)


OFF_QKVA = 0
OFF_Z = 4608
OFF_BETA = 6144
OFF_DEC = 6156
OFF_UV = 6168
OFF_QKVC = 8216
OFF_F = 12824
D_IN = 12836


def mixin_block_plan():
    blocks = []
    for i in range(36):
        blocks.append(("A", i, np.arange(OFF_QKVA + i * 128, OFF_QKVA + (i + 1) * 128)))
    for i in range(12):
        blocks.append(("A", 36 + i, np.arange(OFF_Z + i * 128, OFF_Z + (i + 1) * 128)))
    for i in range(36):
        blocks.append(("C", i, np.arange(OFF_QKVC + i * 128, OFF_QKVC + (i + 1) * 128)))
    for i in range(8):
        blocks.append(("U", i, np.arange(OFF_UV + i * 128, OFF_UV + (i + 1) * 128)))
    for i in range(8):
        blocks.append(("V", i, np.arange(OFF_UV + 1024 + i * 128, OFF_UV + 1024 + (i + 1) * 128)))
    small = np.full(128, -1, np.int64)
    small[0:12] = np.arange(OFF_BETA, OFF_BETA + 12)
    small[12:24] = np.arange(OFF_DEC, OFF_DEC + 12)
    small[24:36] = np.arange(OFF_F, OFF_F + 12)
    blocks.append(("S", 0, small))
    blocks.append(("P", 0, np.full(128, -1, np.int64)))
    return blocks


class TokProg:
    def __init__(self, stages, ntt=NTT, debug=False):
        self.debug = debug
        self.stages = stages
        self.ntt = ntt
        self.ntok = ntt * T
        nc = bass.Bass("TRN2", target_bir_lowering=False)
        self.nc = nc
        self.in_names = []
        self.out_names = []
        with ExitStack() as st:
            self.k = KB(nc, st)
            self._alloc()
            self._emit()
            self.k.finish()

    def din(self, name, shape, dt=F32):
        self.in_names.append(name)
        return self.nc.dram_tensor(name, list(shape), dt, kind="ExternalInput").ap()

    def dout(self, name, shape, dt=F32):
        self.out_names.append(name)
        return self.nc.dram_tensor(name, list(shape), dt, kind="ExternalOutput").ap()

    def dint(self, name, shape, dt=F32):
        return self.nc.dram_tensor(name, list(shape), dt, kind="Internal").ap()

    def _alloc(self):
        k = self.k
        self.hx = k.sb([128, 16384], F32, "hx")
        self.xn = self.hx[:, :].rearrange("p (c t) -> p c t", t=T)
        self.hTv = self.hx[:, 0:8192].bitcast(BF16).rearrange("p (c t) -> p c t", t=T)
        self.hT = k.sub("hT")
        self.xn_t = [k.sub("xn") for _ in range(KC)]
        self.big = k.sb([128, 16384], F32, "big")
        self.bigv = self.big[:, :].bitcast(BF16).rearrange("p (c t) -> p c t", t=T)
        self.wslots = Ring(k, 2, [128, 8192], BF16, "wsl")
        self.xs = Ring(k, 2, [128, T], F32, "xs")
        self.xo = Ring(k, 2, [128, T], F32, "xo")
        self.tb = Ring(k, 2, [128, T], F32, "tb")
        self.sq = Ring(k, 2, [128, T], F32, "sq")
        self.cst = Ring(k, 2, [128, T], BF16, "cst")
        self.ones = k.sb([128, 128], F32, "ones")
        self.ident = k.sb([128, 128], F32, "ident")
        self.mean = k.sb([128, T], F32, "mean")
        self.rstd = k.sb([128, T], F32, "rstd")
        self.vecs = {}
        self.P = [k.ps([128, 512], F32, f"P{i}") for i in range(8)]
        self.ident_d = self.din("ident_in", [128, 128])
        k.op(k.dve, lambda e: e.memset(self.ones[:, :], 1.0), writes=[self.ones])
        k.dma(k.sp, self.ident[:, :], self.ident_d[:, :], self.ident, writes=[self.ident])
        self.eps_ln = k.sb([128, 1], F32, "eps_ln")
        k.op(k.dve, lambda e: e.memset(self.eps_ln[:, :], float(LN_EPS)), writes=[self.eps_ln])

    def vec(self, name):
        if name in self.vecs:
            return self.vecs[name]
        k = self.k
        d = self.din(name, [128, KC])
        t = k.sb([128, KC], F32, "v_" + name)
        k.dma(k.sp, t[:, :], d[:, :], t, writes=[t])
        self.vecs[name] = t
        return t

    def derived(self, name, src, mul, add):
        if name in self.vecs:
            return self.vecs[name]
        k = self.k
        t = k.sb([128, KC], F32, "v_" + name)
        k.op(k.dve, lambda e: e.tensor_scalar(out=t[:, :], in0=src[:, :], scalar1=float(mul), scalar2=float(add),
                                              op0=ALU.mult, op1=ALU.add), reads=[src], writes=[t])
        self.vecs[name] = t
        return t

    def stage_mod(self, x_d, tt, scp, sh):
        k = self.k
        hT = self.hTv
        for kc in range(KC):
            xs = self.xs.next()
            k.dma(k.sp, xs[:, :], x_d[kc * 128:(kc + 1) * 128, tt * T:(tt + 1) * T], xs, writes=[xs])
            k.op(k.act, lambda e, xs=xs, kc=kc: e.activation(out=hT[:, kc, :], in_=xs[:, :], func=AF.Identity,
                                                              bias=sh[:, kc:kc + 1], scale=scp[:, kc:kc + 1]),
                 reads=[xs, scp, sh], writes=[self.hT] + self.xn_t[0:16])

    def linear_in(self, w_d, npairs, epilogue, bank0=0):
        k = self.k
        nc = self.nc
        hT = self.hTv
        for pi in range(npairs):
            ws = self.wslots.next()
            k.dma(k.pool, ws[:, :], w_d[pi, :, :], ws, writes=[ws])
            banks = [self.P[bank0 + 2 * (pi % 2)], self.P[bank0 + 2 * (pi % 2) + 1]]
            w3 = ws[:, :].rearrange("p (k c) -> p k c", c=256)
            nb = epilogue(pi, None)
            with nc.allow_low_precision("bf16 matmul"):
                for j in range(nb):
                    ps = banks[j]
                    for kc in range(KC):
                        k.op(k.pe, lambda e, ps=ps, kc=kc, j=j, w3=w3: e.matmul(
                            ps[:, :], lhsT=w3[:, kc, j * 128:(j + 1) * 128], rhs=hT[:, kc, :],
                            start=(kc == 0), stop=(kc == KC - 1)),
                            reads=[ws, self.hT], writes=[ps], accum=(kc > 0))
            epilogue(pi, banks)

    def linear_out(self, w_d, nkc, bps, opnd, xres_d, xout_d, tt, gco, lng, lnb, bank0=4):
        k = self.k
        nc = self.nc
        ones = self.ones
        psum_s, psum_q = self.P[6], self.P[7]
        pend = None
        for m in range(KC):
            if m % bps == 0:
                ws = self.wslots.next()
                k.dma(k.pool, ws[:, 0:bps * nkc * 128], w_d[m // bps, :, :], ws, writes=[ws])
                w4 = ws[:, 0:bps * nkc * 128].rearrange("p (b k c) -> p b k c", b=bps, c=128)
            ps = self.P[bank0 + (m % 2)]
            with nc.allow_low_precision("bf16 matmul"):
                for kc in range(nkc):
                    k.op(k.pe, lambda e, ps=ps, kc=kc, w4=w4, bi=m % bps: e.matmul(
                        ps[:, :], lhsT=w4[:, bi, kc, :], rhs=opnd(kc), start=(kc == 0), stop=(kc == nkc - 1)),
                        reads=[ws, self.big], writes=[ps], accum=(kc > 0))
            if pend is not None:
                self._stats_mm(*pend)
            xs = self.xs.next()
            k.dma(k.sp, xs[:, :], xres_d[m * 128:(m + 1) * 128, tt * T:(tt + 1) * T], xs, writes=[xs])
            tb = self.tb.next()
            k.op(k.act, lambda e, ps=ps, tb=tb, m=m: e.activation(out=tb[:, :], in_=ps[:, :], func=AF.Copy,
                                                                  scale=gco[:, m:m + 1]),
                 reads=[ps, gco], writes=[tb])
            xt = self.xn_t[m]
            k.op(k.dve, lambda e, xs=xs, tb=tb, m=m: e.scalar_tensor_tensor(
                out=self.xn[:, m, :], in0=xs[:, :], scalar=float(ALPHA), in1=tb[:, :], op0=ALU.mult, op1=ALU.add),
                reads=[xs, tb], writes=[xt] + ([self.hT] if m < 16 else []))
            sq = self.sq.next()
            k.op(k.act, lambda e, sq=sq, m=m: e.activation(out=sq[:, :], in_=self.xn[:, m, :], func=AF.Square),
                 reads=[xt], writes=[sq])
            pend = (m, xt, sq)
        self._stats_mm(*pend)
        mean, rstd = self.mean, self.rstd
        msq = self.tb.next()
        k.op(k.dve, lambda e: e.tensor_scalar(out=mean[:, :], in0=psum_s[:, :], scalar1=1.0 / D, scalar2=None,
                                              op0=ALU.mult), reads=[psum_s], writes=[mean])
        k.op(k.dve, lambda e: e.tensor_tensor(out=msq[:, :], in0=mean[:, :], in1=mean[:, :], op=ALU.mult),
             reads=[mean], writes=[msq])
        k.op(k.dve, lambda e: e.scalar_tensor_tensor(out=rstd[:, :], in0=psum_q[:, :], scalar=1.0 / D, in1=msq[:, :],
                                                    op0=ALU.mult, op1=ALU.subtract),
             reads=[psum_q, msq], writes=[rstd])
        self.rsqrt(rstd, rstd[:, :], LN_EPS)
        for m in range(KC):
            xt = self.xn_t[m]
            tb = self.tb.next()
            k.op(k.dve, lambda e, tb=tb, m=m: e.tensor_tensor(out=tb[:, :], in0=self.xn[:, m, :], in1=mean[:, :],
                                                              op=ALU.subtract), reads=[xt, mean], writes=[tb])
            k.op(k.pool, lambda e, tb=tb: e.tensor_tensor(out=tb[:, :], in0=tb[:, :], in1=rstd[:, :], op=ALU.mult),
                 reads=[tb, rstd], writes=[tb])
            xo = self.xo.next()
            k.op(k.act, lambda e, tb=tb, xo=xo, m=m: e.activation(out=xo[:, :], in_=tb[:, :], func=AF.Identity,
                                                                  bias=lnb[:, m:m + 1], scale=lng[:, m:m + 1]),
                 reads=[tb, lng, lnb], writes=[xo])
            k.dma(k.sp, xout_d[m * 128:(m + 1) * 128, tt * T:(tt + 1) * T], xo[:, :], xo, reads=[xo])

    def rsqrt(self, tl, ap, eps):
        k = self.k
        k.op(k.act, lambda e: e.activation(out=ap, in_=ap, func=AF.Sqrt, bias=self.eps_ln[:, 0:1], scale=1.0),
             reads=[tl, self.eps_ln], writes=[tl])
        k.op(k.dve, lambda e: e.reciprocal(out=ap, in_=ap), reads=[tl], writes=[tl])

    def _stats_mm(self, m, xt, sq):
        k = self.k
        psum_s, psum_q = self.P[6], self.P[7]
        k.op(k.pe, lambda e: e.matmul(psum_s[:, :], lhsT=self.ones[:, :], rhs=self.xn[:, m, :],
                                      start=(m == 0), stop=(m == KC - 1)),
             reads=[self.ones, xt], writes=[psum_s], accum=(m > 0))
        k.op(k.pe, lambda e: e.matmul(psum_q[:, :], lhsT=self.ones[:, :], rhs=sq[:, :],
                                      start=(m == 0), stop=(m == KC - 1)),
             reads=[self.ones, sq], writes=[psum_q], accum=(m > 0))

    def _emit(self):
        k = self.k
        self.x_in = self.din("x_in", [D, self.ntok])
        cur = self.x_in
        plans = []
        last_x = max([i for i, (kd, _) in enumerate(self.stages) if kd in ("ffn", "mixout")] + [-1])
        for si, (kind, tag) in enumerate(self.stages):
            last = si == len(self.stages) - 1
            p = {"kind": kind, "tag": tag, "x_src": cur}
            if kind == "ffn":
                p["w1"] = self.din(f"wfi_{tag}", [64, 128, KC * 256])
                p["w2"] = self.din(f"wfo_{tag}", [32, 128, 64 * 128])
                sc = self.vec(f"sc_{tag}")
                p["scp"] = self.derived(f"scp_{tag}", sc, 1.0, 1.0)
                p["sh"] = self.vec(f"sh_{tag}")
                p["gco"] = self.derived(f"gco_{tag}", self.vec(f"ga_{tag}"), 0.5, 0.0)
                p["lng"] = self.vec(f"lng_{tag}")
                p["lnb"] = self.vec(f"lnb_{tag}")
            elif kind == "mixout":
                p["mix"] = self.din(f"mix_{tag}", [D, self.ntok], BF16)
                p["w2"] = self.din(f"wo_{tag}", [16, 128, 2 * KC * 128])
                p["gco"] = self.vec(f"ga_{tag}")
                p["lng"] = self.vec(f"lng_{tag}")
                p["lnb"] = self.vec(f"lnb_{tag}")
            elif kind == "mixin":
                p["w1"] = self.din(f"win_{tag}", [51, 128, KC * 256])
                sc = self.vec(f"sc_{tag}")
                p["scp"] = self.derived(f"scp_{tag}", sc, 1.0, 1.0)
                p["sh"] = self.vec(f"sh_{tag}")
                p["EA"] = self.dout(f"EA_{tag}", [48 * 128, self.ntok])
                p["EC"] = self.dout(f"EC_{tag}", [36 * 128, self.ntok], BF16)
                p["ES"] = self.dout(f"ES_{tag}", [36, self.ntok])
                p["EB"] = self.dout(f"EB_{tag}", [D_B, self.ntok], BF16)
                p["wsT"] = self.din(f"wsT_{tag}", [128, H_B, 128])
                p["bsr"] = self.din(f"bsr_{tag}", [128, H_B, 128])
                p["sg"] = self.din(f"sgg_{tag}", [128, D_B])
                p["sb"] = self.din(f"sgb_{tag}", [128, D_B])
            if kind in ("ffn", "mixout"):
                if si == last_x:
                    p["x_dst"] = self.dout("x_out", [D, self.ntok])
                else:
                    p["x_dst"] = self.dint(f"xmid_{si}", [D, self.ntok])
                cur = p["x_dst"]
            plans.append(p)
        self.plans = plans
        if any(p["kind"] == "mixin" for p in plans):
            self._alloc_b()
        for tt in range(self.ntt):
            for p in plans:
                getattr(self, "_st_" + p["kind"])(p, tt)
        k.wait_tiles(k.sp, self.xo.tiles + self.cst.tiles + self.tb.tiles + [self.big])

    def _st_ffn(self, p, tt):
        k = self.k
        big = self.bigv
        self.stage_mod(p["x_src"], tt, p["scp"], p["sh"])

        def epi(pi, banks):
            if banks is None:
                return 2
            g = self.tb.next()
            k.op(k.act, lambda e: e.activation(out=g[:, :], in_=banks[0][:, :], func=AF.Silu),
                 reads=[banks[0]], writes=[g])
            k.op(k.dve, lambda e: e.tensor_tensor(out=big[:, pi, :], in0=g[:, :], in1=banks[1][:, :], op=ALU.mult),
                 reads=[g, banks[1]], writes=[self.big])
            return 2

        self.linear_in(p["w1"], 64, epi)
        self.linear_out(p["w2"], 64, 1, lambda kc: big[:, kc, :], p["x_src"], p["x_dst"], tt,
                        p["gco"], p["lng"], p["lnb"])

    def _st_mixout(self, p, tt):
        k = self.k
        big = self.bigv
        mix = p["mix"]
        for q in range(4):
            k.dma(k.sp, big[:, q * 8:(q + 1) * 8, :],
                  mix[q * 1024:(q + 1) * 1024, tt * T:(tt + 1) * T].rearrange("(c p) t -> p c t", p=128),
                  self.big, writes=[self.big], join=(q > 0))
        self.linear_out(p["w2"], KC, 2, lambda kc: big[:, kc, :], p["x_src"], p["x_dst"], tt,
                        p["gco"], p["lng"], p["lnb"])

    def _alloc_b(self):
        k = self.k
        p = [q for q in self.plans if q["kind"] == "mixin"]
        b = self.big
        self.uT = b[:, 0:4096].rearrange("p (g t) -> p g t", t=T)
        self.vT = b[:, 4096:8192].rearrange("p (g t) -> p g t", t=T)
        self.vtok = b[:, 8192:9216]
        self.vn = b[:, 9216:9728].bitcast(BF16)
        self.obT = [b[:, 9728 + i * 512:9728 + (i + 1) * 512].bitcast(BF16).rearrange("p (g t) -> p g t", t=128)
                    for i in range(2)]
        self.obi = 0
        self.bst = k.sb([128, 2, 6], F32, "bst")
        self.bag = k.sb([128, 2], F32, "bag")
        self.brs = k.sb([128, 2], F32, "brs")
        self.mask_su = k.sb([128, 128], F32, "mask_su")
        self.mask_d = self.din("mask_su_in", [128, 128])
        k.dma(k.sp, self.mask_su[:, :], self.mask_d[:, :], self.mask_su, writes=[self.mask_su])
        self.bc = {}
        for q in p:
            tag = q["tag"]
            wsT = k.sb([128, H_B, 128], F32, f"wsT_{tag}")
            k.dma(k.sp, wsT[:, :, :], q["wsT"][:, :, :], wsT, writes=[wsT])
            wc = k.sb([128, H_B, 128], BF16, f"wc_{tag}")
            for g in range(H_B):
                k.op(k.dve, lambda e, g=g, wc=wc, wsT=wsT: e.tensor_tensor(out=wc[:, g, :], in0=wsT[:, g, :],
                                                                         in1=self.mask_su[:, :], op=ALU.mult),
                     reads=[wsT, self.mask_su], writes=[wc])
            bsr = k.sb([128, H_B, 128], F32, f"bsr_{tag}")
            k.dma(k.sp, bsr[:, :, :], q["bsr"][:, :, :], bsr, writes=[bsr])
            sg = k.sb([128, D_B], F32, f"sgg_{tag}")
            k.dma(k.sp, sg[:, :], q["sg"][:, :], sg, writes=[sg])
            sb_ = k.sb([128, D_B], F32, f"sgb_{tag}")
            k.dma(k.sp, sb_[:, :], q["sb"][:, :], sb_, writes=[sb_])
            self.bc[tag] = (wc, bsr, sg, sb_)

    def _st_mixin(self, p, tt):
        k = self.k
        nc = self.nc
        self.stage_mod(p["x_src"], tt, p["scp"], p["sh"])
        plan = mixin_block_plan()
        tsl = slice(tt * T, (tt + 1) * T)

        def epi(pi, banks):
            kinds = [plan[2 * pi], plan[2 * pi + 1]]
            nb = 1 if kinds[1][0] == "P" else 2
            if banks is None:
                return nb
            for j in range(nb):
                kind, idx, _ = kinds[j]
                ps = banks[j]
                if kind == "A":
                    xo = self.xo.next()
                    k.op(k.act, lambda e, xo=xo, ps=ps: e.copy(out=xo[:, :], in_=ps[:, :]), reads=[ps], writes=[xo])
                    k.dma(k.sp, p["EA"][idx * 128:(idx + 1) * 128, tsl], xo[:, :], xo, reads=[xo])
                elif kind == "C":
                    c = self.cst.next()
                    k.op(k.act, lambda e, c=c, ps=ps: e.copy(out=c[:, :], in_=ps[:, :]), reads=[ps], writes=[c])
                    k.dma(k.sp, p["EC"][idx * 128:(idx + 1) * 128, tsl], c[:, :], c, reads=[c])
                elif kind == "S":
                    xo = self.xo.next()
                    k.op(k.act, lambda e, xo=xo, ps=ps: e.copy(out=xo[:, :], in_=ps[:, :]), reads=[ps], writes=[xo])
                    k.dma(k.sp, p["ES"][:, tsl], xo[0:36, :], xo, reads=[xo])
                elif kind == "U":
                    k.op(k.act, lambda e, ps=ps, idx=idx: e.activation(out=self.uT[:, idx, :], in_=ps[:, :],
                                                                       func=AF.Gelu), reads=[ps], writes=[self.big])
                elif kind == "V":
                    k.op(k.act, lambda e, ps=ps, idx=idx: e.activation(out=self.vT[:, idx, :], in_=ps[:, :],
                                                                       func=AF.Gelu), reads=[ps], writes=[self.big])
            return nb

        self.linear_in(p["w1"], 51, epi)
        if getattr(self, "debug", False):
            if not hasattr(self, "dbg_u"):
                self.dbg_u = self.dout("dbg_u", [128, H_B * T])
                self.dbg_v = self.dout("dbg_v", [128, H_B * T])
                self.dbg_vn = self.dout("dbg_vn", [4, 128, D_B], BF16)
                self.dbg_vt = self.dout("dbg_vt", [4, 128, D_B])
            k.dma(k.sp, self.dbg_u[:, :], self.big[:, 0:4096], self.big, reads=[self.big])
            k.dma(k.sp, self.dbg_v[:, :], self.big[:, 4096:8192], self.big, reads=[self.big])
        wc, bsr, sg, sb_ = self.bc[p["tag"]]
        vtok, vn = self.vtok, self.vn
        for j in range(T // 128):
            js = slice(j * 128, (j + 1) * 128)
            pt = [self.P[4], self.P[5]]
            for g in range(H_B):
                k.op(k.pe, lambda e, g=g, js=js: e.transpose(pt[g // 4][:, (g % 4) * 128:(g % 4 + 1) * 128],
                                                             self.vT[:, g, js], self.ident[:, :]),
                     reads=[self.big, self.ident], writes=[pt[g // 4]], accum=(g % 4 > 0))
            for h in range(2):
                k.op(k.act, lambda e, h=h: e.copy(out=vtok[:, h * 512:(h + 1) * 512], in_=pt[h][:, :]),
                     reads=[pt[h]], writes=[self.big])
            if getattr(self, "debug", False):
                k.dma(k.sp, self.dbg_vt[j, :, :], vtok, self.big, reads=[self.big])
            bst, bag, brs = self.bst, self.bag, self.brs
            for h in range(2):
                sqh = self.tb.next()
                k.op(k.act, lambda e, h=h, sqh=sqh: e.activation(out=sqh[:, :], in_=vtok[:, h * 512:(h + 1) * 512],
                                                                func=AF.Square), reads=[self.big], writes=[sqh])
                k.op(k.dve, lambda e, h=h, sqh=sqh: e.reduce_sum(out=bst[:, 0, 2 + h:3 + h], in_=sqh[:, :],
                                                                axis=mybir.AxisListType.X), reads=[sqh], writes=[bst])
                k.op(k.dve, lambda e, h=h: e.reduce_sum(out=bst[:, 0, h:h + 1], in_=vtok[:, h * 512:(h + 1) * 512],
                                                        axis=mybir.AxisListType.X), reads=[self.big], writes=[bst])
            k.op(k.dve, lambda e: e.tensor_scalar(out=bag[:, 0:1], in0=bst[:, 0, 0:1], scalar1=bst[:, 0, 1:2],
                                                  scalar2=1.0 / D_B, op0=ALU.add, op1=ALU.mult),
                 reads=[bst], writes=[bag])
            k.op(k.dve, lambda e: e.tensor_scalar(out=bag[:, 1:2], in0=bst[:, 0, 2:3], scalar1=bst[:, 0, 3:4],
                                                  scalar2=1.0 / D_B, op0=ALU.add, op1=ALU.mult),
                 reads=[bst], writes=[bag])
            k.op(k.dve, lambda e: e.tensor_tensor(out=brs[:, 1:2], in0=bag[:, 0:1], in1=bag[:, 0:1], op=ALU.mult),
                 reads=[bag], writes=[brs])
            k.op(k.dve, lambda e: e.tensor_tensor(out=bag[:, 1:2], in0=bag[:, 1:2], in1=brs[:, 1:2], op=ALU.subtract),
                 reads=[bag, brs], writes=[bag])
            k.op(k.act, lambda e: e.activation(out=brs[:, 0:1], in_=bag[:, 1:2], func=AF.Sqrt, bias=self.eps_ln[:, 0:1],
                                               scale=1.0), reads=[bag, self.eps_ln], writes=[brs])
            k.op(k.dve, lambda e: e.reciprocal(out=brs[:, 0:1], in_=brs[:, 0:1]), reads=[brs], writes=[brs])
            k.op(k.dve, lambda e: e.scalar_tensor_tensor(out=brs[:, 1:2], in0=bag[:, 0:1], scalar=-1.0, in1=brs[:, 0:1],
                                                        op0=ALU.mult, op1=ALU.mult), reads=[bag, brs], writes=[brs])
            k.op(k.act, lambda e: e.activation(out=vtok[:, :], in_=vtok[:, :], func=AF.Identity,
                                               bias=brs[:, 1:2], scale=brs[:, 0:1]), reads=[self.big, brs], writes=[self.big])
            k.op(k.dve, lambda e: e.tensor_tensor(out=vtok[:, :], in0=vtok[:, :], in1=sg[:, :], op=ALU.mult),
                 reads=[self.big, sg], writes=[self.big])
            k.op(k.dve, lambda e: e.tensor_tensor(out=vn[:, :], in0=vtok[:, :], in1=sb_[:, :], op=ALU.add),
                 reads=[self.big, sb_], writes=[self.big])
            if getattr(self, "debug", False):
                k.dma(k.sp, self.dbg_vn[j, :, :], vn, self.big, reads=[self.big])
            ob = self.obT[self.obi % 2]
            self.obi += 1
            pm = [self.P[6], self.P[7]]
            with nc.allow_low_precision("bf16 matmul"):
                for g in range(H_B):
                    k.op(k.pe, lambda e, g=g: e.matmul(pm[g // 4][:, (g % 4) * 128:(g % 4 + 1) * 128],
                                                       lhsT=vn[:, g * 128:(g + 1) * 128], rhs=wc[:, g, :],
                                                       start=True, stop=True),
                         reads=[self.big, wc], writes=[pm[g // 4]], accum=(g % 4 > 0))
            for h in range(2):
                tb = self.tb.next()
                if getattr(self, "debug", False):
                    if not hasattr(self, "dbg_pm"):
                        self.dbg_pm = self.dout("dbg_pm", [4, 2, 128, 512])
                        self.dbg_wc = self.dout("dbg_wc", [128, H_B * 128], BF16)
                        k.dma(k.sp, self.dbg_wc[:, :], wc[:, :, :].rearrange("p g t -> p (g t)"), wc, reads=[wc])
                    xo = self.xo.next()
                    k.op(k.act, lambda e, h=h, xo=xo: e.copy(out=xo[:, :], in_=pm[h][:, :]), reads=[pm[h]], writes=[xo])
                    k.dma(k.sp, self.dbg_pm[j, h, :, :], xo[:, :], xo, reads=[xo])
                k.op(k.dve, lambda e, h=h, tb=tb: e.tensor_tensor(
                    out=tb[:, :], in0=pm[h][:, :], in1=bsr[:, h * 4:(h + 1) * 4, :].rearrange("p g t -> p (g t)"),
                    op=ALU.add), reads=[pm[h], bsr], writes=[tb])
                k.op(k.dve, lambda e, h=h, tb=tb, ob=ob, js=js: e.tensor_tensor(
                    out=ob[:, h * 4:(h + 1) * 4, :], in0=tb[:, :].rearrange("p (g t) -> p g t", g=4),
                    in1=self.uT[:, h * 4:(h + 1) * 4, js], op=ALU.mult), reads=[tb, self.big], writes=[self.big])
            k.dma(k.sp, p["EB"][:, tt * T + j * 128: tt * T + (j + 1) * 128].rearrange("(g c) t -> c g t", c=128),
                  ob[:, :, :], self.big, reads=[self.big])


def prep_ffn_in(W):
    return np.ascontiguousarray(W.reshape(32, 128, 2, 64, 128).transpose(3, 1, 0, 2, 4)).reshape(64, 128, 8192)


def prep_ffn_out(W):
    return np.ascontiguousarray(W.reshape(64, 128, 32, 128).transpose(2, 1, 0, 3)).reshape(32, 128, 8192)


def prep_wo(W):
    return np.ascontiguousarray(W.reshape(32, 128, 16, 2, 128).transpose(2, 1, 3, 0, 4)).reshape(16, 128, 8192)


def prep_win(W):
    plan = mixin_block_plan()
    cols = np.stack([np.concatenate([plan[2 * i][2], plan[2 * i + 1][2]]) for i in range(51)])
    return blocks_in(W, cols)


def b_consts(tag, w_s, b_s, g, b):
    return {
        f"wsT_{tag}": np.ascontiguousarray(w_s.transpose(2, 0, 1)).astype(np.float32),
        f"bsr_{tag}": np.ascontiguousarray(np.broadcast_to(b_s[None], (128, H_B, 128))).astype(np.float32),
        f"sgg_{tag}": np.ascontiguousarray(np.broadcast_to(g[None], (128, D_B))).astype(np.float32),
        f"sgb_{tag}": np.ascontiguousarray(np.broadcast_to(b[None], (128, D_B))).astype(np.float32),
        "mask_su_in": np.triu(np.ones((128, 128), np.float32)),
    }


class HeadProg:
    def __init__(self, na=3, ncu=3, seq=S, debug=False):
        self.debug = debug
        self.na, self.ncu, self.seq = na, ncu, seq
        nc = bass.Bass("TRN2", target_bir_lowering=False)
        self.nc = nc
        self.in_names, self.out_names = [], []
        with ExitStack() as st:
            self.k = KB(nc, st)
            self._common()
            if na:
                self.aoff = 0
                self._alloc_a()
                for u in range(na):
                    self._unit_a(u)
                self.k.barrier()
            if ncu:
                self.aoff = 0
                self._alloc_c()
                for u in range(ncu):
                    self._unit_c(u)
            self.k.wait_tiles(self.k.sp, self.out_tiles)
            self.k.finish()

    def din(self, name, shape, dt=F32):
        self.in_names.append(name)
        return self.nc.dram_tensor(name, list(shape), dt, kind="ExternalInput").ap()

    def dout(self, name, shape, dt=F32):
        self.out_names.append(name)
        return self.nc.dram_tensor(name, list(shape), dt, kind="ExternalOutput").ap()

    def carve(self, shape, dt, name):
        n = 1
        for d_ in shape[1:]:
            n *= d_
        nf = n if dt == F32 else (n + 1) // 2
        ap = self.arena[:, self.aoff:self.aoff + nf]
        self.aoff += nf
        assert self.aoff <= self.arena_n, (name, self.aoff)
        if dt != F32:
            ap = ap.bitcast(dt)
        if len(shape) == 3:
            ap = ap.rearrange("p (a b) -> p a b", b=shape[2])
        self.k.nt += 1
        return Tl(ap, f"{name}_{self.k.nt}")

    def dbg(self, name, tl, ap, shape, dt=F32):
        if not getattr(self, "debug", False):
            return
        if name in self.in_names or name in self.out_names:
            return
        d = self.dout(name, shape, dt)
        self.k.dma(self.k.sp, d, ap, tl, reads=[tl])
        self.out_tiles.append(tl)

    def _common(self):
        k = self.k
        self.out_tiles = []
        self.arena_n = 47 * 1024
        self.arena = self.k.sb([128, self.arena_n], F32, "arena")
        self.P = [k.ps([128, 512], F32, f"P{i}") for i in range(8)]
        self.ident = k.sb([128, 128], F32, "ident")
        self.identb = k.sb([128, 128], BF16, "identb")
        self.ones = k.sb([128, 128], F32, "ones")
        self.onesb = k.sb([128, 128], BF16, "onesb")
        self.mask_su = k.sb([128, 128], F32, "mask_su")
        self.mask_sub = k.sb([128, 128], BF16, "mask_sub")
        self.c_one = k.sb([128, 1], F32, "c_one")
        self.c_eps = k.sb([128, 1], F32, "c_eps")
        idd = self.din("ident_in", [128, 128])
        md = self.din("mask_su_in", [128, 128])
        k.dma(k.sp, self.ident[:, :], idd[:, :], self.ident, writes=[self.ident])
        k.dma(k.sp, self.mask_su[:, :], md[:, :], self.mask_su, writes=[self.mask_su])
        k.op(k.dve, lambda e: e.tensor_copy(out=self.identb[:, :], in_=self.ident[:, :]), reads=[self.ident],
             writes=[self.identb])
        k.op(k.dve, lambda e: e.tensor_copy(out=self.mask_sub[:, :], in_=self.mask_su[:, :]), reads=[self.mask_su],
             writes=[self.mask_sub])
        k.op(k.dve, lambda e: e.memset(self.ones[:, :], 1.0), writes=[self.ones])
        k.op(k.dve, lambda e: e.memset(self.onesb[:, :], 1.0), writes=[self.onesb])
        k.op(k.dve, lambda e: e.memset(self.c_one[:, :], 1.0), writes=[self.c_one])
        k.op(k.dve, lambda e: e.memset(self.c_eps[:, :], float(RMS_EPS)), writes=[self.c_eps])

    def _alloc_c(self):
        k = self.k
        sq = self.seq
        self.c_q = self.carve([128, sq], BF16, "c_q")
        self.c_k = self.carve([128, sq], BF16, "c_k")
        self.c_vt = self.carve([128, sq // 128, 128], BF16, "c_vt")
        self.c_f = self.carve([128, sq], F32, "c_f")
        self.c_nc = self.carve([128, sq], F32, "c_nc")
        self.c_ncc = self.carve([128, sq // 128], F32, "c_ncc")
        self.c_par = self.carve([128, 2], F32, "c_par")
        self.c_tmp = Ring(k, 4, [128, 512], F32, "c_tmp", self.carve)
        self.c_p = Ring(k, 5, [128, 512], BF16, "c_p", self.carve)
        self.c_on = self.carve([128, 512], F32, "c_on")
        self.c_rl = self.carve([128, 512], F32, "c_rl")
        self.c_sq = self.carve([128, 512], F32, "c_sq")
        self.c_ob = Ring(k, 2, [128, 512], BF16, "c_ob", self.carve)
        self.out_tiles += self.c_ob.tiles

    def _unit_c(self, u):
        k = self.k
        nc = self.nc
        sq = self.seq
        nb = sq // 128
        nI = sq // 512
        qk_d = self.din(f"c_qk_{u}", [2, 128, sq], BF16)
        v_d = self.din(f"c_v_{u}", [sq, 128], BF16)
        f_d = self.din(f"c_f_{u}", [1, sq])
        par_d = self.din(f"c_par_{u}", [128, 2])
        o_d = self.dout(f"c_o_{u}", [128, sq], BF16)
        cq, ck, cvt, cf, cnc, cncc, cpar = (self.c_q, self.c_k, self.c_vt, self.c_f, self.c_nc,
                                            self.c_ncc, self.c_par)
        k.dma(k.sp, cq[:, :], qk_d[0, :, :], cq, writes=[cq])
        k.dma(k.sp, ck[:, :], qk_d[1, :, :], ck, writes=[ck])
        k.dma(k.sp, cvt[:, :, :], v_d[:, :].rearrange("(b p) e -> p b e", p=128), cvt, writes=[cvt])
        k.dma(k.sp, cf[:, :], f_d[0:1, :].partition_broadcast(128), cf, writes=[cf])
        k.dma(k.sp, cpar[:, :], par_d[:, :], cpar, writes=[cpar])
        k.op(k.dve, lambda e: e.tensor_scalar(out=cpar[:, 0:1], in0=cpar[:, 0:1], scalar1=-1.0, scalar2=None,
                                              op0=ALU.mult), reads=[cpar], writes=[cpar])
        k.op(k.act, lambda e: e.activation(out=cf[:, :], in_=cf[:, :], func=AF.Exp, bias=cpar[:, 0:1], scale=-1.0),
             reads=[cf, cpar], writes=[cf])
        k.op(k.act, lambda e: e.activation(out=cf[:, :], in_=cf[:, :], func=AF.Ln, bias=self.c_one[:, 0:1], scale=1.0),
             reads=[cf, self.c_one], writes=[cf])
        k.op(k.dve, lambda e: e.tensor_tensor_scan(out=cnc[:, :], data0=self.c_one[:, 0:1].to_broadcast([128, sq]),
                                                  data1=cf[:, :], initial=0.0, op0=ALU.mult, op1=ALU.add),
             reads=[cf, self.c_one], writes=[cnc])
        ptr = self.P[6]
        for g4 in range(nb // 4):
            for i in range(4):
                j = g4 * 4 + i
                k.op(k.pe, lambda e, i=i, j=j: e.transpose(ptr[:, i * 128:(i + 1) * 128], cnc[:, j * 128:(j + 1) * 128],
                                                           self.ident[:, :]),
                     reads=[cnc, self.ident], writes=[ptr], accum=(i > 0))
            k.op(k.dve, lambda e, g4=g4: e.tensor_copy(
                out=cncc[:, g4 * 4:(g4 + 1) * 4], in_=ptr[:, :].rearrange("p (a b) -> p a b", b=128)[:, :, 0]),
                reads=[ptr], writes=[cncc])
        scale = float(HD ** -0.5)
        oacc, lacc, pss = self.P[4], self.P[5], self.P[7]
        scb = [self.P[0], self.P[1], self.P[2], self.P[3]]
        for I in range(nI):
            jobs = list(range(4 * I + 4))
            pend = []

            def qk(j, I=I):
                t0 = max(512 * I, 128 * j)
                off = t0 - 512 * I
                ps = scb[j % 4]
                with nc.allow_low_precision("bf16 matmul"):
                    k.op(k.pe, lambda e: e.matmul(ps[:, off:512], lhsT=ck[:, j * 128:(j + 1) * 128],
                                                  rhs=cq[:, t0:512 * (I + 1)], start=True, stop=True),
                         reads=[ck, cq], writes=[ps])
                tmp = self.c_tmp.next()
                k.op(k.dve, lambda e: e.scalar_tensor_tensor(out=tmp[:, off:512], in0=ps[:, off:512], scalar=scale,
                                                            in1=cnc[:, t0:512 * (I + 1)], op0=ALU.mult,
                                                            op1=ALU.subtract), reads=[ps, cnc], writes=[tmp])
                pt = self.c_p.next()
                k.op(k.act, lambda e: e.activation(out=pt[:, off:512], in_=tmp[:, off:512], func=AF.Exp,
                                                   bias=cncc[:, j:j + 1], scale=1.0), reads=[tmp, cncc], writes=[pt])
                if 128 * j >= 512 * I:
                    k.op(k.pool, lambda e: e.tensor_tensor(out=pt[:, off:off + 128], in0=pt[:, off:off + 128],
                                                           in1=self.mask_sub[:, :], op=ALU.mult),
                         reads=[pt, self.mask_sub], writes=[pt])
                return (j, off, pt)

            def pv(j, off, pt, I=I):
                last = 4 * I + 3
                with nc.allow_low_precision("bf16 matmul"):
                    k.op(k.pe, lambda e: e.matmul(oacc[:, off:512], lhsT=cvt[:, j, :], rhs=pt[:, off:512],
                                                  start=(j == 0), stop=(j == last)),
                         reads=[cvt, pt], writes=[oacc], accum=(j > 0))
                    k.op(k.pe, lambda e: e.matmul(lacc[:, off:512], lhsT=self.onesb[:, :], rhs=pt[:, off:512],
                                                  start=(j == 0), stop=(j == last)),
                         reads=[self.onesb, pt], writes=[lacc], accum=(j > 0))

            for j in jobs:
                pend.append(qk(j))
                if len(pend) > 3:
                    pv(*pend.pop(0))
            while pend:
                pv(*pend.pop(0))
            con, crl, csq = self.c_on, self.c_rl, self.c_sq
            k.op(k.dve, lambda e: e.reciprocal(out=crl[:, :], in_=lacc[:, :]), reads=[lacc], writes=[crl])
            k.op(k.dve, lambda e: e.tensor_tensor(out=con[:, :], in0=oacc[:, :], in1=crl[:, :], op=ALU.mult),
                 reads=[oacc, crl], writes=[con])
            k.op(k.act, lambda e: e.activation(out=csq[:, :], in_=con[:, :], func=AF.Square), reads=[con], writes=[csq])
            k.op(k.pe, lambda e: e.matmul(pss[:, :], lhsT=self.ones[:, :], rhs=csq[:, :], start=True, stop=True),
                 reads=[self.ones, csq], writes=[pss])
            k.op(k.act, lambda e: e.activation(out=crl[:, :], in_=pss[:, :], func=AF.Sqrt, bias=self.c_eps[:, 0:1],
                                               scale=1.0 / HD), reads=[pss, self.c_eps], writes=[crl])
            k.op(k.dve, lambda e: e.reciprocal(out=crl[:, :], in_=crl[:, :]), reads=[crl], writes=[crl])
            k.op(k.dve, lambda e: e.tensor_tensor(out=con[:, :], in0=con[:, :], in1=crl[:, :], op=ALU.mult),
                 reads=[con, crl], writes=[con])
            ob = self.c_ob.next()
            k.op(k.act, lambda e, ob=ob: e.activation(out=ob[:, :], in_=con[:, :], func=AF.Copy, scale=cpar[:, 1:2]),
                 reads=[con, cpar], writes=[ob])
            k.dma(k.sp, o_d[:, 512 * I:512 * (I + 1)], ob[:, :], ob, reads=[ob])

    def _alloc_a(self):
        k = self.k
        cv = self.carve
        SG = self.SG = min(2048, self.seq)
        NB = self.NB = SG // 128
        self.a_xin = Ring(k, 2, [128, SG + 4], F32, "a_xin", cv)
        self.a_qs = cv([128, SG], F32, "a_qs")
        self.a_ks = cv([128, SG], F32, "a_ks")
        self.a_vs = cv([128, SG], F32, "a_vs")
        self.a_sq = cv([128, SG], F32, "a_sq")
        self.a_rn = cv([128, SG], F32, "a_rn")
        self.a_br = cv([128, SG], F32, "a_br")
        self.a_gr = cv([128, SG], F32, "a_gr")
        self.a_Gr = cv([128, SG], F32, "a_Gr")
        self.a_m64 = cv([128, SG], F32, "a_m64")
        self.a_qg = cv([128, SG], BF16, "a_qg")
        self.a_wT = cv([128, SG], BF16, "a_wT")
        self.a_u = cv([128, NB, 128], F32, "a_u")
        self.a_qkT = cv([128, NB, 128], BF16, "a_qkT")
        self.a_kd = cv([128, NB, 128], BF16, "a_kd")
        self.a_o = cv([128, NB, 128], F32, "a_o")
        self.a_z = cv([128, NB, 128], F32, "a_z")
        self.a_ob = Ring(k, 2, [128, NB, 128], BF16, "a_ob", cv)
        self.out_tiles += self.a_ob.tiles
        self.a_col = cv([128, 2, NB], F32, "a_col")
        self.a_cg = cv([128, NB], F32, "a_cg")
        self.a_cG = cv([128, NB], F32, "a_cG")
        self.a_cGl = cv([128, NB], F32, "a_cGl")
        self.a_cbg = cv([128, NB], F32, "a_cbg")
        self.a_cdl = cv([128, NB], F32, "a_cdl")
        self.a_ss = cv([128, NB], F32, "a_ss")
        self.a_par = cv([128, 16], F32, "a_par")
        self.a_nea = cv([128, 1], F32, "a_nea")
        self.a_na = cv([128, 128], F32, "a_na")
        sm = lambda n: cv([128, 128], F32, n)
        self.a_sets = []
        for q in range(2):
            self.a_sets.append(dict(dm=sm(f"a_dm{q}"), E=sm(f"a_E{q}"), U0=sm(f"a_U0{q}"),
                                    X=[sm(f"a_X0{q}"), sm(f"a_X1{q}")], XT=[sm(f"a_XT0{q}"), sm(f"a_XT1{q}")],
                                    R=sm(f"a_R{q}"), kbg=sm(f"a_kbg{q}"), vb=sm(f"a_vb{q}"),
                                    P=[self.P[4 * q + i] for i in range(4)]))
        self.a_S = sm("a_S")
        self.a_Sb = cv([128, 128], BF16, "a_Sb")
        self.a_vn = Ring(k, 2, [128, 128], BF16, "a_vn", cv)
        self.maskU = sm("maskU")
        self.maskS = sm("maskS")
        self.triC = sm("triC")
        self.sameC = sm("sameC")
        mu_d = self.din("maskU_in", [128, 128])
        ms_d = self.din("maskS_in", [128, 128])
        sc_d = self.din("sameC_in", [128, 128])
        k.dma(k.sp, self.maskU[:, :], mu_d[:, :], self.maskU, writes=[self.maskU])
        k.dma(k.sp, self.maskS[:, :], ms_d[:, :], self.maskS, writes=[self.maskS])
        k.dma(k.sp, self.sameC[:, :], sc_d[:, :], self.sameC, writes=[self.sameC])
        m64 = self.a_m64
        k.op(k.dve, lambda e: e.memset(m64[:, :], 1.0), writes=[m64])
        k.op(k.dve, lambda e: e.memset(m64[:, :].rearrange("p (a b) -> p a b", b=64)[:, :, 0:1], 0.0), writes=[m64])

    def _unit_a(self, u):
        k = self.k
        nc = self.nc
        sq = self.seq
        SG, NB = self.SG, self.NB
        nseg = sq // SG
        qkv_d = self.din(f"a_qkv_{u}", [3, 128, sq])
        z_d = self.din(f"a_z_{u}", [sq, 128])
        bd_d = self.din(f"a_bd_{u}", [2, sq])
        bdc_d = self.din(f"a_bdc_{u}", [128, 2, sq // 128])
        par_d = self.din(f"a_par_{u}", [128, 16])
        na_d = self.din(f"a_na_{u}", [128, 128])
        o_d = self.dout(f"a_o_{u}", [sq, 128], BF16)
        par, nea, na = self.a_par, self.a_nea, self.a_na
        ident, ones = self.ident, self.ones
        k.dma(k.sp, par[:, :], par_d[:, :], par, writes=[par])
        k.dma(k.sp, na[:, :], na_d[:, :], na, writes=[na])
        k.op(k.act, lambda e: e.activation(out=nea[:, :], in_=par[:, 12:13], func=AF.Exp), reads=[par], writes=[nea])
        k.op(k.dve, lambda e: e.tensor_scalar(out=nea[:, :], in0=nea[:, :], scalar1=-1.0, scalar2=None, op0=ALU.mult),
             reads=[nea], writes=[nea])
        S, Sb = self.a_S, self.a_Sb
        k.op(k.dve, lambda e: e.memset(S[:, :], 0.0), writes=[S])
        k.op(k.dve, lambda e: e.memset(Sb[:, :], 0.0), writes=[Sb])
        P = self.P
        for sg in range(nseg):
            s0 = sg * SG
            outs = [self.a_qs, self.a_ks, self.a_vs]
            for i in range(3):
                xin = self.a_xin.next()
                if sg == 0:
                    k.op(k.dve, lambda e, xin=xin: e.memset(xin[:, 0:3], 0.0), writes=[xin])
                    k.dma(k.sp, xin[:, 3:3 + SG], qkv_d[i, :, 0:SG], xin, writes=[xin], join=False)
                else:
                    k.dma(k.sp, xin[:, 0:3 + SG], qkv_d[i, :, s0 - 3:s0 + SG], xin, writes=[xin])
                o = outs[i]
                eng = k.dve
                k.op(eng, lambda e, xin=xin, o=o, i=i: e.tensor_scalar(out=o[:, :], in0=xin[:, 3:3 + SG],
                                                                      scalar1=par[:, 4 * i + 3:4 * i + 4], scalar2=None,
                                                                      op0=ALU.mult), reads=[xin, par], writes=[o])
                for j in range(3):
                    k.op(eng, lambda e, xin=xin, o=o, i=i, j=j: e.scalar_tensor_tensor(
                        out=o[:, :], in0=xin[:, j:j + SG], scalar=par[:, 4 * i + j:4 * i + j + 1], in1=o[:, :],
                        op0=ALU.mult, op1=ALU.add), reads=[xin, par, o], writes=[o])
                k.op(k.act, lambda e, o=o: e.activation(out=o[:, :], in_=o[:, :], func=AF.Silu), reads=[o], writes=[o])
            qs, ks, vs, sqb, rn = self.a_qs, self.a_ks, self.a_vs, self.a_sq, self.a_rn
            for i, t_ in enumerate((qs, ks)):
                k.op(k.act, lambda e, t_=t_: e.activation(out=sqb[:, :], in_=t_[:, :], func=AF.Square),
                     reads=[t_], writes=[sqb])
                for c in range(SG // 512):
                    ps = P[c % 2]
                    k.op(k.pe, lambda e, c=c, ps=ps: e.matmul(ps[:, :], lhsT=ones[:, :], rhs=sqb[:, c * 512:(c + 1) * 512],
                                                              start=True, stop=True), reads=[ones, sqb], writes=[ps])
                    k.op(k.act, lambda e, c=c, ps=ps: e.activation(out=rn[:, c * 512:(c + 1) * 512], in_=ps[:, :],
                                                                   func=AF.Sqrt, bias=self.c_eps[:, 0:1], scale=1.0),
                         reads=[ps, self.c_eps], writes=[rn])
                k.op(k.dve, lambda e: e.reciprocal(out=rn[:, :], in_=rn[:, :]), reads=[rn], writes=[rn])
                if i == 0:
                    k.op(k.dve, lambda e: e.scalar_tensor_tensor(out=qs[:, :], in0=qs[:, :], scalar=float(HD ** -0.5),
                                                                in1=rn[:, :], op0=ALU.mult, op1=ALU.mult),
                         reads=[qs, rn], writes=[qs])
                else:
                    k.op(k.dve, lambda e: e.tensor_tensor(out=ks[:, :], in0=ks[:, :], in1=rn[:, :], op=ALU.mult),
                         reads=[ks, rn], writes=[ks])
            self.dbg("d_qs", qs, qs[:, :], [128, SG])
            self.dbg("d_ks", ks, ks[:, :], [128, SG])
            self.dbg("d_vs", vs, vs[:, :], [128, SG])
            br, gr, Gr = self.a_br, self.a_gr, self.a_Gr
            k.dma(k.sp, br[:, :], bd_d[0:1, s0:s0 + SG].partition_broadcast(128), br, writes=[br])
            k.dma(k.sp, gr[:, :], bd_d[1:2, s0:s0 + SG].partition_broadcast(128), gr, writes=[gr])
            k.op(k.act, lambda e: e.activation(out=br[:, :], in_=br[:, :], func=AF.Sigmoid), reads=[br], writes=[br])
            k.op(k.act, lambda e: e.activation(out=gr[:, :], in_=gr[:, :], func=AF.Exp, bias=par[:, 13:14], scale=1.0),
                 reads=[gr, par], writes=[gr])
            k.op(k.act, lambda e: e.activation(out=gr[:, :], in_=gr[:, :], func=AF.Ln, bias=self.c_one[:, 0:1], scale=1.0),
                 reads=[gr, self.c_one], writes=[gr])
            k.op(k.dve, lambda e: e.tensor_scalar(out=gr[:, :], in0=gr[:, :], scalar1=nea[:, 0:1], scalar2=None,
                                                  op0=ALU.mult), reads=[gr, nea], writes=[gr])
            k.op(k.dve, lambda e: e.tensor_tensor_scan(out=Gr[:, :], data0=self.a_m64[:, :], data1=gr[:, :], initial=0.0,
                                                      op0=ALU.mult, op1=ALU.add), reads=[gr, self.a_m64], writes=[Gr])
            k.op(k.act, lambda e: e.activation(out=gr[:, :], in_=Gr[:, :], func=AF.Exp), reads=[Gr], writes=[gr])
            qg = self.a_qg
            k.op(k.dve, lambda e: e.tensor_tensor(out=qg[:, :], in0=qs[:, :], in1=gr[:, :], op=ALU.mult),
                 reads=[qs, gr], writes=[qg])
            col, cg, cG, cGl, cbg, cdl = self.a_col, self.a_cg, self.a_cG, self.a_cGl, self.a_cbg, self.a_cdl
            k.dma(k.sp, col[:, :, :], bdc_d[:, :, sg * NB:(sg + 1) * NB], col, writes=[col])
            k.op(k.act, lambda e: e.activation(out=col[:, 0, :], in_=col[:, 0, :], func=AF.Sigmoid), reads=[col], writes=[col])
            k.op(k.act, lambda e: e.activation(out=cg[:, :], in_=col[:, 1, :], func=AF.Exp, bias=par[:, 13:14], scale=1.0),
                 reads=[col, par], writes=[cg])
            k.op(k.act, lambda e: e.activation(out=cg[:, :], in_=cg[:, :], func=AF.Ln, bias=self.c_one[:, 0:1], scale=1.0),
                 reads=[cg, self.c_one], writes=[cg])
            k.op(k.dve, lambda e: e.tensor_scalar(out=cg[:, :], in0=cg[:, :], scalar1=nea[:, 0:1], scalar2=None,
                                                  op0=ALU.mult), reads=[cg, nea], writes=[cg])
            k.op(k.pe, lambda e: e.matmul(P[0][:, 0:NB], lhsT=self.maskU[:, :], rhs=cg[:, :], start=True, stop=True),
                 reads=[self.maskU, cg], writes=[P[0]])
            k.op(k.pe, lambda e: e.matmul(P[1][:, 0:NB], lhsT=self.sameC[:, :], rhs=cg[:, :], start=True, stop=True),
                 reads=[self.sameC, cg], writes=[P[1]])
            k.op(k.dve, lambda e: e.tensor_copy(out=cG[:, :], in_=P[0][:, 0:NB]), reads=[P[0]], writes=[cG])
            k.op(k.dve, lambda e: e.tensor_tensor(out=cdl[:, :], in0=P[1][:, 0:NB], in1=cG[:, :], op=ALU.subtract),
                 reads=[P[1], cG], writes=[cdl])
            k.op(k.act, lambda e: e.activation(out=cdl[:, :], in_=cdl[:, :], func=AF.Exp), reads=[cdl], writes=[cdl])
            k.op(k.act, lambda e: e.activation(out=cbg[:, :], in_=cG[:, :], func=AF.Exp), reads=[cG], writes=[cbg])
            k.op(k.dve, lambda e: e.tensor_tensor(out=cbg[:, :], in0=cbg[:, :], in1=col[:, 0, :], op=ALU.mult),
                 reads=[cbg, col], writes=[cbg])
            self.dbg("d_br", br, br[:, :], [128, SG])
            self.dbg("d_Gr", Gr, Gr[:, :], [128, SG])
            self.dbg("d_eG", gr, gr[:, :], [128, SG])
            self.dbg("d_cG", cG, cG[:, :], [128, NB])
            self.dbg("d_cdl", cdl, cdl[:, :], [128, NB])
            self.dbg("d_cbg", cbg, cbg[:, :], [128, NB])
            def blk(b, st):
                dm, E, U0, R, kbg, vb, Pq = st["dm"], st["E"], st["U0"], st["R"], st["kbg"], st["vb"], st["P"]
                bs = slice(b * 128, (b + 1) * 128)
                k.op(k.pe, lambda e: e.matmul(Pq[0][:, 0:128], lhsT=ks[:, bs], rhs=ks[:, bs], start=True, stop=True),
                     reads=[ks], writes=[Pq[0]])
                k.op(k.pe, lambda e: e.matmul(Pq[1][:, 0:128], lhsT=ks[:, bs], rhs=qs[:, bs], start=True, stop=True),
                     reads=[ks, qs], writes=[Pq[1]])
                k.op(k.dve, lambda e: e.tensor_scalar(out=dm[:, :], in0=Gr[:, bs], scalar1=cG[:, b:b + 1],
                                                      scalar2=0.0, op0=ALU.subtract, op1=ALU.min),
                     reads=[Gr, cG], writes=[dm])
                k.op(k.act, lambda e: e.activation(out=E[:, :], in_=dm[:, :], func=AF.Exp), reads=[dm], writes=[E])
                yield
                k.op(k.dve, lambda e: e.tensor_tensor(out=E[:, :], in0=E[:, :], in1=self.maskU[:, :], op=ALU.mult),
                     reads=[E, self.maskU], writes=[E])
                k.op(k.dve, lambda e: e.tensor_tensor(out=self.a_qkT[:, b, :], in0=Pq[1][:, 0:128], in1=E[:, :],
                                                      op=ALU.mult), reads=[Pq[1], E], writes=[self.a_qkT])
                k.op(k.dve, lambda e: e.tensor_tensor(out=U0[:, :], in0=Pq[0][:, 0:128], in1=E[:, :], op=ALU.mult),
                     reads=[Pq[0], E], writes=[U0])
                yield
                k.op(k.pool, lambda e: e.tensor_tensor(out=U0[:, :], in0=U0[:, :], in1=self.maskS[:, :], op=ALU.mult),
                     reads=[U0, self.maskS], writes=[U0])
                X0, XT0 = st["X"][0], st["XT"][0]
                k.op(k.pool, lambda e: e.tensor_tensor(out=X0[:, :], in0=U0[:, :], in1=br[:, bs], op=ALU.mult),
                     reads=[U0, br], writes=[X0])
                yield
                k.op(k.pe, lambda e: e.transpose(Pq[3][:, 0:128], X0[:, :], ident[:, :]), reads=[X0, ident],
                     writes=[Pq[3]])
                k.op(k.act, lambda e: e.copy(out=XT0[:, :], in_=Pq[3][:, 0:128]), reads=[Pq[3]], writes=[XT0])
                k.op(k.dve, lambda e: e.tensor_tensor(out=R[:, :], in0=ident[:, :], in1=X0[:, :], op=ALU.subtract),
                     reads=[ident, X0], writes=[R])
                yield
                X, XT = X0, XT0
                for lvl in range(1, 6):
                    Xn, XTn = st["X"][lvl % 2], st["XT"][lvl % 2]
                    if lvl < 5:
                        k.op(k.pe, lambda e, X=X, XT=XT: e.matmul(Pq[0][:, 0:128], lhsT=XT[:, :], rhs=X[:, :], start=True,
                                                                  stop=True), reads=[X, XT], writes=[Pq[0]])
                    k.op(k.pe, lambda e, X=X, XT=XT: e.matmul(Pq[1][:, 0:128], lhsT=X[:, :], rhs=XT[:, :], start=True,
                                                              stop=True), reads=[X, XT], writes=[Pq[1]])
                    yield
                    if lvl < 5:
                        k.op(k.act, lambda e, Xn=Xn: e.copy(out=Xn[:, :], in_=Pq[0][:, 0:128]), reads=[Pq[0]], writes=[Xn])
                    k.op(k.dve, lambda e, XTn=XTn: e.tensor_copy(out=XTn[:, :], in_=Pq[1][:, 0:128]), reads=[Pq[1]],
                         writes=[XTn])
                    yield
                    k.op(k.pe, lambda e, XTn=XTn: e.matmul(Pq[2][:, 0:128], lhsT=XTn[:, :], rhs=R[:, :], start=True,
                                                           stop=True), reads=[XTn, R], writes=[Pq[2]])
                    yield
                    k.op(k.dve, lambda e: e.tensor_tensor(out=R[:, :], in0=R[:, :], in1=Pq[2][:, 0:128], op=ALU.add),
                         reads=[R, Pq[2]], writes=[R])
                    yield
                    X, XT = Xn, XTn
                k.op(k.pe, lambda e: e.transpose(Pq[3][:, 0:128], ks[:, bs], ident[:, :]), reads=[ks, ident],
                     writes=[Pq[3]])
                k.op(k.pe, lambda e: e.transpose(Pq[0][:, 0:128], vs[:, bs], ident[:, :]), reads=[vs, ident],
                     writes=[Pq[0]])
                yield
                k.op(k.act, lambda e: e.activation(out=kbg[:, :], in_=Pq[3][:, 0:128], func=AF.Copy,
                                                   scale=cbg[:, b:b + 1]), reads=[Pq[3], cbg], writes=[kbg])
                k.op(k.dve, lambda e: e.tensor_scalar(out=vb[:, :], in0=Pq[0][:, 0:128], scalar1=col[:, 0, b:b + 1],
                                                      scalar2=None, op0=ALU.mult), reads=[Pq[0], col], writes=[vb])
                k.op(k.dve, lambda e: e.tensor_scalar(out=self.a_kd[:, b, :], in0=Pq[3][:, 0:128],
                                                      scalar1=cdl[:, b:b + 1], scalar2=None, op0=ALU.mult),
                     reads=[Pq[3], cdl], writes=[self.a_kd])
                yield
                k.op(k.pe, lambda e: e.matmul(Pq[1][:, 0:128], lhsT=R[:, :], rhs=vb[:, :], start=True, stop=True),
                     reads=[R, vb], writes=[Pq[1]])
                k.op(k.pe, lambda e: e.matmul(Pq[2][:, 0:128], lhsT=kbg[:, :], rhs=R[:, :], start=True, stop=True),
                     reads=[R, kbg], writes=[Pq[2]])
                yield
                k.op(k.act, lambda e: e.copy(out=self.a_u[:, b, :], in_=Pq[1][:, 0:128]), reads=[Pq[1]],
                     writes=[self.a_u])
                k.op(k.dve, lambda e: e.tensor_copy(out=self.a_wT[:, bs], in_=Pq[2][:, 0:128]), reads=[Pq[2]],
                     writes=[self.a_wT])
                if b == 0:
                    self.dbg("d_R", R, R[:, :], [128, 128])

            for b0 in range(0, NB, 2):
                gens = [blk(b0 + q, self.a_sets[q]) for q in range(min(2, NB - b0))]
                while gens:
                    for g_ in list(gens):
                        try:
                            next(g_)
                        except StopIteration:
                            gens.remove(g_)
            self.dbg("d_u", self.a_u, self.a_u[:, :, :], [128, NB, 128])
            self.dbg("d_wT", self.a_wT, self.a_wT[:, :], [128, SG], BF16)
            self.dbg("d_qkT", self.a_qkT, self.a_qkT[:, :, :], [128, NB, 128], BF16)
            self.dbg("d_kd", self.a_kd, self.a_kd[:, :, :], [128, NB, 128], BF16)
            k.dma(k.sp, self.a_z[:, :, :], z_d[s0:s0 + SG, :].rearrange("(b p) e -> p b e", p=128), self.a_z,
                  writes=[self.a_z])
            wT, au, qkT, kd, ao = self.a_wT, self.a_u, self.a_qkT, self.a_kd, self.a_o
            for n in range(2 * NB):
                b, hf = n // 2, n % 2
                bs = slice(b * 128, (b + 1) * 128)
                rs = slice(hf * 64, hf * 64 + 64)
                vn = self.a_vn.next()
                with nc.allow_low_precision("bf16 matmul"):
                    k.op(k.pe, lambda e, bs=bs: e.matmul(P[4][:, 0:128], lhsT=wT[:, bs], rhs=Sb[:, :], start=True, stop=True),
                         reads=[wT, Sb], writes=[P[4]])
                    k.op(k.dve, lambda e, rs=rs, b=b, vn=vn: e.tensor_tensor(out=vn[rs, :], in0=au[rs, b, :],
                                                                            in1=P[4][rs, 0:128], op=ALU.subtract),
                         reads=[au, P[4]], writes=[vn])
                    k.op(k.pe, lambda e, bs=bs: e.matmul(P[5][:, 0:128], lhsT=qg[:, bs], rhs=Sb[:, :], start=True, stop=False),
                         reads=[qg, Sb], writes=[P[5]])
                    k.op(k.pe, lambda e, rs=rs, b=b, vn=vn: e.matmul(P[5][:, 0:128], lhsT=qkT[rs, b, :], rhs=vn[rs, :],
                                                                    start=False, stop=True),
                         reads=[qkT, vn], writes=[P[5]], accum=True)
                    k.op(k.pe, lambda e, rs=rs, b=b, vn=vn: e.matmul(P[6][:, 0:128], lhsT=kd[rs, b, :], rhs=vn[rs, :],
                                                                    start=True, stop=True),
                         reads=[kd, vn], writes=[P[6]])
                c63 = n * 64 + 63
                k.op(k.dve, lambda e, c63=c63: e.scalar_tensor_tensor(out=S[:, :], in0=S[:, :], scalar=gr[:, c63:c63 + 1],
                                                                     in1=P[6][:, 0:128], op0=ALU.mult, op1=ALU.add),
                     reads=[S, gr, P[6]], writes=[S])
                k.op(k.act, lambda e: e.copy(out=Sb[:, :], in_=S[:, :]), reads=[S], writes=[Sb])
                k.op(k.act, lambda e, rs=rs, b=b: e.copy(out=ao[rs, b, :], in_=P[5][rs, 0:128]), reads=[P[5]], writes=[ao])
            self.dbg("d_o", ao, ao[:, :, :], [128, NB, 128])
            ss, az = self.a_ss, self.a_z
            sq3 = self.a_sq[:, :].rearrange("p (a b) -> p a b", b=128)
            k.op(k.act, lambda e: e.activation(out=sq3, in_=ao[:, :, :], func=AF.Square), reads=[ao], writes=[self.a_sq])
            k.op(k.dve, lambda e: e.reduce_sum(out=ss[:, :], in_=sq3, axis=mybir.AxisListType.X), reads=[self.a_sq],
                 writes=[ss])
            k.op(k.act, lambda e: e.activation(out=ss[:, :], in_=ss[:, :], func=AF.Sqrt, bias=self.c_eps[:, 0:1],
                                               scale=1.0 / HD), reads=[ss, self.c_eps], writes=[ss])
            k.op(k.dve, lambda e: e.reciprocal(out=ss[:, :], in_=ss[:, :]), reads=[ss], writes=[ss])
            k.op(k.act, lambda e: e.activation(out=az[:, :, :], in_=az[:, :, :], func=AF.Silu), reads=[az], writes=[az])
            k.op(k.dve, lambda e: e.tensor_tensor(out=ao[:, :, :], in0=ao[:, :, :],
                                                  in1=ss[:, :].unsqueeze(2).to_broadcast([128, NB, 128]), op=ALU.mult),
                 reads=[ao, ss], writes=[ao])
            k.op(k.dve, lambda e: e.tensor_tensor(out=ao[:, :, :], in0=ao[:, :, :],
                                                  in1=na[:, :].unsqueeze(1).to_broadcast([128, NB, 128]), op=ALU.mult),
                 reads=[ao, na], writes=[ao])
            ob = self.a_ob.next()
            k.op(k.dve, lambda e, ob=ob: e.tensor_tensor(out=ob[:, :, :], in0=ao[:, :, :], in1=az[:, :, :], op=ALU.mult),
                 reads=[ao, az], writes=[ob])
            k.dma(k.sp, o_d[s0:s0 + SG, :].rearrange("(b p) e -> p b e", p=128), ob[:, :, :], ob, reads=[ob])


class AdaProg:
    NCOL = 9 * D // NCORE

    def __init__(self):
        nc = bass.Bass("TRN2", target_bir_lowering=False)
        self.nc = nc
        NCOL = self.NCOL
        w_d = nc.dram_tensor("ada_w", [DEPTH, D, NCOL], F32, kind="ExternalInput").ap()
        b_d = nc.dram_tensor("ada_b", [DEPTH, 2, NCOL], F32, kind="ExternalInput").ap()
        c_d = nc.dram_tensor("ada_c", [128, KC, 2], F32, kind="ExternalInput").ap()
        o_d = nc.dram_tensor("ada_o", [DEPTH, 2, NCOL], F32, kind="ExternalOutput").ap()
        with ExitStack() as st:
            k = KB(nc, st)
            ca = k.sb([128, KC, 2], F32, "ca")
            k.dma(k.sp, ca[:, :, :], c_d[:, :, :], ca, writes=[ca])
            k.op(k.act, lambda e: e.activation(out=ca[:, :, :], in_=ca[:, :, :], func=AF.Silu), reads=[ca], writes=[ca])
            HW = 2560
            wr = Ring(k, 3, [128, HW], F32, "wr")
            P = [k.ps([128, 512], F32, f"P{i}") for i in range(5)]
            bt = k.sb([2, DEPTH, NCOL], F32, "bt")
            res = k.sb([2, DEPTH, NCOL], F32, "res")
            for l in range(DEPTH):
                k.dma(k.sp, bt[:, l, :], b_d[l, :, :], bt, writes=[bt], join=(l > 0))
            for l in range(DEPTH):
                for c0 in (0, HW):
                    cw = min(HW, NCOL - c0)
                    nb = (cw + 511) // 512
                    for kc in range(KC):
                        wt = wr.next()
                        k.dma(k.sp if kc % 2 == 0 else k.act, wt[:, 0:cw], w_d[l, kc * 128:(kc + 1) * 128, c0:c0 + cw], wt,
                              writes=[wt])
                        for bi in range(nb):
                            n = min(512, cw - bi * 512)
                            k.op(k.pe, lambda e, bi=bi, n=n, kc=kc, wt=wt: e.matmul(
                                P[bi][0:2, 0:n], lhsT=ca[:, kc, :], rhs=wt[:, bi * 512:bi * 512 + n],
                                start=(kc == 0), stop=(kc == KC - 1)), reads=[ca, wt], writes=[P[bi]], accum=(kc > 0))
                    for bi in range(nb):
                        n = min(512, cw - bi * 512)
                        k.op(k.dve, lambda e, bi=bi, n=n, l=l, c0=c0: e.tensor_tensor(
                            out=res[0:2, l, c0 + bi * 512:c0 + bi * 512 + n], in0=P[bi][0:2, 0:n],
                            in1=bt[0:2, l, c0 + bi * 512:c0 + bi * 512 + n], op=ALU.add),
                            reads=[P[bi], bt], writes=[res])
            for l in range(DEPTH):
                k.dma(k.sp, o_d[l, :, :], res[0:2, l, :], res, reads=[res])
            k.wait_tiles(k.sp, [res])
            k.finish()


_PROGS = {}


def _prog(key, fn):
    if key not in _PROGS:
        _PROGS[key] = fn()
    return _PROGS[key]


def _run(prog_nc, in_maps):
    res = run_bass_kernel_spmd(prog_nc, in_maps, core_ids=list(range(NCORE)))
    return res.results


def _consts():
    idx = np.arange(128)
    same = (idx[:, None] // 64) == (idx[None, :] // 64)
    return {
        "ident_in": np.eye(128, dtype=np.float32),
        "mask_su_in": np.triu(np.ones((128, 128), np.float32)),
        "maskU_in": (same & (idx[:, None] <= idx[None, :])).astype(np.float32),
        "maskS_in": (same & (idx[:, None] < idx[None, :])).astype(np.float32),
        "sameC_in": same.astype(np.float32),
    }


def _pick(d, names):
    return {n: d[n] for n in names}


def kernel(x, c, w_ada, b_ada, ln_g, ln_b, w_ffn_in, w_ffn_out, w_in, conv_w, a_log, dt_bias, norm_a,
           sgu_ln_g, sgu_ln_b, w_s, b_s, b_f, norm_c, w_o):
    f32 = np.float32
    x = np.asarray(x, f32)
    consts = _consts()
    ada = _prog("ada", AdaProg)
    NCOL = AdaProg.NCOL
    cT = np.ascontiguousarray(np.asarray(c, f32).reshape(B, KC, 128).transpose(2, 1, 0))
    in_maps = []
    for i in range(NCORE):
        cs = slice(i * NCOL, (i + 1) * NCOL)
        in_maps.append({
            "ada_w": np.ascontiguousarray(np.asarray(w_ada)[:, :, cs]),
            "ada_b": np.ascontiguousarray(np.broadcast_to(np.asarray(b_ada, f32)[:, None, cs], (DEPTH, 2, NCOL))),
            "ada_c": cT,
        })
    r = _run(ada.nc, in_maps)
    del in_maps
    mod = np.concatenate([r[i]["ada_o"] for i in range(NCORE)], axis=-1)
    mods = mod.reshape(DEPTH, B, 9, D)

    def stage_vecs(tag, l, b_, kind, j=0):
        out = {}
        if kind == "ffn":
            base, lni = (0, 0) if j == 0 else (6, 2)
            out[f"sh_{tag}"] = vec_pp(mods[l, b_, base])
            out[f"sc_{tag}"] = vec_pp(mods[l, b_, base + 1])
            out[f"ga_{tag}"] = vec_pp(mods[l, b_, base + 2])
            out[f"lng_{tag}"] = vec_pp(ln_g[l][lni])
            out[f"lnb_{tag}"] = vec_pp(ln_b[l][lni])
        elif kind == "mixin":
            out[f"sh_{tag}"] = vec_pp(mods[l, b_, 3])
            out[f"sc_{tag}"] = vec_pp(mods[l, b_, 4])
        elif kind == "mixout":
            out[f"ga_{tag}"] = vec_pp(mods[l, b_, 5])
            out[f"lng_{tag}"] = vec_pp(ln_g[l][1])
            out[f"lnb_{tag}"] = vec_pp(ln_b[l][1])
        return out

    def core_tok(i):
        return i // 4, (i % 4) * NTOK

    def weights_for(stages):
        w = {}
        for kind, tag, l, j in stages:
            if kind == "ffn":
                w[f"wfi_{tag}"] = prep_ffn_in(np.asarray(w_ffn_in[l][j], f32))
                w[f"wfo_{tag}"] = prep_ffn_out(np.asarray(w_ffn_out[l][j], f32))
            elif kind == "mixin":
                w[f"win_{tag}"] = prep_win(np.asarray(w_in[l], f32))
                w.update(b_consts(tag, np.asarray(w_s[l], f32), np.asarray(b_s[l], f32), np.asarray(sgu_ln_g[l], f32),
                                  np.asarray(sgu_ln_b[l], f32)))
            elif kind == "mixout":
                w[f"wo_{tag}"] = prep_wo(np.asarray(w_o[l], f32))
        return w

    def run_tok(stages, xT_list, mix_list=None):
        key = ("tok",) + tuple((kd, tg) for kd, tg, _, _ in stages)
        prog = _prog(key, lambda: TokProg([(kd, tg) for kd, tg, _, _ in stages]))
        w = weights_for(stages)
        in_maps = []
        for i in range(NCORE):
            b_, s0 = core_tok(i)
            m = {"ident_in": consts["ident_in"], "x_in": xT_list[i]}
            m.update(w)
            for kind, tag, l, j in stages:
                m.update(stage_vecs(tag, l, b_, kind, j))
                if kind == "mixout":
                    m[f"mix_{tag}"] = mix_list[i]
            in_maps.append(_pick(m, prog.in_names))
        r_ = _run(prog.nc, in_maps)
        del in_maps, w
        return r_

    def run_head(l, EA, EC, ES):
        prog = _prog("head", lambda: HeadProg(3, 3, S))
        in_maps = []
        for i in range(NCORE):
            m = dict(consts)
            for sl in range(3):
                uid = i * 3 + sl
                b_, h = uid // H_A, uid % H_A
                m[f"a_qkv_{sl}"] = np.ascontiguousarray(np.stack(
                    [EA[b_][t_ * D_A + h * 128:t_ * D_A + (h + 1) * 128] for t_ in range(3)]))
                m[f"a_z_{sl}"] = np.ascontiguousarray(EA[b_][3 * D_A + h * 128:3 * D_A + (h + 1) * 128].T)
                bd = np.ascontiguousarray(np.stack([ES[b_][h], ES[b_][H_A + h]]))
                m[f"a_bd_{sl}"] = bd
                m[f"a_bdc_{sl}"] = np.ascontiguousarray(bd.reshape(2, S // 128, 128).transpose(2, 0, 1))
                par = np.zeros((128, 16), f32)
                for t_ in range(3):
                    for j in range(4):
                        par[:, 4 * t_ + j] = conv_w[l][j, t_ * D_A + h * 128:t_ * D_A + (h + 1) * 128]
                par[:, 12] = a_log[l][h]
                par[:, 13] = dt_bias[l][h]
                m[f"a_par_{sl}"] = par
                m[f"a_na_{sl}"] = np.ascontiguousarray(np.broadcast_to(np.asarray(norm_a[l], f32)[None], (128, 128)))
                m[f"c_qk_{sl}"] = np.ascontiguousarray(np.stack(
                    [EC[b_][t_ * D_C + h * 128:t_ * D_C + (h + 1) * 128] for t_ in range(2)]))
                m[f"c_v_{sl}"] = np.ascontiguousarray(EC[b_][2 * D_C + h * 128:2 * D_C + (h + 1) * 128].T)
                m[f"c_f_{sl}"] = np.ascontiguousarray(ES[b_][2 * H_A + h][None])
                cp = np.zeros((128, 2), f32)
                cp[:, 0] = b_f[l][h]
                cp[:, 1] = norm_c[l]
                m[f"c_par_{sl}"] = cp
            in_maps.append(_pick(m, prog.in_names))
        r_ = _run(prog.nc, in_maps)
        o_a = [[None] * H_A for _ in range(B)]
        o_c = [[None] * H_C for _ in range(B)]
        for i in range(NCORE):
            for sl in range(3):
                uid = i * 3 + sl
                b_, h = uid // H_A, uid % H_A
                o_a[b_][h] = r_[i][f"a_o_{sl}"]
                o_c[b_][h] = r_[i][f"c_o_{sl}"]
        return o_a, o_c

    def gather(r_, name):
        return [np.concatenate([r_[b_ * 4 + q][name] for q in range(4)], axis=1) for b_ in range(B)]

    def build_mix(o_a, o_c, EB):
        mix = []
        for i in range(NCORE):
            b_, s0 = core_tok(i)
            rows = [np.ascontiguousarray(o_a[b_][h][s0:s0 + NTOK].T) for h in range(H_A)]
            rows.append(EB[i])
            rows += [o_c[b_][h][:, s0:s0 + NTOK] for h in range(H_C)]
            mix.append(np.ascontiguousarray(np.concatenate(rows, axis=0)))
        return mix

    xT = []
    for i in range(NCORE):
        b_, s0 = core_tok(i)
        xT.append(np.ascontiguousarray(x[b_, s0:s0 + NTOK].T))
    r1 = run_tok([("ffn", "f00", 0, 0), ("mixin", "m0", 0, 0)], xT)
    x1 = [r1[i]["x_out"] for i in range(NCORE)]
    EB = [r1[i]["EB_m0"] for i in range(NCORE)]
    o_a, o_c = run_head(0, gather(r1, "EA_m0"), gather(r1, "EC_m0"), gather(r1, "ES_m0"))
    del r1
    mix = build_mix(o_a, o_c, EB)
    r3 = run_tok([("mixout", "o0", 0, 0), ("ffn", "f01", 0, 1), ("ffn", "f10", 1, 0), ("mixin", "m1", 1, 0)], x1, mix)
    x1 = [r3[i]["x_out"] for i in range(NCORE)]
    EB = [r3[i]["EB_m1"] for i in range(NCORE)]
    o_a, o_c = run_head(1, gather(r3, "EA_m1"), gather(r3, "EC_m1"), gather(r3, "ES_m1"))
    del r3
    mix = build_mix(o_a, o_c, EB)
    r5 = run_tok([("mixout", "o1", 1, 0), ("ffn", "f11", 1, 1)], x1, mix)
    out = np.empty((B, S, D), f32)
    for i in range(NCORE):
        b_, s0 = core_tok(i)
        out[b_, s0:s0 + NTOK] = r5[i]["x_out"].T
    return out
```

```python
import numpy as np
from contextlib import ExitStack
import concourse.bass as bass
import concourse.mybir as mybir
from concourse.bass_utils import run_bass_kernel_spmd
import ml_dtypes

F32 = mybir.dt.float32
BF16 = mybir.dt.bfloat16
AF = mybir.ActivationFunctionType
ALU = mybir.AluOpType
NPBF = ml_dtypes.bfloat16

D = 4096
B = 2
S = 8192
DEPTH = 2
HD = 128
H_A = 12
H_B = 8
H_C = 12
D_A = 1536
D_B = 1024
D_C = 1536
DFF = 8192
NCORE = 8
NTOK = 2048
T = 512
NTT = NTOK // T
KC = D // 128
ALPHA = (2.0 * DEPTH) ** 0.25
LN_EPS = 1e-5
RMS_EPS = 1e-6


class Tl:
    __slots__ = ("h", "w", "r", "dsem", "dcnt", "name", "excl")

    def __init__(self, h, name):
        self.excl = False
        self.h = h
        self.name = name
        self.w = None
        self.r = {}
        self.dsem = None
        self.dcnt = 0

    def __getitem__(self, k):
        return self.h[k]


class Eng:
    def __init__(self, key, h, sem):
        self.key = key
        self.h = h
        self.sem = sem
        self.cnt = 0
        self.seen = {}
        self.prog = []


class KB:
    def __init__(self, nc, stack):
        self.nc = nc
        self.stack = stack
        self.pe = self._eng("pe", nc.tensor)
        self.act = self._eng("act", nc.scalar)
        self.dve = self._eng("dve", nc.vector)
        self.pool = self._eng("pool", nc.gpsimd)
        self.sp = self._eng("sp", nc.sync)
        self.nt = 0
        self.dma_tls = []

    def _sem(self, name):
        return self.stack.enter_context(self.nc.semaphore(name))

    def _eng(self, key, h):
        return Eng(key, h, self._sem("s_" + key))

    def sb(self, shape, dt, name):
        h = self.stack.enter_context(self.nc.sbuf_tensor("sb_" + name, list(shape), dt))
        return Tl(h, name)

    def ps(self, shape, dt, name):
        h = self.stack.enter_context(self.nc.psum_tensor(name, list(shape), dt))
        t = Tl(h, name)
        t.excl = True
        return t

    def sub(self, name):
        self.nt += 1
        return Tl(None, f"{name}_{self.nt}")

    def _deps(self, eng, reads, writes, accum=False, join=None):
        deps = {}

        def add(tok):
            if tok is None:
                return
            key, sem, val = tok
            if key not in deps or deps[key][1] < val:
                deps[key] = (sem, val)

        for t in reads:
            add(t.w)
            if t.excl:
                for kk, tok in t.r.items():
                    if kk != eng.key:
                        add(tok)
        for t in writes:
            w = t.w
            if w is not None:
                if accum and w[0] == eng.key:
                    pass
                elif join is not None and w[0] == join:
                    pass
                else:
                    add(w)
            for tok in t.r.values():
                add(tok)
        waits = []
        for key, (sem, val) in deps.items():
            if eng.seen.get(key, 0) >= val:
                continue
            eng.seen[key] = val
            waits.append((sem, val))
        return waits

    @staticmethod
    def _mark(tok, reads, writes):
        key = tok[0]
        for t in reads:
            t.r[key] = tok
        for t in writes:
            t.w = tok
            t.r = {}

    def op(self, eng, fn, reads=(), writes=(), accum=False):
        waits = self._deps(eng, reads, writes, accum=accum)
        eng.cnt += 1
        tok = (eng.key, eng.sem, eng.cnt)
        eng.prog.append((waits, fn, eng.sem, 1))
        self._mark(tok, reads, writes)
        return tok

    def dma(self, eng, out_ap, in_ap, sbt, reads=(), writes=(), join=False, **kw):
        if sbt.dsem is None:
            sbt.dsem = self._sem("d_" + sbt.name)
            self.dma_tls.append(sbt)
        dkey = ("d", sbt.name)
        waits = self._deps(eng, reads, writes, join=dkey if join else None)
        sbt.dcnt += 16
        tok = (dkey, sbt.dsem, sbt.dcnt)
        eng.prog.append((waits, lambda e: e.dma_start(out=out_ap, in_=in_ap, **kw), sbt.dsem, 16))
        self._mark(tok, reads, writes)
        return tok

    def barrier(self):
        engs = [self.pe, self.act, self.dve, self.pool, self.sp]
        for e in engs:
            waits = []
            for o in engs:
                if o is e or o.cnt == 0:
                    continue
                if e.seen.get(o.key, 0) < o.cnt:
                    e.seen[o.key] = o.cnt
                    waits.append((o.sem, o.cnt))
            for t in self.dma_tls:
                dk = ("d", t.name)
                if e.seen.get(dk, 0) < t.dcnt:
                    e.seen[dk] = t.dcnt
                    waits.append((t.dsem, t.dcnt))
            if waits:
                e.prog.append((waits, None, None, 0))

    def wait_tiles(self, eng, tiles):
        waits = self._deps(eng, (), tiles)
        if waits:
            eng.prog.append((waits, None, None, 0))

    def finish(self):
        nc = self.nc
        with nc.Block() as block:
            def run(eng):
                def body(e):
                    for waits, fn, sem, inc in eng.prog:
                        for s, v in waits:
                            e.wait_ge(s, v)
                        if fn is not None:
                            fn(e).then_inc(sem, inc)
                return body

            block.tensor(run(self.pe))
            block.scalar(run(self.act))
            block.vector(run(self.dve))
            block.gpsimd(run(self.pool))
            block.sync(run(self.sp))


class Ring:
    def __init__(self, k, n, shape, dt, name, alloc=None):
        alloc = alloc or k.sb
        self.tiles = [alloc(shape, dt, f"{name}{i}") for i in range(n)]
        self.i = 0

    def next(self):
        t = self.tiles[self.i % len(self.tiles)]
        self.i += 1
        return t


def vec_pp(v):
    v = np.asarray(v, np.float32)
    return np.ascontiguousarray(v.reshape(-1, 128).T)


def blocks_in(W, cols_blocks):
    K = W.shape[0]
    kc = K // 128
    nblk, ncol = cols_blocks.shape
    Wp = np.concatenate([W, np.zeros((K, 1), W.dtype)], axis=1)
    out = np.empty((nblk, 128, kc, ncol), np.float32)
    for b in range(nblk):
        blk = Wp[:, cols_blocks[b]]
        out[b] = blk.reshape(kc, 128, ncol).transpose(1, 0, 2)
    return out.reshape(nblk, 128, kc * ncol)


OFF_QKVA = 0
OFF_Z = 4608
OFF_BETA = 6144
OFF_DEC = 6156
OFF_UV = 6168
OFF_QKVC = 8216
OFF_F = 12824
D_IN = 12836


def mixin_block_plan():
    blocks = []
    for i in range(36):
        blocks.append(("A", i, np.arange(OFF_QKVA + i * 128, OFF_QKVA + (i + 1) * 128)))
    for i in range(12):
        blocks.append(("A", 36 + i, np.arange(OFF_Z + i * 128, OFF_Z + (i + 1) * 128)))
    for i in range(36):
        blocks.append(("C", i, np.arange(OFF_QKVC + i * 128, OFF_QKVC + (i + 1) * 128)))
    for i in range(8):
        blocks.append(("U", i, np.arange(OFF_UV + i * 128, OFF_UV + (i + 1) * 128)))
    for i in range(8):
        blocks.append(("V", i, np.arange(OFF_UV + 1024 + i * 128, OFF_UV + 1024 + (i + 1) * 128)))
    small = np.full(128, -1, np.int64)
    small[0:12] = np.arange(OFF_BETA, OFF_BETA + 12)
    small[12:24] = np.arange(OFF_DEC, OFF_DEC + 12)
    small[24:36] = np.arange(OFF_F, OFF_F + 12)
    blocks.append(("S", 0, small))
    blocks.append(("P", 0, np.full(128, -1, np.int64)))
    return blocks


class TokProg:
    def __init__(self, stages, ntt=NTT, debug=False):
        self.debug = debug
        self.stages = stages
        self.ntt = ntt
        self.ntok = ntt * T
        nc = bass.Bass("TRN2", target_bir_lowering=False)
        self.nc = nc
        self.in_names = []
        self.out_names = []
        with ExitStack() as st:
            self.k = KB(nc, st)
            self._alloc()
            self._emit()
            self.k.finish()

    def din(self, name, shape, dt=F32):
        self.in_names.append(name)
        return self.nc.dram_tensor(name, list(shape), dt, kind="ExternalInput").ap()

    def dout(self, name, shape, dt=F32):
        self.out_names.append(name)
        return self.nc.dram_tensor(name, list(shape), dt, kind="ExternalOutput").ap()

    def dint(self, name, shape, dt=F32):
        return self.nc.dram_tensor(name, list(shape), dt, kind="Internal").ap()

    def _alloc(self):
        k = self.k
        self.hx = k.sb([128, 16384], F32, "hx")
        self.xn = self.hx[:, :].rearrange("p (c t) -> p c t", t=T)
        self.hTv = self.hx[:, 0:8192].bitcast(BF16).rearrange("p (c t) -> p c t", t=T)
        self.hT = k.sub("hT")
        self.xn_t = [k.sub("xn") for _ in range(KC)]
        self.big = k.sb([128, 16384], F32, "big")
        self.bigv = self.big[:, :].bitcast(BF16).rearrange("p (c t) -> p c t", t=T)
        self.wslots = Ring(k, 2, [128, 8192], BF16, "wsl")
        self.xs = Ring(k, 2, [128, T], F32, "xs")
        self.xo = Ring(k, 2, [128, T], F32, "xo")
        self.tb = Ring(k, 2, [128, T], F32, "tb")
        self.sq = Ring(k, 2, [128, T], F32, "sq")
        self.cst = Ring(k, 2, [128, T], BF16, "cst")
        self.ones = k.sb([128, 128], F32, "ones")
        self.ident = k.sb([128, 128], F32, "ident")
        self.mean = k.sb([128, T], F32, "mean")
        self.rstd = k.sb([128, T], F32, "rstd")
        self.acc_s = self.mean
        self.acc_q = self.rstd
        self.vecs = {}
        self.P = [k.ps([128, 512], F32, f"P{i}") for i in range(8)]
        self.ident_d = self.din("ident_in", [128, 128])
        k.op(k.dve, lambda e: e.memset(self.ones[:, :], 1.0), writes=[self.ones])
        k.dma(k.sp, self.ident[:, :], self.ident_d[:, :], self.ident, writes=[self.ident])
        self.eps_ln = k.sb([128, 1], F32, "eps_ln")
        k.op(k.dve, lambda e: e.memset(self.eps_ln[:, :], float(LN_EPS)), writes=[self.eps_ln])

    def vec(self, name):
        if name in self.vecs:
            return self.vecs[name]
        k = self.k
        d = self.din(name, [128, KC])
        t = k.sb([128, KC], F32, "v_" + name)
        k.dma(k.sp, t[:, :], d[:, :], t, writes=[t])
        self.vecs[name] = t
        return t

    def derived(self, name, src, mul, add):
        if name in self.vecs:
            return self.vecs[name]
        k = self.k
        t = k.sb([128, KC], F32, "v_" + name)
        k.op(k.dve, lambda e: e.tensor_scalar(out=t[:, :], in0=src[:, :], scalar1=float(mul), scalar2=float(add),
                                              op0=ALU.mult, op1=ALU.add), reads=[src], writes=[t])
        self.vecs[name] = t
        return t

    def stage_mod(self, x_d, tt, scp, sh):
        k = self.k
        hT = self.hTv
        for kc in range(KC):
            xs = self.xs.next()
            k.dma(k.sp, xs[:, :], x_d[kc * 128:(kc + 1) * 128, tt * T:(tt + 1) * T], xs, writes=[xs])
            k.op(k.act, lambda e, xs=xs, kc=kc: e.activation(out=hT[:, kc, :], in_=xs[:, :], func=AF.Identity,
                                                              bias=sh[:, kc:kc + 1], scale=scp[:, kc:kc + 1]),
                 reads=[xs, scp, sh], writes=[self.hT] + self.xn_t[0:16])

    def linear_in(self, w_d, npairs, epilogue, bank0=0):
        k = self.k
        nc = self.nc
        hT = self.hTv
        for pi in range(npairs):
            ws = self.wslots.next()
            k.dma(k.pool, ws[:, :], w_d[pi, :, :], ws, writes=[ws])
            banks = [self.P[bank0 + 2 * (pi % 2)], self.P[bank0 + 2 * (pi % 2) + 1]]
            w3 = ws[:, :].rearrange("p (k c) -> p k c", c=256)
            nb = epilogue(pi, None)
            with nc.allow_low_precision("bf16 matmul"):
                for j in range(nb):
                    ps = banks[j]
                    for kc in range(KC):
                        k.op(k.pe, lambda e, ps=ps, kc=kc, j=j, w3=w3: e.matmul(
                            ps[:, :], lhsT=w3[:, kc, j * 128:(j + 1) * 128], rhs=hT[:, kc, :],
                            start=(kc == 0), stop=(kc == KC - 1)),
                            reads=[ws, self.hT], writes=[ps], accum=(kc > 0))
            epilogue(pi, banks)

    def linear_out(self, w_d, nkc, bps, opnd, xres_d, xout_d, tt, gco, lng, lnb, bank0=4):
        k = self.k
        nc = self.nc
        ones = self.ones
        psum_s, psum_q = self.P[6], self.P[7]
        pend = None
        for m in range(KC):
            if m % bps == 0:
                ws = self.wslots.next()
                k.dma(k.pool, ws[:, 0:bps * nkc * 128], w_d[m // bps, :, :], ws, writes=[ws])
                w4 = ws[:, 0:bps * nkc * 128].rearrange("p (b k c) -> p b k c", b=bps, c=128)
            ps = self.P[bank0 + (m % 2)]
            with nc.allow_low_precision("bf16 matmul"):
                for kc in range(nkc):
                    k.op(k.pe, lambda e, ps=ps, kc=kc, w4=w4, bi=m % bps: e.matmul(
                        ps[:, :], lhsT=w4[:, bi, kc, :], rhs=opnd(kc), start=(kc == 0), stop=(kc == nkc - 1)),
                        reads=[ws, self.big], writes=[ps], accum=(kc > 0))
            xs = self.xs.next()
            k.dma(k.sp, xs[:, :], xres_d[m * 128:(m + 1) * 128, tt * T:(tt + 1) * T], xs, writes=[xs])
            tb = self.tb.next()
            k.op(k.act, lambda e, ps=ps, tb=tb, m=m: e.activation(out=tb[:, :], in_=ps[:, :], func=AF.Copy,
                                                                  scale=gco[:, m:m + 1]),
                 reads=[ps, gco], writes=[tb])
            xt = self.xn_t[m]
            k.op(k.dve, lambda e, xs=xs, tb=tb, m=m: e.scalar_tensor_tensor(
                out=self.xn[:, m, :], in0=xs[:, :], scalar=float(ALPHA), in1=tb[:, :], op0=ALU.mult, op1=ALU.add),
                reads=[xs, tb], writes=[xt] + ([self.hT] if m < 16 else []))
            sq = self.sq.next()
            k.op(k.act, lambda e, sq=sq, m=m: e.activation(out=sq[:, :], in_=self.xn[:, m, :], func=AF.Square),
                 reads=[xt], writes=[sq])
            acs, acq = self.acc_s, self.acc_q
            if m == 0:
                k.op(k.dve, lambda e, m=m: e.tensor_copy(out=acs[:, :], in_=self.xn[:, m, :]), reads=[xt], writes=[acs])
                k.op(k.dve, lambda e, sq=sq: e.tensor_copy(out=acq[:, :], in_=sq[:, :]), reads=[sq], writes=[acq])
            else:
                k.op(k.dve, lambda e, m=m: e.tensor_tensor(out=acs[:, :], in0=acs[:, :], in1=self.xn[:, m, :], op=ALU.add),
                     reads=[acs, xt], writes=[acs])
                k.op(k.dve, lambda e, sq=sq: e.tensor_tensor(out=acq[:, :], in0=acq[:, :], in1=sq[:, :], op=ALU.add),
                     reads=[acq, sq], writes=[acq])
        k.op(k.pe, lambda e: e.matmul(psum_s[:, :], lhsT=self.ones[:, :], rhs=self.acc_s[:, :], start=True, stop=True),
             reads=[self.ones, self.acc_s], writes=[psum_s])
        k.op(k.pe, lambda e: e.matmul(psum_q[:, :], lhsT=self.ones[:, :], rhs=self.acc_q[:, :], start=True, stop=True),
             reads=[self.ones, self.acc_q], writes=[psum_q])
        mean, rstd = self.mean, self.rstd
        msq = self.tb.next()
        k.op(k.dve, lambda e: e.tensor_scalar(out=mean[:, :], in0=psum_s[:, :], scalar1=1.0 / D, scalar2=None,
                                              op0=ALU.mult), reads=[psum_s], writes=[mean])
        k.op(k.dve, lambda e: e.tensor_tensor(out=msq[:, :], in0=mean[:, :], in1=mean[:, :], op=ALU.mult),
             reads=[mean], writes=[msq])
        k.op(k.dve, lambda e: e.scalar_tensor_tensor(out=rstd[:, :], in0=psum_q[:, :], scalar=1.0 / D, in1=msq[:, :],
                                                    op0=ALU.mult, op1=ALU.subtract),
             reads=[psum_q, msq], writes=[rstd])
        self.rsqrt(rstd, rstd[:, :], LN_EPS)
        for m in range(KC):
            xt = self.xn_t[m]
            tb = self.tb.next()
            k.op(k.dve, lambda e, tb=tb, m=m: e.tensor_tensor(out=tb[:, :], in0=self.xn[:, m, :], in1=mean[:, :],
                                                              op=ALU.subtract), reads=[xt, mean], writes=[tb])
            k.op(k.pool, lambda e, tb=tb: e.tensor_tensor(out=tb[:, :], in0=tb[:, :], in1=rstd[:, :], op=ALU.mult),
                 reads=[tb, rstd], writes=[tb])
            xo = self.xo.next()
            k.op(k.act, lambda e, tb=tb, xo=xo, m=m: e.activation(out=xo[:, :], in_=tb[:, :], func=AF.Identity,
                                                                  bias=lnb[:, m:m + 1], scale=lng[:, m:m + 1]),
                 reads=[tb, lng, lnb], writes=[xo])
            k.dma(k.sp, xout_d[m * 128:(m + 1) * 128, tt * T:(tt + 1) * T], xo[:, :], xo, reads=[xo])

    def rsqrt(self, tl, ap, eps):
        k = self.k
        k.op(k.act, lambda e: e.activation(out=ap, in_=ap, func=AF.Sqrt, bias=self.eps_ln[:, 0:1], scale=1.0),
             reads=[tl, self.eps_ln], writes=[tl])
        k.op(k.dve, lambda e: e.reciprocal(out=ap, in_=ap), reads=[tl], writes=[tl])

    def _stats_mm(self, m, xt, sq):
        k = self.k
        psum_s, psum_q = self.P[6], self.P[7]
        k.op(k.pe, lambda e: e.matmul(psum_s[:, :], lhsT=self.ones[:, :], rhs=self.xn[:, m, :],
                                      start=(m == 0), stop=(m == KC - 1)),
             reads=[self.ones, xt], writes=[psum_s], accum=(m > 0))
        k.op(k.pe, lambda e: e.matmul(psum_q[:, :], lhsT=self.ones[:, :], rhs=sq[:, :],
                                      start=(m == 0), stop=(m == KC - 1)),
             reads=[self.ones, sq], writes=[psum_q], accum=(m > 0))

    def _emit(self):
        k = self.k
        self.x_in = self.din("x_in", [D, self.ntok])
        cur = self.x_in
        plans = []
        last_x = max([i for i, (kd, _) in enumerate(self.stages) if kd in ("ffn", "mixout")] + [-1])
        for si, (kind, tag) in enumerate(self.stages):
            last = si == len(self.stages) - 1
            p = {"kind": kind, "tag": tag, "x_src": cur}
            if kind == "ffn":
                p["w1"] = self.din(f"wfi_{tag}", [64, 128, KC * 256])
                p["w2"] = self.din(f"wfo_{tag}", [32, 128, 64 * 128])
                sc = self.vec(f"sc_{tag}")
                p["scp"] = self.derived(f"scp_{tag}", sc, 1.0, 1.0)
                p["sh"] = self.vec(f"sh_{tag}")
                p["gco"] = self.derived(f"gco_{tag}", self.vec(f"ga_{tag}"), 0.5, 0.0)
                p["lng"] = self.vec(f"lng_{tag}")
                p["lnb"] = self.vec(f"lnb_{tag}")
            elif kind == "mixout":
                p["mix"] = self.din(f"mix_{tag}", [D, self.ntok], BF16)
                p["w2"] = self.din(f"wo_{tag}", [16, 128, 2 * KC * 128])
                p["gco"] = self.vec(f"ga_{tag}")
                p["lng"] = self.vec(f"lng_{tag}")
                p["lnb"] = self.vec(f"lnb_{tag}")
            elif kind == "mixin":
                p["w1"] = self.din(f"win_{tag}", [51, 128, KC * 256])
                sc = self.vec(f"sc_{tag}")
                p["scp"] = self.derived(f"scp_{tag}", sc, 1.0, 1.0)
                p["sh"] = self.vec(f"sh_{tag}")
                p["EA"] = self.dout(f"EA_{tag}", [48 * 128, self.ntok])
                p["EC"] = self.dout(f"EC_{tag}", [36 * 128, self.ntok], BF16)
                p["ES"] = self.dout(f"ES_{tag}", [36, self.ntok])
                p["EB"] = self.dout(f"EB_{tag}", [D_B, self.ntok], BF16)
                p["wsT"] = self.din(f"wsT_{tag}", [128, H_B, 128])
                p["bsr"] = self.din(f"bsr_{tag}", [128, H_B, 128])
                p["sg"] = self.din(f"sgg_{tag}", [128, D_B])
                p["sb"] = self.din(f"sgb_{tag}", [128, D_B])
            if kind in ("ffn", "mixout"):
                if si == last_x:
                    p["x_dst"] = self.dout("x_out", [D, self.ntok])
                else:
                    p["x_dst"] = self.dint(f"xmid_{si}", [D, self.ntok])
                cur = p["x_dst"]
            plans.append(p)
        self.plans = plans
        if any(p["kind"] == "mixin" for p in plans):
            self._alloc_b()
        for tt in range(self.ntt):
            for p in plans:
                getattr(self, "_st_" + p["kind"])(p, tt)
        k.wait_tiles(k.sp, self.xo.tiles + self.cst.tiles + self.tb.tiles + [self.big])

    def _st_ffn(self, p, tt):
        k = self.k
        big = self.bigv
        self.stage_mod(p["x_src"], tt, p["scp"], p["sh"])

        def epi(pi, banks):
            if banks is None:
                return 2
            g = self.tb.next()
            k.op(k.act, lambda e: e.activation(out=g[:, :], in_=banks[0][:, :], func=AF.Silu),
                 reads=[banks[0]], writes=[g])
            k.op(k.dve, lambda e: e.tensor_tensor(out=big[:, pi, :], in0=g[:, :], in1=banks[1][:, :], op=ALU.mult),
                 reads=[g, banks[1]], writes=[self.big])
            return 2

        self.linear_in(p["w1"], 64, epi)
        self.linear_out(p["w2"], 64, 1, lambda kc: big[:, kc, :], p["x_src"], p["x_dst"], tt,
                        p["gco"], p["lng"], p["lnb"])

    def _st_mixout(self, p, tt):
        k = self.k
        big = self.bigv
        mix = p["mix"]
        for q in range(4):
            k.dma(k.sp, big[:, q * 8:(q + 1) * 8, :],
                  mix[q * 1024:(q + 1) * 1024, tt * T:(tt + 1) * T].rearrange("(c p) t -> p c t", p=128),
                  self.big, writes=[self.big], join=(q > 0))
        self.linear_out(p["w2"], KC, 2, lambda kc: big[:, kc, :], p["x_src"], p["x_dst"], tt,
                        p["gco"], p["lng"], p["lnb"])

    def _alloc_b(self):
        k = self.k
        p = [q for q in self.plans if q["kind"] == "mixin"]
        b = self.big
        self.uT = b[:, 0:4096].rearrange("p (g t) -> p g t", t=T)
        self.vT = b[:, 4096:8192].rearrange("p (g t) -> p g t", t=T)
        self.vtok = b[:, 8192:9216]
        self.vn = b[:, 9216:9728].bitcast(BF16)
        self.obT = [b[:, 9728 + i * 512:9728 + (i + 1) * 512].bitcast(BF16).rearrange("p (g t) -> p g t", t=128)
                    for i in range(2)]
        self.obi = 0
        self.bst = k.sb([128, 2, 6], F32, "bst")
        self.bag = k.sb([128, 2], F32, "bag")
        self.brs = k.sb([128, 2], F32, "brs")
        self.mask_su = k.sb([128, 128], F32, "mask_su")
        self.mask_d = self.din("mask_su_in", [128, 128])
        k.dma(k.sp, self.mask_su[:, :], self.mask_d[:, :], self.mask_su, writes=[self.mask_su])
        self.bc = {}
        for q in p:
            tag = q["tag"]
            wsT = k.sb([128, H_B, 128], F32, f"wsT_{tag}")
            k.dma(k.sp, wsT[:, :, :], q["wsT"][:, :, :], wsT, writes=[wsT])
            wc = k.sb([128, H_B, 128], BF16, f"wc_{tag}")
            for g in range(H_B):
                k.op(k.dve, lambda e, g=g, wc=wc, wsT=wsT: e.tensor_tensor(out=wc[:, g, :], in0=wsT[:, g, :],
                                                                         in1=self.mask_su[:, :], op=ALU.mult),
                     reads=[wsT, self.mask_su], writes=[wc])
            bsr = k.sb([128, H_B, 128], F32, f"bsr_{tag}")
            k.dma(k.sp, bsr[:, :, :], q["bsr"][:, :, :], bsr, writes=[bsr])
            sg = k.sb([128, D_B], F32, f"sgg_{tag}")
            k.dma(k.sp, sg[:, :], q["sg"][:, :], sg, writes=[sg])
            sb_ = k.sb([128, D_B], F32, f"sgb_{tag}")
            k.dma(k.sp, sb_[:, :], q["sb"][:, :], sb_, writes=[sb_])
            self.bc[tag] = (wc, bsr, sg, sb_)

    def _st_mixin(self, p, tt):
        k = self.k
        nc = self.nc
        self.stage_mod(p["x_src"], tt, p["scp"], p["sh"])
        plan = mixin_block_plan()
        tsl = slice(tt * T, (tt + 1) * T)

        def epi(pi, banks):
            kinds = [plan[2 * pi], plan[2 * pi + 1]]
            nb = 1 if kinds[1][0] == "P" else 2
            if banks is None:
                return nb
            for j in range(nb):
                kind, idx, _ = kinds[j]
                ps = banks[j]
                if kind == "A":
                    xo = self.xo.next()
                    k.op(k.act, lambda e, xo=xo, ps=ps: e.copy(out=xo[:, :], in_=ps[:, :]), reads=[ps], writes=[xo])
                    k.dma(k.sp, p["EA"][idx * 128:(idx + 1) * 128, tsl], xo[:, :], xo, reads=[xo])
                elif kind == "C":
                    c = self.cst.next()
                    k.op(k.act, lambda e, c=c, ps=ps: e.copy(out=c[:, :], in_=ps[:, :]), reads=[ps], writes=[c])
                    k.dma(k.sp, p["EC"][idx * 128:(idx + 1) * 128, tsl], c[:, :], c, reads=[c])
                elif kind == "S":
                    xo = self.xo.next()
                    k.op(k.act, lambda e, xo=xo, ps=ps: e.copy(out=xo[:, :], in_=ps[:, :]), reads=[ps], writes=[xo])
                    k.dma(k.sp, p["ES"][:, tsl], xo[0:36, :], xo, reads=[xo])
                elif kind == "U":
                    k.op(k.act, lambda e, ps=ps, idx=idx: e.activation(out=self.uT[:, idx, :], in_=ps[:, :],
                                                                       func=AF.Gelu), reads=[ps], writes=[self.big])
                elif kind == "V":
                    k.op(k.act, lambda e, ps=ps, idx=idx: e.activation(out=self.vT[:, idx, :], in_=ps[:, :],
                                                                       func=AF.Gelu), reads=[ps], writes=[self.big])
            return nb

        self.linear_in(p["w1"], 51, epi)
        if getattr(self, "debug", False):
            if not hasattr(self, "dbg_u"):
                self.dbg_u = self.dout("dbg_u", [128, H_B * T])
                self.dbg_v = self.dout("dbg_v", [128, H_B * T])
                self.dbg_vn = self.dout("dbg_vn", [4, 128, D_B], BF16)
                self.dbg_vt = self.dout("dbg_vt", [4, 128, D_B])
            k.dma(k.sp, self.dbg_u[:, :], self.big[:, 0:4096], self.big, reads=[self.big])
            k.dma(k.sp, self.dbg_v[:, :], self.big[:, 4096:8192], self.big, reads=[self.big])
        wc, bsr, sg, sb_ = self.bc[p["tag"]]
        vtok, vn = self.vtok, self.vn
        for j in range(T // 128):
            js = slice(j * 128, (j + 1) * 128)
            pt = [self.P[4], self.P[5]]
            for g in range(H_B):
                k.op(k.pe, lambda e, g=g, js=js: e.transpose(pt[g // 4][:, (g % 4) * 128:(g % 4 + 1) * 128],
                                                             self.vT[:, g, js], self.ident[:, :]),
                     reads=[self.big, self.ident], writes=[pt[g // 4]], accum=(g % 4 > 0))
            for h in range(2):
                k.op(k.act, lambda e, h=h: e.copy(out=vtok[:, h * 512:(h + 1) * 512], in_=pt[h][:, :]),
                     reads=[pt[h]], writes=[self.big])
            if getattr(self, "debug", False):
                k.dma(k.sp, self.dbg_vt[j, :, :], vtok, self.big, reads=[self.big])
            bst, bag, brs = self.bst, self.bag, self.brs
            for h in range(2):
                sqh = self.tb.next()
                k.op(k.act, lambda e, h=h, sqh=sqh: e.activation(out=sqh[:, :], in_=vtok[:, h * 512:(h + 1) * 512],
                                                                func=AF.Square), reads=[self.big], writes=[sqh])
                k.op(k.dve, lambda e, h=h, sqh=sqh: e.reduce_sum(out=bst[:, 0, 2 + h:3 + h], in_=sqh[:, :],
                                                                axis=mybir.AxisListType.X), reads=[sqh], writes=[bst])
                k.op(k.dve, lambda e, h=h: e.reduce_sum(out=bst[:, 0, h:h + 1], in_=vtok[:, h * 512:(h + 1) * 512],
                                                        axis=mybir.AxisListType.X), reads=[self.big], writes=[bst])
            k.op(k.dve, lambda e: e.tensor_scalar(out=bag[:, 0:1], in0=bst[:, 0, 0:1], scalar1=bst[:, 0, 1:2],
                                                  scalar2=1.0 / D_B, op0=ALU.add, op1=ALU.mult),
                 reads=[bst], writes=[bag])
            k.op(k.dve, lambda e: e.tensor_scalar(out=bag[:, 1:2], in0=bst[:, 0, 2:3], scalar1=bst[:, 0, 3:4],
                                                  scalar2=1.0 / D_B, op0=ALU.add, op1=ALU.mult),
                 reads=[bst], writes=[bag])
            k.op(k.dve, lambda e: e.tensor_tensor(out=brs[:, 1:2], in0=bag[:, 0:1], in1=bag[:, 0:1], op=ALU.mult),
                 reads=[bag], writes=[brs])
            k.op(k.dve, lambda e: e.tensor_tensor(out=bag[:, 1:2], in0=bag[:, 1:2], in1=brs[:, 1:2], op=ALU.subtract),
                 reads=[bag, brs], writes=[bag])
            k.op(k.act, lambda e: e.activation(out=brs[:, 0:1], in_=bag[:, 1:2], func=AF.Sqrt, bias=self.eps_ln[:, 0:1],
                                               scale=1.0), reads=[bag, self.eps_ln], writes=[brs])
            k.op(k.dve, lambda e: e.reciprocal(out=brs[:, 0:1], in_=brs[:, 0:1]), reads=[brs], writes=[brs])
            k.op(k.dve, lambda e: e.scalar_tensor_tensor(out=brs[:, 1:2], in0=bag[:, 0:1], scalar=-1.0, in1=brs[:, 0:1],
                                                        op0=ALU.mult, op1=ALU.mult), reads=[bag, brs], writes=[brs])
            k.op(k.act, lambda e: e.activation(out=vtok[:, :], in_=vtok[:, :], func=AF.Identity,
                                               bias=brs[:, 1:2], scale=brs[:, 0:1]), reads=[self.big, brs], writes=[self.big])
            k.op(k.dve, lambda e: e.tensor_tensor(out=vtok[:, :], in0=vtok[:, :], in1=sg[:, :], op=ALU.mult),
                 reads=[self.big, sg], writes=[self.big])
            k.op(k.dve, lambda e: e.tensor_tensor(out=vn[:, :], in0=vtok[:, :], in1=sb_[:, :], op=ALU.add),
                 reads=[self.big, sb_], writes=[self.big])
            if getattr(self, "debug", False):
                k.dma(k.sp, self.dbg_vn[j, :, :], vn, self.big, reads=[self.big])
            ob = self.obT[self.obi % 2]
            self.obi += 1
            pm = [self.P[6], self.P[7]]
            with nc.allow_low_precision("bf16 matmul"):
                for g in range(H_B):
                    k.op(k.pe, lambda e, g=g: e.matmul(pm[g // 4][:, (g % 4) * 128:(g % 4 + 1) * 128],
                                                       lhsT=vn[:, g * 128:(g + 1) * 128], rhs=wc[:, g, :],
                                                       start=True, stop=True),
                         reads=[self.big, wc], writes=[pm[g // 4]], accum=(g % 4 > 0))
            for h in range(2):
                tb = self.tb.next()
                if getattr(self, "debug", False):
                    if not hasattr(self, "dbg_pm"):
                        self.dbg_pm = self.dout("dbg_pm", [4, 2, 128, 512])
                        self.dbg_wc = self.dout("dbg_wc", [128, H_B * 128], BF16)
                        k.dma(k.sp, self.dbg_wc[:, :], wc[:, :, :].rearrange("p g t -> p (g t)"), wc, reads=[wc])
                    xo = self.xo.next()
                    k.op(k.act, lambda e, h=h, xo=xo: e.copy(out=xo[:, :], in_=pm[h][:, :]), reads=[pm[h]], writes=[xo])
                    k.dma(k.sp, self.dbg_pm[j, h, :, :], xo[:, :], xo, reads=[xo])
                k.op(k.dve, lambda e, h=h, tb=tb: e.tensor_tensor(
                    out=tb[:, :], in0=pm[h][:, :], in1=bsr[:, h * 4:(h + 1) * 4, :].rearrange("p g t -> p (g t)"),
                    op=ALU.add), reads=[pm[h], bsr], writes=[tb])
                k.op(k.dve, lambda e, h=h, tb=tb, ob=ob, js=js: e.tensor_tensor(
                    out=ob[:, h * 4:(h + 1) * 4, :], in0=tb[:, :].rearrange("p (g t) -> p g t", g=4),
                    in1=self.uT[:, h * 4:(h + 1) * 4, js], op=ALU.mult), reads=[tb, self.big], writes=[self.big])
            k.dma(k.sp, p["EB"][:, tt * T + j * 128: tt * T + (j + 1) * 128].rearrange("(g c) t -> c g t", c=128),
                  ob[:, :, :], self.big, reads=[self.big])


def prep_ffn_in(W):
    return np.ascontiguousarray(W.reshape(32, 128, 2, 64, 128).transpose(3, 1, 0, 2, 4)).reshape(64, 128, 8192)


def prep_ffn_out(W):
    return np.ascontiguousarray(W.reshape(64, 128, 32, 128).transpose(2, 1, 0, 3)).reshape(32, 128, 8192)


def prep_wo(W):
    return np.ascontiguousarray(W.reshape(32, 128, 16, 2, 128).transpose(2, 1, 3, 0, 4)).reshape(16, 128, 8192)


def prep_win(W):
    plan = mixin_block_plan()
    cols = np.stack([np.concatenate([plan[2 * i][2], plan[2 * i + 1][2]]) for i in range(51)])
    return blocks_in(W, cols)


def b_consts(tag, w_s, b_s, g, b):
    return {
        f"wsT_{tag}": np.ascontiguousarray(w_s.transpose(2, 0, 1)).astype(np.float32),
        f"bsr_{tag}": np.ascontiguousarray(np.broadcast_to(b_s[None], (128, H_B, 128))).astype(np.float32),
        f"sgg_{tag}": np.ascontiguousarray(np.broadcast_to(g[None], (128, D_B))).astype(np.float32),
        f"sgb_{tag}": np.ascontiguousarray(np.broadcast_to(b[None], (128, D_B))).astype(np.float32),
        "mask_su_in": np.triu(np.ones((128, 128), np.float32)),
    }


class HeadProg:
    def __init__(self, na=3, ncu=3, seq=S, debug=False):
        self.debug = debug
        self.na, self.ncu, self.seq = na, ncu, seq
        nc = bass.Bass("TRN2", target_bir_lowering=False)
        self.nc = nc
        self.in_names, self.out_names = [], []
        with ExitStack() as st:
            self.k = KB(nc, st)
            self._common()
            if na:
                self.aoff = 0
                self._alloc_a()
                self._phase_a()
                self.k.barrier()
            if ncu:
                self.aoff = 0
                self._alloc_c()
                for u in range(ncu):
                    self._unit_c(u)
            self.k.wait_tiles(self.k.sp, self.out_tiles)
            self.k.finish()

    def din(self, name, shape, dt=F32):
        self.in_names.append(name)
        return self.nc.dram_tensor(name, list(shape), dt, kind="ExternalInput").ap()

    def dout(self, name, shape, dt=F32):
        self.out_names.append(name)
        return self.nc.dram_tensor(name, list(shape), dt, kind="ExternalOutput").ap()

    def carve(self, shape, dt, name):
        n = 1
        for d_ in shape[1:]:
            n *= d_
        nf = n if dt == F32 else (n + 1) // 2
        ap = self.arena[:, self.aoff:self.aoff + nf]
        self.aoff += nf
        assert self.aoff <= self.arena_n, (name, self.aoff)
        if dt != F32:
            ap = ap.bitcast(dt)
        if len(shape) == 3:
            ap = ap.rearrange("p (a b) -> p a b", b=shape[2])
        self.k.nt += 1
        return Tl(ap, f"{name}_{self.k.nt}")

    def dbg(self, name, tl, ap, shape, dt=F32):
        if not getattr(self, "debug", False):
            return
        if name in self.in_names or name in self.out_names:
            return
        d = self.dout(name, shape, dt)
        self.k.dma(self.k.sp, d, ap, tl, reads=[tl])
        self.out_tiles.append(tl)

    def _common(self):
        k = self.k
        self.out_tiles = []
        self.arena_n = 51 * 1024
        self.arena = self.k.sb([128, self.arena_n], F32, "arena")
        self.P = [k.ps([128, 512], F32, f"P{i}") for i in range(8)]
        self.ident = k.sb([128, 128], F32, "ident")
        self.identb = k.sb([128, 128], BF16, "identb")
        self.ones = k.sb([128, 128], F32, "ones")
        self.onesb = k.sb([128, 128], BF16, "onesb")
        self.mask_su = k.sb([128, 128], F32, "mask_su")
        self.mask_sub = k.sb([128, 128], BF16, "mask_sub")
        self.c_one = k.sb([128, 1], F32, "c_one")
        self.c_eps = k.sb([128, 1], F32, "c_eps")
        idd = self.din("ident_in", [128, 128])
        md = self.din("mask_su_in", [128, 128])
        k.dma(k.sp, self.ident[:, :], idd[:, :], self.ident, writes=[self.ident])
        k.dma(k.sp, self.mask_su[:, :], md[:, :], self.mask_su, writes=[self.mask_su])
        k.op(k.dve, lambda e: e.tensor_copy(out=self.identb[:, :], in_=self.ident[:, :]), reads=[self.ident],
             writes=[self.identb])
        k.op(k.dve, lambda e: e.tensor_copy(out=self.mask_sub[:, :], in_=self.mask_su[:, :]), reads=[self.mask_su],
             writes=[self.mask_sub])
        k.op(k.dve, lambda e: e.memset(self.ones[:, :], 1.0), writes=[self.ones])
        k.op(k.dve, lambda e: e.memset(self.onesb[:, :], 1.0), writes=[self.onesb])
        k.op(k.dve, lambda e: e.memset(self.c_one[:, :], 1.0), writes=[self.c_one])
        k.op(k.dve, lambda e: e.memset(self.c_eps[:, :], float(RMS_EPS)), writes=[self.c_eps])

    def _alloc_c(self):
        k = self.k
        sq = self.seq
        self.c_q = self.carve([128, sq], BF16, "c_q")
        self.c_k = self.carve([128, sq], BF16, "c_k")
        self.c_vt = self.carve([128, sq // 128, 128], BF16, "c_vt")
        self.c_f = self.carve([128, sq], F32, "c_f")
        self.c_nc = self.carve([128, sq], F32, "c_nc")
        self.c_ncc = self.carve([128, sq // 128], F32, "c_ncc")
        self.c_par = self.carve([128, 2], F32, "c_par")
        self.c_tmp = Ring(k, 4, [128, 512], F32, "c_tmp", self.carve)
        self.c_p = Ring(k, 5, [128, 512], BF16, "c_p", self.carve)
        self.c_on = self.carve([128, 512], F32, "c_on")
        self.c_rl = self.carve([128, 512], F32, "c_rl")
        self.c_sq = self.carve([128, 512], F32, "c_sq")
        self.c_ob = Ring(k, 2, [128, 512], BF16, "c_ob", self.carve)
        self.out_tiles += self.c_ob.tiles

    def _unit_c(self, u):
        k = self.k
        nc = self.nc
        sq = self.seq
        nb = sq // 128
        nI = sq // 512
        qk_d = self.din(f"c_qk_{u}", [2, 128, sq], BF16)
        v_d = self.din(f"c_v_{u}", [sq, 128], BF16)
        f_d = self.din(f"c_f_{u}", [1, sq])
        par_d = self.din(f"c_par_{u}", [128, 2])
        o_d = self.dout(f"c_o_{u}", [128, sq], BF16)
        cq, ck, cvt, cf, cnc, cncc, cpar = (self.c_q, self.c_k, self.c_vt, self.c_f, self.c_nc,
                                            self.c_ncc, self.c_par)
        k.dma(k.sp, cq[:, :], qk_d[0, :, :], cq, writes=[cq])
        k.dma(k.sp, ck[:, :], qk_d[1, :, :], ck, writes=[ck])
        k.dma(k.sp, cvt[:, :, :], v_d[:, :].rearrange("(b p) e -> p b e", p=128), cvt, writes=[cvt])
        k.dma(k.sp, cf[:, :], f_d[0:1, :].partition_broadcast(128), cf, writes=[cf])
        k.dma(k.sp, cpar[:, :], par_d[:, :], cpar, writes=[cpar])
        k.op(k.dve, lambda e: e.tensor_scalar(out=cpar[:, 0:1], in0=cpar[:, 0:1], scalar1=-1.0, scalar2=None,
                                              op0=ALU.mult), reads=[cpar], writes=[cpar])
        k.op(k.act, lambda e: e.activation(out=cf[:, :], in_=cf[:, :], func=AF.Exp, bias=cpar[:, 0:1], scale=-1.0),
             reads=[cf, cpar], writes=[cf])
        k.op(k.act, lambda e: e.activation(out=cf[:, :], in_=cf[:, :], func=AF.Ln, bias=self.c_one[:, 0:1], scale=1.0),
             reads=[cf, self.c_one], writes=[cf])
        k.op(k.dve, lambda e: e.tensor_tensor_scan(out=cnc[:, :], data0=self.c_one[:, 0:1].to_broadcast([128, sq]),
                                                  data1=cf[:, :], initial=0.0, op0=ALU.mult, op1=ALU.add),
             reads=[cf, self.c_one], writes=[cnc])
        ptr = self.P[6]
        for g4 in range(nb // 4):
            for i in range(4):
                j = g4 * 4 + i
                k.op(k.pe, lambda e, i=i, j=j: e.transpose(ptr[:, i * 128:(i + 1) * 128], cnc[:, j * 128:(j + 1) * 128],
                                                           self.ident[:, :]),
                     reads=[cnc, self.ident], writes=[ptr], accum=(i > 0))
            k.op(k.dve, lambda e, g4=g4: e.tensor_copy(
                out=cncc[:, g4 * 4:(g4 + 1) * 4], in_=ptr[:, :].rearrange("p (a b) -> p a b", b=128)[:, :, 0]),
                reads=[ptr], writes=[cncc])
        scale = float(HD ** -0.5)
        oacc, lacc, pss = self.P[4], self.P[5], self.P[7]
        scb = [self.P[0], self.P[1], self.P[2], self.P[3]]
        for I in range(nI):
            jobs = list(range(4 * I + 4))
            pend = []

            def qk(j, I=I):
                t0 = max(512 * I, 128 * j)
                off = t0 - 512 * I
                ps = scb[j % 4]
                with nc.allow_low_precision("bf16 matmul"):
                    k.op(k.pe, lambda e: e.matmul(ps[:, off:512], lhsT=ck[:, j * 128:(j + 1) * 128],
                                                  rhs=cq[:, t0:512 * (I + 1)], start=True, stop=True),
                         reads=[ck, cq], writes=[ps])
                tmp = self.c_tmp.next()
                k.op(k.dve, lambda e: e.scalar_tensor_tensor(out=tmp[:, off:512], in0=ps[:, off:512], scalar=scale,
                                                            in1=cnc[:, t0:512 * (I + 1)], op0=ALU.mult,
                                                            op1=ALU.subtract), reads=[ps, cnc], writes=[tmp])
                pt = self.c_p.next()
                k.op(k.act, lambda e: e.activation(out=pt[:, off:512], in_=tmp[:, off:512], func=AF.Exp,
                                                   bias=cncc[:, j:j + 1], scale=1.0), reads=[tmp, cncc], writes=[pt])
                if 128 * j >= 512 * I:
                    k.op(k.pool, lambda e: e.tensor_tensor(out=pt[:, off:off + 128], in0=pt[:, off:off + 128],
                                                           in1=self.mask_sub[:, :], op=ALU.mult),
                         reads=[pt, self.mask_sub], writes=[pt])
                return (j, off, pt)

            def pv(j, off, pt, I=I):
                last = 4 * I + 3
                with nc.allow_low_precision("bf16 matmul"):
                    k.op(k.pe, lambda e: e.matmul(oacc[:, off:512], lhsT=cvt[:, j, :], rhs=pt[:, off:512],
                                                  start=(j == 0), stop=(j == last)),
                         reads=[cvt, pt], writes=[oacc], accum=(j > 0))
                    k.op(k.pe, lambda e: e.matmul(lacc[:, off:512], lhsT=self.onesb[:, :], rhs=pt[:, off:512],
                                                  start=(j == 0), stop=(j == last)),
                         reads=[self.onesb, pt], writes=[lacc], accum=(j > 0))

            for j in jobs:
                pend.append(qk(j))
                if len(pend) > 3:
                    pv(*pend.pop(0))
            while pend:
                pv(*pend.pop(0))
            con, crl, csq = self.c_on, self.c_rl, self.c_sq
            k.op(k.dve, lambda e: e.reciprocal(out=crl[:, :], in_=lacc[:, :]), reads=[lacc], writes=[crl])
            k.op(k.dve, lambda e: e.tensor_tensor(out=con[:, :], in0=oacc[:, :], in1=crl[:, :], op=ALU.mult),
                 reads=[oacc, crl], writes=[con])
            k.op(k.act, lambda e: e.activation(out=csq[:, :], in_=con[:, :], func=AF.Square), reads=[con], writes=[csq])
            k.op(k.pe, lambda e: e.matmul(pss[:, :], lhsT=self.ones[:, :], rhs=csq[:, :], start=True, stop=True),
                 reads=[self.ones, csq], writes=[pss])
            k.op(k.act, lambda e: e.activation(out=crl[:, :], in_=pss[:, :], func=AF.Sqrt, bias=self.c_eps[:, 0:1],
                                               scale=1.0 / HD), reads=[pss, self.c_eps], writes=[crl])
            k.op(k.dve, lambda e: e.reciprocal(out=crl[:, :], in_=crl[:, :]), reads=[crl], writes=[crl])
            k.op(k.dve, lambda e: e.tensor_tensor(out=con[:, :], in0=con[:, :], in1=crl[:, :], op=ALU.mult),
                 reads=[con, crl], writes=[con])
            ob = self.c_ob.next()
            k.op(k.act, lambda e, ob=ob: e.activation(out=ob[:, :], in_=con[:, :], func=AF.Copy, scale=cpar[:, 1:2]),
                 reads=[con, cpar], writes=[ob])
            k.dma(k.sp, o_d[:, 512 * I:512 * (I + 1)], ob[:, :], ob, reads=[ob])

    def _alloc_a(self):
        k = self.k
        cv = self.carve
        SG = self.SG = min(2048, self.seq)
        NB = self.NB = SG // 128
        self.a_xin = Ring(k, 2, [128, SG + 4], F32, "a_xin", cv)
        self.a_qs = cv([128, SG], F32, "a_qs")
        self.a_ks = cv([128, SG], F32, "a_ks")
        self.a_vs = cv([128, SG], F32, "a_vs")
        self.a_sq = cv([128, SG], F32, "a_sq")
        self.a_rn = cv([128, SG], F32, "a_rn")
        self.a_br = cv([128, SG], F32, "a_br")
        self.a_Gr = cv([128, SG], F32, "a_Gr")
        self.a_m64 = cv([128, SG], F32, "a_m64")
        self.a_F = [dict(gr=cv([128, SG], F32, f"a_gr{q}"), qg=cv([128, SG], BF16, f"a_qg{q}"),
                         wT=cv([128, SG], BF16, f"a_wT{q}"), u=cv([128, NB, 128], F32, f"a_u{q}"),
                         qkT=cv([128, NB, 128], BF16, f"a_qkT{q}"), kd=cv([128, NB, 128], BF16, f"a_kd{q}"))
                    for q in range(2)]
        self.a_o = cv([128, NB, 128], F32, "a_o")
        self.a_z = cv([128, NB, 128], F32, "a_z")
        self.a_ob = Ring(k, 2, [128, NB, 128], BF16, "a_ob", cv)
        self.out_tiles += self.a_ob.tiles
        self.a_col = cv([128, 2, NB], F32, "a_col")
        self.a_cg = cv([128, NB], F32, "a_cg")
        self.a_cG = cv([128, NB], F32, "a_cG")
        self.a_cGl = cv([128, NB], F32, "a_cGl")
        self.a_cbg = cv([128, NB], F32, "a_cbg")
        self.a_cdl = cv([128, NB], F32, "a_cdl")
        self.a_ss = cv([128, NB], F32, "a_ss")
        self.a_par = cv([128, 16], F32, "a_par")
        self.a_nea = cv([128, 1], F32, "a_nea")
        self.a_na = cv([128, 128], F32, "a_na")
        sm = lambda n: cv([128, 128], F32, n)
        self.a_sets = []
        for q in range(2):
            self.a_sets.append(dict(dm=sm(f"a_dm{q}"), E=sm(f"a_E{q}"), U0=sm(f"a_U0{q}"),
                                    X=[sm(f"a_X0{q}"), sm(f"a_X1{q}")], XT=[sm(f"a_XT0{q}"), sm(f"a_XT1{q}")],
                                    R=sm(f"a_R{q}"), kbg=sm(f"a_kbg{q}"), vb=sm(f"a_vb{q}"),
                                    P=[self.P[3 * q + i] for i in range(3)]))
        self.a_S = sm("a_S")
        self.a_Sb = cv([128, 128], BF16, "a_Sb")
        self.a_vn = Ring(k, 2, [128, 128], BF16, "a_vn", cv)
        self.maskU = sm("maskU")
        self.maskS = sm("maskS")
        self.triC = sm("triC")
        self.sameC = sm("sameC")
        mu_d = self.din("maskU_in", [128, 128])
        ms_d = self.din("maskS_in", [128, 128])
        sc_d = self.din("sameC_in", [128, 128])
        k.dma(k.sp, self.maskU[:, :], mu_d[:, :], self.maskU, writes=[self.maskU])
        k.dma(k.sp, self.maskS[:, :], ms_d[:, :], self.maskS, writes=[self.maskS])
        k.dma(k.sp, self.sameC[:, :], sc_d[:, :], self.sameC, writes=[self.sameC])
        m64 = self.a_m64
        k.op(k.dve, lambda e: e.memset(m64[:, :], 1.0), writes=[m64])
        k.op(k.dve, lambda e: e.memset(m64[:, :].rearrange("p (a b) -> p a b", b=64)[:, :, 0:1], 0.0), writes=[m64])

    def _a_front(self, u, sg, F):
        k = self.k
        nc = self.nc
        sq = self.seq
        SG, NB = self.SG, self.NB
        if sg == 0:
            self.a_dram[u] = dict(
                qkv=self.din(f"a_qkv_{u}", [3, 128, sq]), z=self.din(f"a_z_{u}", [sq, 128]),
                bd=self.din(f"a_bd_{u}", [2, sq]), bdc=self.din(f"a_bdc_{u}", [128, 2, sq // 128]),
                par=self.din(f"a_par_{u}", [128, 16]), na=self.din(f"a_na_{u}", [128, 128]),
                o=self.dout(f"a_o_{u}", [sq, 128], BF16))
        dd = self.a_dram[u]
        qkv_d, bd_d, bdc_d, par_d = dd["qkv"], dd["bd"], dd["bdc"], dd["par"]
        par, nea = self.a_par, self.a_nea
        ident, ones = self.ident, self.ones
        P = self.P
        if sg == 0:
            k.dma(k.sp, par[:, :], par_d[:, :], par, writes=[par])
            k.op(k.act, lambda e: e.activation(out=nea[:, :], in_=par[:, 12:13], func=AF.Exp), reads=[par], writes=[nea])
            k.op(k.dve, lambda e: e.tensor_scalar(out=nea[:, :], in0=nea[:, :], scalar1=-1.0, scalar2=None, op0=ALU.mult),
                 reads=[nea], writes=[nea])
        s0 = sg * SG
        outs = [self.a_qs, self.a_ks, self.a_vs]
        for i in range(3):
            xin = self.a_xin.next()
            if sg == 0:
                k.op(k.dve, lambda e, xin=xin: e.memset(xin[:, 0:3], 0.0), writes=[xin])
                k.dma(k.sp, xin[:, 3:3 + SG], qkv_d[i, :, 0:SG], xin, writes=[xin], join=False)
            else:
                k.dma(k.sp, xin[:, 0:3 + SG], qkv_d[i, :, s0 - 3:s0 + SG], xin, writes=[xin])
            o = outs[i]
            eng = k.dve
            k.op(eng, lambda e, xin=xin, o=o, i=i: e.tensor_scalar(out=o[:, :], in0=xin[:, 3:3 + SG],
                                                                  scalar1=par[:, 4 * i + 3:4 * i + 4], scalar2=None,
                                                                  op0=ALU.mult), reads=[xin, par], writes=[o])
            for j in range(3):
                k.op(eng, lambda e, xin=xin, o=o, i=i, j=j: e.scalar_tensor_tensor(
                    out=o[:, :], in0=xin[:, j:j + SG], scalar=par[:, 4 * i + j:4 * i + j + 1], in1=o[:, :],
                    op0=ALU.mult, op1=ALU.add), reads=[xin, par, o], writes=[o])
            k.op(k.act, lambda e, o=o: e.activation(out=o[:, :], in_=o[:, :], func=AF.Silu), reads=[o], writes=[o])
            yield
        qs, ks, vs, sqb, rn = self.a_qs, self.a_ks, self.a_vs, self.a_sq, self.a_rn
        for i, t_ in enumerate((qs, ks)):
            k.op(k.act, lambda e, t_=t_: e.activation(out=sqb[:, :], in_=t_[:, :], func=AF.Square),
                 reads=[t_], writes=[sqb])
            for c in range(SG // 512):
                ps = P[c % 2]
                k.op(k.pe, lambda e, c=c, ps=ps: e.matmul(ps[:, :], lhsT=ones[:, :], rhs=sqb[:, c * 512:(c + 1) * 512],
                                                          start=True, stop=True), reads=[ones, sqb], writes=[ps])
                k.op(k.act, lambda e, c=c, ps=ps: e.activation(out=rn[:, c * 512:(c + 1) * 512], in_=ps[:, :],
                                                               func=AF.Sqrt, bias=self.c_eps[:, 0:1], scale=1.0),
                     reads=[ps, self.c_eps], writes=[rn])
            yield
            k.op(k.dve, lambda e: e.reciprocal(out=rn[:, :], in_=rn[:, :]), reads=[rn], writes=[rn])
            if i == 0:
                k.op(k.dve, lambda e: e.scalar_tensor_tensor(out=qs[:, :], in0=qs[:, :], scalar=float(HD ** -0.5),
                                                            in1=rn[:, :], op0=ALU.mult, op1=ALU.mult),
                     reads=[qs, rn], writes=[qs])
            else:
                k.op(k.dve, lambda e: e.tensor_tensor(out=ks[:, :], in0=ks[:, :], in1=rn[:, :], op=ALU.mult),
                     reads=[ks, rn], writes=[ks])
        self.dbg("d_qs", qs, qs[:, :], [128, SG])
        self.dbg("d_ks", ks, ks[:, :], [128, SG])
        self.dbg("d_vs", vs, vs[:, :], [128, SG])
        br, gr, Gr = self.a_br, F['gr'], self.a_Gr
        k.dma(k.sp, br[:, :], bd_d[0:1, s0:s0 + SG].partition_broadcast(128), br, writes=[br])
        k.dma(k.sp, gr[:, :], bd_d[1:2, s0:s0 + SG].partition_broadcast(128), gr, writes=[gr])
        k.op(k.act, lambda e: e.activation(out=br[:, :], in_=br[:, :], func=AF.Sigmoid), reads=[br], writes=[br])
        k.op(k.act, lambda e: e.activation(out=gr[:, :], in_=gr[:, :], func=AF.Exp, bias=par[:, 13:14], scale=1.0),
             reads=[gr, par], writes=[gr])
        k.op(k.act, lambda e: e.activation(out=gr[:, :], in_=gr[:, :], func=AF.Ln, bias=self.c_one[:, 0:1], scale=1.0),
             reads=[gr, self.c_one], writes=[gr])
        k.op(k.dve, lambda e: e.tensor_scalar(out=gr[:, :], in0=gr[:, :], scalar1=nea[:, 0:1], scalar2=None,
                                              op0=ALU.mult), reads=[gr, nea], writes=[gr])
        k.op(k.dve, lambda e: e.tensor_tensor_scan(out=Gr[:, :], data0=self.a_m64[:, :], data1=gr[:, :], initial=0.0,
                                                  op0=ALU.mult, op1=ALU.add), reads=[gr, self.a_m64], writes=[Gr])
        k.op(k.act, lambda e: e.activation(out=gr[:, :], in_=Gr[:, :], func=AF.Exp), reads=[Gr], writes=[gr])
        qg = F['qg']
        k.op(k.dve, lambda e: e.tensor_tensor(out=qg[:, :], in0=qs[:, :], in1=gr[:, :], op=ALU.mult),
             reads=[qs, gr], writes=[qg])
        yield
        col, cg, cG, cGl, cbg, cdl = self.a_col, self.a_cg, self.a_cG, self.a_cGl, self.a_cbg, self.a_cdl
        k.dma(k.sp, col[:, :, :], bdc_d[:, :, sg * NB:(sg + 1) * NB], col, writes=[col])
        k.op(k.act, lambda e: e.activation(out=col[:, 0, :], in_=col[:, 0, :], func=AF.Sigmoid), reads=[col], writes=[col])
        k.op(k.act, lambda e: e.activation(out=cg[:, :], in_=col[:, 1, :], func=AF.Exp, bias=par[:, 13:14], scale=1.0),
             reads=[col, par], writes=[cg])
        k.op(k.act, lambda e: e.activation(out=cg[:, :], in_=cg[:, :], func=AF.Ln, bias=self.c_one[:, 0:1], scale=1.0),
             reads=[cg, self.c_one], writes=[cg])
        k.op(k.dve, lambda e: e.tensor_scalar(out=cg[:, :], in0=cg[:, :], scalar1=nea[:, 0:1], scalar2=None,
                                              op0=ALU.mult), reads=[cg, nea], writes=[cg])
        k.op(k.pe, lambda e: e.matmul(P[0][:, 0:NB], lhsT=self.maskU[:, :], rhs=cg[:, :], start=True, stop=True),
             reads=[self.maskU, cg], writes=[P[0]])
        k.op(k.pe, lambda e: e.matmul(P[1][:, 0:NB], lhsT=self.sameC[:, :], rhs=cg[:, :], start=True, stop=True),
             reads=[self.sameC, cg], writes=[P[1]])
        k.op(k.dve, lambda e: e.tensor_copy(out=cG[:, :], in_=P[0][:, 0:NB]), reads=[P[0]], writes=[cG])
        k.op(k.dve, lambda e: e.tensor_tensor(out=cdl[:, :], in0=P[1][:, 0:NB], in1=cG[:, :], op=ALU.subtract),
             reads=[P[1], cG], writes=[cdl])
        k.op(k.act, lambda e: e.activation(out=cdl[:, :], in_=cdl[:, :], func=AF.Exp), reads=[cdl], writes=[cdl])
        k.op(k.act, lambda e: e.activation(out=cbg[:, :], in_=cG[:, :], func=AF.Exp), reads=[cG], writes=[cbg])
        k.op(k.dve, lambda e: e.tensor_tensor(out=cbg[:, :], in0=cbg[:, :], in1=col[:, 0, :], op=ALU.mult),
             reads=[cbg, col], writes=[cbg])
        self.dbg("d_br", br, br[:, :], [128, SG])
        self.dbg("d_Gr", Gr, Gr[:, :], [128, SG])
        self.dbg("d_eG", gr, gr[:, :], [128, SG])
        self.dbg("d_cG", cG, cG[:, :], [128, NB])
        self.dbg("d_cdl", cdl, cdl[:, :], [128, NB])
        self.dbg("d_cbg", cbg, cbg[:, :], [128, NB])
        yield
        def blk(b, st):
            dm, E, U0, R, kbg, vb, Pq = st["dm"], st["E"], st["U0"], st["R"], st["kbg"], st["vb"], st["P"]
            bs = slice(b * 128, (b + 1) * 128)
            k.op(k.pe, lambda e: e.matmul(Pq[0][:, 0:128], lhsT=ks[:, bs], rhs=ks[:, bs], start=True, stop=True),
                 reads=[ks], writes=[Pq[0]])
            k.op(k.pe, lambda e: e.matmul(Pq[1][:, 0:128], lhsT=ks[:, bs], rhs=qs[:, bs], start=True, stop=True),
                 reads=[ks, qs], writes=[Pq[1]])
            k.op(k.dve, lambda e: e.tensor_scalar(out=dm[:, :], in0=Gr[:, bs], scalar1=cG[:, b:b + 1],
                                                  scalar2=0.0, op0=ALU.subtract, op1=ALU.min),
                 reads=[Gr, cG], writes=[dm])
            k.op(k.act, lambda e: e.activation(out=E[:, :], in_=dm[:, :], func=AF.Exp), reads=[dm], writes=[E])
            yield
            k.op(k.dve, lambda e: e.tensor_tensor(out=E[:, :], in0=E[:, :], in1=self.maskU[:, :], op=ALU.mult),
                 reads=[E, self.maskU], writes=[E])
            k.op(k.dve, lambda e: e.tensor_tensor(out=F['qkT'][:, b, :], in0=Pq[1][:, 0:128], in1=E[:, :],
                                                  op=ALU.mult), reads=[Pq[1], E], writes=[F['qkT']])
            k.op(k.dve, lambda e: e.tensor_tensor(out=U0[:, :], in0=Pq[0][:, 0:128], in1=E[:, :], op=ALU.mult),
                 reads=[Pq[0], E], writes=[U0])
            yield
            k.op(k.pool, lambda e: e.tensor_tensor(out=U0[:, :], in0=U0[:, :], in1=self.maskS[:, :], op=ALU.mult),
                 reads=[U0, self.maskS], writes=[U0])
            X0, XT0 = st["X"][0], st["XT"][0]
            k.op(k.pool, lambda e: e.tensor_tensor(out=X0[:, :], in0=U0[:, :], in1=br[:, bs], op=ALU.mult),
                 reads=[U0, br], writes=[X0])
            yield
            k.op(k.pe, lambda e: e.transpose(Pq[2][:, 0:128], X0[:, :], ident[:, :]), reads=[X0, ident],
                 writes=[Pq[2]])
            k.op(k.act, lambda e: e.copy(out=XT0[:, :], in_=Pq[2][:, 0:128]), reads=[Pq[2]], writes=[XT0])
            k.op(k.dve, lambda e: e.tensor_tensor(out=R[:, :], in0=ident[:, :], in1=X0[:, :], op=ALU.subtract),
                 reads=[ident, X0], writes=[R])
            yield
            X, XT = X0, XT0
            for lvl in range(1, 6):
                Xn, XTn = st["X"][lvl % 2], st["XT"][lvl % 2]
                if lvl < 5:
                    k.op(k.pe, lambda e, X=X, XT=XT: e.matmul(Pq[0][:, 0:128], lhsT=XT[:, :], rhs=X[:, :], start=True,
                                                              stop=True), reads=[X, XT], writes=[Pq[0]])
                k.op(k.pe, lambda e, X=X, XT=XT: e.matmul(Pq[1][:, 0:128], lhsT=X[:, :], rhs=XT[:, :], start=True,
                                                          stop=True), reads=[X, XT], writes=[Pq[1]])
                yield
                if lvl < 5:
                    k.op(k.act, lambda e, Xn=Xn: e.copy(out=Xn[:, :], in_=Pq[0][:, 0:128]), reads=[Pq[0]], writes=[Xn])
                k.op(k.dve, lambda e, XTn=XTn: e.tensor_copy(out=XTn[:, :], in_=Pq[1][:, 0:128]), reads=[Pq[1]],
                     writes=[XTn])
                yield
                k.op(k.pe, lambda e, XTn=XTn: e.matmul(Pq[2][:, 0:128], lhsT=XTn[:, :], rhs=R[:, :], start=True,
                                                       stop=True), reads=[XTn, R], writes=[Pq[2]])
                yield
                k.op(k.dve, lambda e: e.tensor_tensor(out=R[:, :], in0=R[:, :], in1=Pq[2][:, 0:128], op=ALU.add),
                     reads=[R, Pq[2]], writes=[R])
                yield
                X, XT = Xn, XTn
            k.op(k.pe, lambda e: e.transpose(Pq[2][:, 0:128], ks[:, bs], ident[:, :]), reads=[ks, ident],
                 writes=[Pq[2]])
            k.op(k.pe, lambda e: e.transpose(Pq[0][:, 0:128], vs[:, bs], ident[:, :]), reads=[vs, ident],
                 writes=[Pq[0]])
            yield
            k.op(k.act, lambda e: e.activation(out=kbg[:, :], in_=Pq[2][:, 0:128], func=AF.Copy,
                                               scale=cbg[:, b:b + 1]), reads=[Pq[2], cbg], writes=[kbg])
            k.op(k.dve, lambda e: e.tensor_scalar(out=vb[:, :], in0=Pq[0][:, 0:128], scalar1=col[:, 0, b:b + 1],
                                                  scalar2=None, op0=ALU.mult), reads=[Pq[0], col], writes=[vb])
            k.op(k.dve, lambda e: e.tensor_scalar(out=F['kd'][:, b, :], in0=Pq[2][:, 0:128],
                                                  scalar1=cdl[:, b:b + 1], scalar2=None, op0=ALU.mult),
                 reads=[Pq[2], cdl], writes=[F['kd']])
            yield
            k.op(k.pe, lambda e: e.matmul(Pq[1][:, 0:128], lhsT=R[:, :], rhs=vb[:, :], start=True, stop=True),
                 reads=[R, vb], writes=[Pq[1]])
            k.op(k.pe, lambda e: e.matmul(Pq[2][:, 0:128], lhsT=kbg[:, :], rhs=R[:, :], start=True, stop=True),
                 reads=[R, kbg], writes=[Pq[2]])
            yield
            k.op(k.act, lambda e: e.copy(out=F['u'][:, b, :], in_=Pq[1][:, 0:128]), reads=[Pq[1]],
                 writes=[F['u']])
            k.op(k.dve, lambda e: e.tensor_copy(out=F['wT'][:, bs], in_=Pq[2][:, 0:128]), reads=[Pq[2]],
                 writes=[F['wT']])
            if b == 0:
                self.dbg("d_R", R, R[:, :], [128, 128])

        for b0 in range(0, NB, 2):
            gens = [blk(b0 + q, self.a_sets[q]) for q in range(min(2, NB - b0))]
            while gens:
                for g_ in list(gens):
                    try:
                        next(g_)
                    except StopIteration:
                        gens.remove(g_)
                yield

    def _a_back(self, u, sg, F):
        k = self.k
        nc = self.nc
        SG, NB = self.SG, self.NB
        s0 = sg * SG
        dd = self.a_dram[u]
        z_d, na_d, o_d = dd["z"], dd["na"], dd["o"]
        na = self.a_na
        S, Sb = self.a_S, self.a_Sb
        P = self.P
        if sg == 0:
            k.dma(k.sp, na[:, :], na_d[:, :], na, writes=[na])
            k.op(k.dve, lambda e: e.memset(S[:, :], 0.0), writes=[S])
            k.op(k.dve, lambda e: e.memset(Sb[:, :], 0.0), writes=[Sb])
        k.dma(k.sp, self.a_z[:, :, :], z_d[s0:s0 + SG, :].rearrange("(b p) e -> p b e", p=128), self.a_z,
              writes=[self.a_z])
        wT, au, qkT, kd, ao = F['wT'], F['u'], F['qkT'], F['kd'], self.a_o
        gr, qg = F['gr'], F['qg']
        for n in range(2 * NB):
            b, hf = n // 2, n % 2
            bs = slice(b * 128, (b + 1) * 128)
            rs = slice(hf * 64, hf * 64 + 64)
            vn = self.a_vn.next()
            with nc.allow_low_precision("bf16 matmul"):
                k.op(k.pe, lambda e, bs=bs: e.matmul(P[6][:, 0:128], lhsT=wT[:, bs], rhs=Sb[:, :], start=True, stop=True),
                     reads=[wT, Sb], writes=[P[6]])
                k.op(k.dve, lambda e, rs=rs, b=b, vn=vn: e.tensor_tensor(out=vn[rs, :], in0=au[rs, b, :],
                                                                        in1=P[6][rs, 0:128], op=ALU.subtract),
                     reads=[au, P[6]], writes=[vn])
                k.op(k.pe, lambda e, bs=bs: e.matmul(P[7][:, 0:128], lhsT=qg[:, bs], rhs=Sb[:, :], start=True, stop=False),
                     reads=[qg, Sb], writes=[P[7]])
                k.op(k.pe, lambda e, rs=rs, b=b, vn=vn: e.matmul(P[7][:, 0:128], lhsT=qkT[rs, b, :], rhs=vn[rs, :],
                                                                start=False, stop=True),
                     reads=[qkT, vn], writes=[P[7]], accum=True)
                k.op(k.pe, lambda e, rs=rs, b=b, vn=vn: e.matmul(P[6][:, 128:256], lhsT=kd[rs, b, :], rhs=vn[rs, :],
                                                                start=True, stop=True),
                     reads=[kd, vn], writes=[P[6]])
            yield
            c63 = n * 64 + 63
            k.op(k.dve, lambda e, c63=c63: e.scalar_tensor_tensor(out=S[:, :], in0=S[:, :], scalar=gr[:, c63:c63 + 1],
                                                                 in1=P[6][:, 128:256], op0=ALU.mult, op1=ALU.add),
                 reads=[S, gr, P[6]], writes=[S])
            k.op(k.act, lambda e: e.copy(out=Sb[:, :], in_=S[:, :]), reads=[S], writes=[Sb])
            k.op(k.act, lambda e, rs=rs, b=b: e.copy(out=ao[rs, b, :], in_=P[7][rs, 0:128]), reads=[P[7]], writes=[ao])
            yield
        self.dbg("d_o", ao, ao[:, :, :], [128, NB, 128])
        ss, az = self.a_ss, self.a_z
        sq3 = self.a_sq[:, :].rearrange("p (a b) -> p a b", b=128)
        k.op(k.act, lambda e: e.activation(out=sq3, in_=ao[:, :, :], func=AF.Square), reads=[ao], writes=[self.a_sq])
        k.op(k.dve, lambda e: e.reduce_sum(out=ss[:, :], in_=sq3, axis=mybir.AxisListType.X), reads=[self.a_sq],
             writes=[ss])
        k.op(k.act, lambda e: e.activation(out=ss[:, :], in_=ss[:, :], func=AF.Sqrt, bias=self.c_eps[:, 0:1],
                                           scale=1.0 / HD), reads=[ss, self.c_eps], writes=[ss])
        k.op(k.dve, lambda e: e.reciprocal(out=ss[:, :], in_=ss[:, :]), reads=[ss], writes=[ss])
        k.op(k.act, lambda e: e.activation(out=az[:, :, :], in_=az[:, :, :], func=AF.Silu), reads=[az], writes=[az])
        k.op(k.dve, lambda e: e.tensor_tensor(out=ao[:, :, :], in0=ao[:, :, :],
                                              in1=ss[:, :].unsqueeze(2).to_broadcast([128, NB, 128]), op=ALU.mult),
             reads=[ao, ss], writes=[ao])
        k.op(k.dve, lambda e: e.tensor_tensor(out=ao[:, :, :], in0=ao[:, :, :],
                                              in1=na[:, :].unsqueeze(1).to_broadcast([128, NB, 128]), op=ALU.mult),
             reads=[ao, na], writes=[ao])
        ob = self.a_ob.next()
        k.op(k.dve, lambda e, ob=ob: e.tensor_tensor(out=ob[:, :, :], in0=ao[:, :, :], in1=az[:, :, :], op=ALU.mult),
             reads=[ao, az], writes=[ob])
        k.dma(k.sp, o_d[s0:s0 + SG, :].rearrange("(b p) e -> p b e", p=128), ob[:, :, :], ob, reads=[ob])

    def _phase_a(self):
        nseg = self.seq // self.SG
        items = [(u, sg) for u in range(self.na) for sg in range(nseg)]
        self.a_dram = {}

        def drain(gens):
            while gens:
                for g_ in list(gens):
                    try:
                        next(g_)
                    except StopIteration:
                        gens.remove(g_)

        drain([self._a_front(items[0][0], items[0][1], self.a_F[0])])
        for i, (u, sg) in enumerate(items):
            gens = [self._a_back(u, sg, self.a_F[i % 2])]
            if i + 1 < len(items):
                gens.append(self._a_front(items[i + 1][0], items[i + 1][1], self.a_F[(i + 1) % 2]))
            drain(gens)


class AdaProg:
    NCOL = 9 * D // NCORE

    def __init__(self):
        nc = bass.Bass("TRN2", target_bir_lowering=False)
        self.nc = nc
        NCOL = self.NCOL
        w_d = nc.dram_tensor("ada_w", [DEPTH, D, NCOL], F32, kind="ExternalInput").ap()
        b_d = nc.dram_tensor("ada_b", [DEPTH, 2, NCOL], F32, kind="ExternalInput").ap()
        c_d = nc.dram_tensor("ada_c", [128, KC, 2], F32, kind="ExternalInput").ap()
        o_d = nc.dram_tensor("ada_o", [DEPTH, 2, NCOL], F32, kind="ExternalOutput").ap()
        with ExitStack() as st:
            k = KB(nc, st)
            ca = k.sb([128, KC, 2], F32, "ca")
            k.dma(k.sp, ca[:, :, :], c_d[:, :, :], ca, writes=[ca])
            k.op(k.act, lambda e: e.activation(out=ca[:, :, :], in_=ca[:, :, :], func=AF.Silu), reads=[ca], writes=[ca])
            HW = 2560
            wr = Ring(k, 3, [128, HW], F32, "wr")
            P = [k.ps([128, 512], F32, f"P{i}") for i in range(5)]
            bt = k.sb([2, DEPTH, NCOL], F32, "bt")
            res = k.sb([2, DEPTH, NCOL], F32, "res")
            for l in range(DEPTH):
                k.dma(k.sp, bt[:, l, :], b_d[l, :, :], bt, writes=[bt], join=(l > 0))
            for l in range(DEPTH):
                for c0 in (0, HW):
                    cw = min(HW, NCOL - c0)
                    nb = (cw + 511) // 512
                    for kc in range(KC):
                        wt = wr.next()
                        k.dma(k.sp if kc % 2 == 0 else k.act, wt[:, 0:cw], w_d[l, kc * 128:(kc + 1) * 128, c0:c0 + cw], wt,
                              writes=[wt])
                        for bi in range(nb):
                            n = min(512, cw - bi * 512)
                            k.op(k.pe, lambda e, bi=bi, n=n, kc=kc, wt=wt: e.matmul(
                                P[bi][0:2, 0:n], lhsT=ca[:, kc, :], rhs=wt[:, bi * 512:bi * 512 + n],
                                start=(kc == 0), stop=(kc == KC - 1)), reads=[ca, wt], writes=[P[bi]], accum=(kc > 0))
                    for bi in range(nb):
                        n = min(512, cw - bi * 512)
                        k.op(k.dve, lambda e, bi=bi, n=n, l=l, c0=c0: e.tensor_tensor(
                            out=res[0:2, l, c0 + bi * 512:c0 + bi * 512 + n], in0=P[bi][0:2, 0:n],
                            in1=bt[0:2, l, c0 + bi * 512:c0 + bi * 512 + n], op=ALU.add),
                            reads=[P[bi], bt], writes=[res])
            for l in range(DEPTH):
                k.dma(k.sp, o_d[l, :, :], res[0:2, l, :], res, reads=[res])
            k.wait_tiles(k.sp, [res])
            k.finish()


_PROGS = {}


def _prog(key, fn):
    if key not in _PROGS:
        _PROGS[key] = fn()
    return _PROGS[key]


def _run(prog_nc, in_maps):
    res = run_bass_kernel_spmd(prog_nc, in_maps, core_ids=list(range(NCORE)))
    return res.results


def _consts():
    idx = np.arange(128)
    same = (idx[:, None] // 64) == (idx[None, :] // 64)
    return {
        "ident_in": np.eye(128, dtype=np.float32),
        "mask_su_in": np.triu(np.ones((128, 128), np.float32)),
        "maskU_in": (same & (idx[:, None] <= idx[None, :])).astype(np.float32),
        "maskS_in": (same & (idx[:, None] < idx[None, :])).astype(np.float32),
        "sameC_in": same.astype(np.float32),
    }


def _pick(d, names):
    return {n: d[n] for n in names}


def kernel(x, c, w_ada, b_ada, ln_g, ln_b, w_ffn_in, w_ffn_out, w_in, conv_w, a_log, dt_bias, norm_a,
           sgu_ln_g, sgu_ln_b, w_s, b_s, b_f, norm_c, w_o):
    f32 = np.float32
    x = np.asarray(x, f32)
    consts = _consts()
    ada = _prog("ada", AdaProg)
    NCOL = AdaProg.NCOL
    cT = np.ascontiguousarray(np.asarray(c, f32).reshape(B, KC, 128).transpose(2, 1, 0))
    in_maps = []
    for i in range(NCORE):
        cs = slice(i * NCOL, (i + 1) * NCOL)
        in_maps.append({
            "ada_w": np.ascontiguousarray(np.asarray(w_ada)[:, :, cs]),
            "ada_b": np.ascontiguousarray(np.broadcast_to(np.asarray(b_ada, f32)[:, None, cs], (DEPTH, 2, NCOL))),
            "ada_c": cT,
        })
    r = _run(ada.nc, in_maps)
    del in_maps
    mod = np.concatenate([r[i]["ada_o"] for i in range(NCORE)], axis=-1)
    mods = mod.reshape(DEPTH, B, 9, D)

    def stage_vecs(tag, l, b_, kind, j=0):
        out = {}
        if kind == "ffn":
            base, lni = (0, 0) if j == 0 else (6, 2)
            out[f"sh_{tag}"] = vec_pp(mods[l, b_, base])
            out[f"sc_{tag}"] = vec_pp(mods[l, b_, base + 1])
            out[f"ga_{tag}"] = vec_pp(mods[l, b_, base + 2])
            out[f"lng_{tag}"] = vec_pp(ln_g[l][lni])
            out[f"lnb_{tag}"] = vec_pp(ln_b[l][lni])
        elif kind == "mixin":
            out[f"sh_{tag}"] = vec_pp(mods[l, b_, 3])
            out[f"sc_{tag}"] = vec_pp(mods[l, b_, 4])
        elif kind == "mixout":
            out[f"ga_{tag}"] = vec_pp(mods[l, b_, 5])
            out[f"lng_{tag}"] = vec_pp(ln_g[l][1])
            out[f"lnb_{tag}"] = vec_pp(ln_b[l][1])
        return out

    def core_tok(i):
        return i // 4, (i % 4) * NTOK

    def weights_for(stages):
        w = {}
        for kind, tag, l, j in stages:
            if kind == "ffn":
                w[f"wfi_{tag}"] = prep_ffn_in(np.asarray(w_ffn_in[l][j], f32))
                w[f"wfo_{tag}"] = prep_ffn_out(np.asarray(w_ffn_out[l][j], f32))
            elif kind == "mixin":
                w[f"win_{tag}"] = prep_win(np.asarray(w_in[l], f32))
                w.update(b_consts(tag, np.asarray(w_s[l], f32), np.asarray(b_s[l], f32), np.asarray(sgu_ln_g[l], f32),
                                  np.asarray(sgu_ln_b[l], f32)))
            elif kind == "mixout":
                w[f"wo_{tag}"] = prep_wo(np.asarray(w_o[l], f32))
        return w

    def run_tok(stages, xT_list, mix_list=None):
        key = ("tok",) + tuple((kd, tg) for kd, tg, _, _ in stages)
        prog = _prog(key, lambda: TokProg([(kd, tg) for kd, tg, _, _ in stages]))
        w = weights_for(stages)
        in_maps = []
        for i in range(NCORE):
            b_, s0 = core_tok(i)
            m = {"ident_in": consts["ident_in"], "x_in": xT_list[i]}
            m.update(w)
            for kind, tag, l, j in stages:
                m.update(stage_vecs(tag, l, b_, kind, j))
                if kind == "mixout":
                    m[f"mix_{tag}"] = mix_list[i]
            in_maps.append(_pick(m, prog.in_names))
        r_ = _run(prog.nc, in_maps)
        del in_maps, w
        return r_

    def run_head(l, EA, EC, ES):
        prog = _prog("head", lambda: HeadProg(3, 3, S))
        in_maps = []
        for i in range(NCORE):
            m = dict(consts)
            for sl in range(3):
                uid = i * 3 + sl
                b_, h = uid // H_A, uid % H_A
                m[f"a_qkv_{sl}"] = np.ascontiguousarray(np.stack(
                    [EA[b_][t_ * D_A + h * 128:t_ * D_A + (h + 1) * 128] for t_ in range(3)]))
                m[f"a_z_{sl}"] = np.ascontiguousarray(EA[b_][3 * D_A + h * 128:3 * D_A + (h + 1) * 128].T)
                bd = np.ascontiguousarray(np.stack([ES[b_][h], ES[b_][H_A + h]]))
                m[f"a_bd_{sl}"] = bd
                m[f"a_bdc_{sl}"] = np.ascontiguousarray(bd.reshape(2, S // 128, 128).transpose(2, 0, 1))
                par = np.zeros((128, 16), f32)
                for t_ in range(3):
                    for j in range(4):
                        par[:, 4 * t_ + j] = conv_w[l][j, t_ * D_A + h * 128:t_ * D_A + (h + 1) * 128]
                par[:, 12] = a_log[l][h]
                par[:, 13] = dt_bias[l][h]
                m[f"a_par_{sl}"] = par
                m[f"a_na_{sl}"] = np.ascontiguousarray(np.broadcast_to(np.asarray(norm_a[l], f32)[None], (128, 128)))
                m[f"c_qk_{sl}"] = np.ascontiguousarray(np.stack(
                    [EC[b_][t_ * D_C + h * 128:t_ * D_C + (h + 1) * 128] for t_ in range(2)]))
                m[f"c_v_{sl}"] = np.ascontiguousarray(EC[b_][2 * D_C + h * 128:2 * D_C + (h + 1) * 128].T)
                m[f"c_f_{sl}"] = np.ascontiguousarray(ES[b_][2 * H_A + h][None])
                cp = np.zeros((128, 2), f32)
                cp[:, 0] = b_f[l][h]
                cp[:, 1] = norm_c[l]
                m[f"c_par_{sl}"] = cp
            in_maps.append(_pick(m, prog.in_names))
        r_ = _run(prog.nc, in_maps)
        o_a = [[None] * H_A for _ in range(B)]
        o_c = [[None] * H_C for _ in range(B)]
        for i in range(NCORE):
            for sl in range(3):
                uid = i * 3 + sl
                b_, h = uid // H_A, uid % H_A
                o_a[b_][h] = r_[i][f"a_o_{sl}"]
                o_c[b_][h] = r_[i][f"c_o_{sl}"]
        return o_a, o_c

    def gather(r_, name):
        return [np.concatenate([r_[b_ * 4 + q][name] for q in range(4)], axis=1) for b_ in range(B)]

    def build_mix(o_a, o_c, EB):
        mix = []
        for i in range(NCORE):
            b_, s0 = core_tok(i)
            rows = [np.ascontiguousarray(o_a[b_][h][s0:s0 + NTOK].T) for h in range(H_A)]
            rows.append(EB[i])
            rows += [o_c[b_][h][:, s0:s0 + NTOK] for h in range(H_C)]
            mix.append(np.ascontiguousarray(np.concatenate(rows, axis=0)))
        return mix

    xT = []
    for i in range(NCORE):
        b_, s0 = core_tok(i)
        xT.append(np.ascontiguousarray(x[b_, s0:s0 + NTOK].T))
    r1 = run_tok([("ffn", "f00", 0, 0), ("mixin", "m0", 0, 0)], xT)
    x1 = [r1[i]["x_out"] for i in range(NCORE)]
    EB = [r1[i]["EB_m0"] for i in range(NCORE)]
    o_a, o_c = run_head(0, gather(r1, "EA_m0"), gather(r1, "EC_m0"), gather(r1, "ES_m0"))
    del r1
    mix = build_mix(o_a, o_c, EB)
    r3 = run_tok([("mixout", "o0", 0, 0), ("ffn", "f01", 0, 1), ("ffn", "f10", 1, 0), ("mixin", "m1", 1, 0)], x1, mix)
    x1 = [r3[i]["x_out"] for i in range(NCORE)]
    EB = [r3[i]["EB_m1"] for i in range(NCORE)]
    o_a, o_c = run_head(1, gather(r3, "EA_m1"), gather(r3, "EC_m1"), gather(r3, "ES_m1"))
    del r3
    mix = build_mix(o_a, o_c, EB)
    r5 = run_tok([("mixout", "o1", 1, 0), ("ffn", "f11", 1, 1)], x1, mix)
    out = np.empty((B, S, D), f32)
    for i in range(NCORE):
        b_, s0 = core_tok(i)
        out[b_, s0:s0 + NTOK] = r5[i]["x_out"].T
    return out
```
